# Optimizing a Trainium2 kernel written in Bass

```python
import math
import jax, jax.numpy as jnp
from jax import lax
import numpy as np

D_MODEL = 1024
BATCH = 8
SEQ = 4096
DEPTH = 2

A_WIDTH = D_MODEL // 2
B_WIDTH = D_MODEL // 2
A_CONV = 3
B_CONV = 31
EVEN_IN = 3 * A_WIDTH + 2 * B_WIDTH

C_HEAD_DIM = 64
C_WIDTH = D_MODEL // 2
C_HEADS = C_WIDTH // (2 * C_HEAD_DIM)
C_QK = C_HEADS * 2 * C_HEAD_DIM
Q_BLOCK = 128
D_WIDTH = D_MODEL // 2
CHUNK = 128
D_GROUP_DIM = 128
D_GROUPS = D_WIDTH // D_GROUP_DIM
ODD_IN = 3 * C_QK + 2 * D_WIDTH

D_FF = -(-8 * D_MODEL // (3 * 256)) * 256

N_EVEN = (DEPTH + 1) // 2
N_ODD = DEPTH // 2
DEEPNORM_ALPHA = (2 * DEPTH) ** 0.25
DEEPNORM_BETA = (8 * DEPTH) ** -0.25
LN_EPS = 1e-5

kernel_name = "hybrid_shortconv_conformer_diffattn_gmlp"


def layer_norm(x, g, b):
    xf = x.astype(jnp.float32)
    mu = jnp.mean(xf, axis=-1, keepdims=True)
    xc = xf - mu
    var = jnp.mean(xc * xc, axis=-1, keepdims=True)
    y = xc * lax.rsqrt(var + LN_EPS) * g.astype(jnp.float32) + b.astype(jnp.float32)
    return y.astype(x.dtype)


def rms_norm(x, g):
    xf = x.astype(jnp.float32)
    y = xf * lax.rsqrt(jnp.mean(xf * xf, axis=-1, keepdims=True) + LN_EPS) * g.astype(jnp.float32)
    return y.astype(x.dtype)


def causal_depthwise_conv(x, w):
    k_width, ch = w.shape
    return lax.conv_general_dilated(
        x, w[:, None, :].astype(x.dtype), window_strides=(1,),
        padding=[(k_width - 1, 0)], dimension_numbers=("NWC", "WIO", "NWC"),
        feature_group_count=ch)


def shortconv_conformer_mixer(x, w_in, conv_a_w, conv_b_w, conv_b_bias, conv_ln_g, conv_ln_b, w_out):
    proj = x @ w_in.astype(x.dtype)
    gate_b, gate_c, h, glu_a, glu_g = jnp.split(
        proj, [A_WIDTH, 2 * A_WIDTH, 3 * A_WIDTH, 3 * A_WIDTH + B_WIDTH], axis=-1)
    y_a = gate_b * causal_depthwise_conv(gate_c * h, conv_a_w)
    z = glu_a * jax.nn.sigmoid(glu_g)
    z = causal_depthwise_conv(z, conv_b_w) + conv_b_bias.astype(x.dtype)
    y_b = jax.nn.silu(layer_norm(z, conv_ln_g, conv_ln_b))
    return jnp.concatenate([y_a, y_b], axis=-1) @ w_out.astype(x.dtype)


def diffattn_gmlp_mixer(x, w_in, lambda_q1, lambda_k1, lambda_q2, lambda_k2, subln_g,
                        gmlp_ln_g, gmlp_ln_b, spatial_w, spatial_b, w_out, lambda_init):
    bsz, seq, _ = x.shape
    proj = x @ w_in.astype(x.dtype)
    q, k, v, uv = jnp.split(proj, [C_QK, 2 * C_QK, 3 * C_QK], axis=-1)
    q = q.reshape(bsz, seq, C_HEADS, 2, C_HEAD_DIM)
    k = k.reshape(bsz, seq, C_HEADS, 2, C_HEAD_DIM)
    v = v.reshape(bsz, seq, C_HEADS, 2 * C_HEAD_DIM)

    lam = (jnp.exp(jnp.sum(lambda_q1.astype(jnp.float32) * lambda_k1.astype(jnp.float32)))
           - jnp.exp(jnp.sum(lambda_q2.astype(jnp.float32) * lambda_k2.astype(jnp.float32)))
           + lambda_init)
    scale = C_HEAD_DIM ** -0.5
    n_blocks = seq // Q_BLOCK
    q_blocks = q.reshape(bsz, n_blocks, Q_BLOCK, C_HEADS, 2, C_HEAD_DIM).transpose(1, 0, 2, 3, 4, 5)
    key_pos = jnp.arange(seq)

    def attend(args):
        q_blk, blk = args
        s = jnp.einsum("bqhcd,bkhcd->bhcqk", q_blk, k).astype(jnp.float32) * scale
        q_pos = blk * Q_BLOCK + jnp.arange(Q_BLOCK)
        mask = key_pos[None, :] <= q_pos[:, None]
        s = jnp.where(mask, s, -jnp.inf)
        p = jax.nn.softmax(s, axis=-1)
        a = p[:, :, 0] - lam * p[:, :, 1]
        return jnp.einsum("bhqk,bkhe->bqhe", a.astype(v.dtype), v)

    o = lax.map(attend, (q_blocks, jnp.arange(n_blocks)))
    o = o.transpose(1, 0, 2, 3, 4).reshape(bsz, seq, C_HEADS, 2 * C_HEAD_DIM)
    o = rms_norm(o, subln_g) * (1.0 - lambda_init)
    y_c = o.reshape(bsz, seq, C_WIDTH)

    z = jax.nn.gelu(uv)
    u, vg = jnp.split(z, 2, axis=-1)
    vg = vg.reshape(bsz, seq // CHUNK, CHUNK, D_GROUPS, D_GROUP_DIM)
    vg = layer_norm(vg, gmlp_ln_g, gmlp_ln_b)
    tri = jnp.tril(jnp.ones((CHUNK, CHUNK), dtype=bool))
    w_causal = jnp.where(tri[None], spatial_w, 0.0).astype(x.dtype)
    sp = jnp.einsum("gts,bcsgd->bctgd", w_causal, vg) + spatial_b.T.astype(x.dtype)[:, :, None]
    y_d = u * sp.reshape(bsz, seq, D_WIDTH)

    return jnp.concatenate([y_c, y_d], axis=-1) @ w_out.astype(x.dtype)


def swiglu(x, w_gate, w_up, w_down):
    return (jax.nn.silu(x @ w_gate.astype(x.dtype)) * (x @ w_up.astype(x.dtype))) @ w_down.astype(x.dtype)


def setup_inputs(seed: int = 0) -> dict:
    key = jax.random.key(seed)
    ks = jax.random.split(key, 26)
    f32 = jnp.float32
    nrm = lambda k, shape, s: jax.random.normal(k, shape, f32) * s
    mix_w = 2 * D_MODEL // 2
    return {
        "x": nrm(ks[0], (BATCH, SEQ, D_MODEL), 1.0),
        "even_w_in": nrm(ks[1], (N_EVEN, D_MODEL, EVEN_IN), D_MODEL ** -0.5),
        "even_conv_a_w": nrm(ks[2], (N_EVEN, A_CONV, A_WIDTH), A_CONV ** -0.5),
        "even_conv_b_w": nrm(ks[3], (N_EVEN, B_CONV, B_WIDTH), B_CONV ** -0.5),
        "even_conv_b_bias": nrm(ks[4], (N_EVEN, B_WIDTH), 0.01),
        "even_conv_ln_g": 1.0 + nrm(ks[5], (N_EVEN, B_WIDTH), 0.05),
        "even_conv_ln_b": nrm(ks[6], (N_EVEN, B_WIDTH), 0.01),
        "even_w_out": nrm(ks[7], (N_EVEN, mix_w, D_MODEL), mix_w ** -0.5 * DEEPNORM_BETA),
        "odd_w_in": nrm(ks[8], (N_ODD, D_MODEL, ODD_IN), D_MODEL ** -0.5),
        "odd_lambda_q1": nrm(ks[9], (N_ODD, C_HEAD_DIM), 0.1),
        "odd_lambda_k1": nrm(ks[10], (N_ODD, C_HEAD_DIM), 0.1),
        "odd_lambda_q2": nrm(ks[11], (N_ODD, C_HEAD_DIM), 0.1),
        "odd_lambda_k2": nrm(ks[12], (N_ODD, C_HEAD_DIM), 0.1),
        "odd_subln_g": 1.0 + nrm(ks[13], (N_ODD, 2 * C_HEAD_DIM), 0.05),
        "odd_gmlp_ln_g": 1.0 + nrm(ks[14], (N_ODD, D_GROUPS, D_GROUP_DIM), 0.05),
        "odd_gmlp_ln_b": nrm(ks[15], (N_ODD, D_GROUPS, D_GROUP_DIM), 0.01),
        "odd_spatial_w": nrm(ks[16], (N_ODD, D_GROUPS, CHUNK, CHUNK), CHUNK ** -0.5),
        "odd_spatial_b": 1.0 + nrm(ks[17], (N_ODD, D_GROUPS, CHUNK), 0.05),
        "odd_w_out": nrm(ks[18], (N_ODD, mix_w, D_MODEL), mix_w ** -0.5 * DEEPNORM_BETA),
        "mix_ln_g": 1.0 + nrm(ks[19], (DEPTH, D_MODEL), 0.05),
        "mix_ln_b": nrm(ks[20], (DEPTH, D_MODEL), 0.01),
        "ffn_w_gate": nrm(ks[21], (DEPTH, D_MODEL, D_FF), D_MODEL ** -0.5),
        "ffn_w_up": nrm(ks[22], (DEPTH, D_MODEL, D_FF), D_MODEL ** -0.5),
        "ffn_w_down": nrm(ks[23], (DEPTH, D_FF, D_MODEL), D_FF ** -0.5 * DEEPNORM_BETA),
        "ffn_ln_g": 1.0 + nrm(ks[24], (DEPTH, D_MODEL), 0.05),
        "ffn_ln_b": nrm(ks[25], (DEPTH, D_MODEL), 0.01),
    }


def reference(x, even_w_in, even_conv_a_w, even_conv_b_w, even_conv_b_bias, even_conv_ln_g,
              even_conv_ln_b, even_w_out, odd_w_in, odd_lambda_q1, odd_lambda_k1, odd_lambda_q2,
              odd_lambda_k2, odd_subln_g, odd_gmlp_ln_g, odd_gmlp_ln_b, odd_spatial_w, odd_spatial_b,
              odd_w_out, mix_ln_g, mix_ln_b, ffn_w_gate, ffn_w_up, ffn_w_down, ffn_ln_g, ffn_ln_b):
    for layer in range(DEPTH):
        j = layer // 2
        if layer % 2 == 0:
            m = shortconv_conformer_mixer(x, even_w_in[j], even_conv_a_w[j], even_conv_b_w[j],
                                          even_conv_b_bias[j], even_conv_ln_g[j], even_conv_ln_b[j],
                                          even_w_out[j])
        else:
            lambda_init = 0.8 - 0.6 * math.exp(-0.3 * layer)
            m = diffattn_gmlp_mixer(x, odd_w_in[j], odd_lambda_q1[j], odd_lambda_k1[j],
                                    odd_lambda_q2[j], odd_lambda_k2[j], odd_subln_g[j],
                                    odd_gmlp_ln_g[j], odd_gmlp_ln_b[j], odd_spatial_w[j],
                                    odd_spatial_b[j], odd_w_out[j], lambda_init)
        x = layer_norm(DEEPNORM_ALPHA * x + m, mix_ln_g[layer], mix_ln_b[layer])
        f = swiglu(x, ffn_w_gate[layer], ffn_w_up[layer], ffn_w_down[layer])
        x = layer_norm(DEEPNORM_ALPHA * x + f, ffn_ln_g[layer], ffn_ln_b[layer])
    return x
```

```python
import math
from contextlib import ExitStack
import numpy as np
import concourse.bass as bass
import concourse.mybir as mybir
from concourse.bass_utils import run_bass_kernel_spmd

F32 = mybir.dt.float32
BF16 = mybir.dt.bfloat16
AF = mybir.ActivationFunctionType
ALU = mybir.AluOpType
AX = mybir.AxisListType

S, D, TT, NT = 4096, 1024, 512, 8
KC = 8
NFC = 22
ALPHA = (2 * 2) ** 0.25
LN_EPS = 1e-5
LAMBDA_INIT = 0.8 - 0.6 * math.exp(-0.3 * 1)
NPL = 24
NP = 2 * NPL
PW = 4096
NSLOT = 4
DOWN_GROUPS = [(0, 8), (8, 16), (16, 22)]
ARENA_B = 28704

CV_CA = 0
CV_CB = 12
CV_BB = 136
CV_LG = 140
CV_LB = 144
CV_MG = 148
CV_MB = 164
CV_FG = 180
CV_FB = 196
NCV = 212
CB_GG = 0
CB_GB = 512
CB_SB = 1024
CB_SG = 1536
CB_LQ1 = 1664
CB_LK1 = 1728
CB_LQ2 = 1792
CB_LK2 = 1856
NCB = 1920


def _piece_in(w, j):
    return np.ascontiguousarray(w.reshape(8, 128, -1)[:, :, j * 512:(j + 1) * 512].transpose(1, 0, 2)).reshape(128, PW)


def _piece_gu(g, u, j):
    a = np.empty((128, 8, 2, 256), np.float32)
    a[:, :, 0, :] = g.reshape(8, 128, -1)[:, :, j * 256:(j + 1) * 256].transpose(1, 0, 2)
    a[:, :, 1, :] = u.reshape(8, 128, -1)[:, :, j * 256:(j + 1) * 256].transpose(1, 0, 2)
    return a.reshape(128, PW)


def _piece_down(wd, dh, grp):
    lo, hi = DOWN_GROUPS[grp]
    a = np.zeros((128, 8, 512), np.float32)
    a[:, :hi - lo, :] = wd.reshape(NFC, 128, 1024)[lo:hi, :, dh * 512:(dh + 1) * 512].transpose(1, 0, 2)
    return a.reshape(128, PW)


def _chunked(v):
    return np.ascontiguousarray(np.asarray(v, np.float32).reshape(-1, 128).T)


def make_shared(inp):
    f = lambda k: np.asarray(inp[k], np.float32)
    wall = np.empty((NP, 128, PW), np.float32)
    for L in range(2):
        w_in = f("even_w_in")[0] if L == 0 else f("odd_w_in")[0]
        w_out = f("even_w_out")[0] if L == 0 else f("odd_w_out")[0]
        b = L * NPL
        for j in range(5):
            wall[b + j] = _piece_in(w_in, j)
        for j in range(2):
            wall[b + 5 + j] = _piece_in(w_out, j)
        for j in range(11):
            wall[b + 7 + j] = _piece_gu(f("ffn_w_gate")[L], f("ffn_w_up")[L], j)
        for dh in range(2):
            for g in range(3):
                wall[b + 18 + dh * 3 + g] = _piece_down(f("ffn_w_down")[L], dh, g)
    cvec = np.zeros((128, NCV), np.float32)
    ca = f("even_conv_a_w")[0]
    cb = f("even_conv_b_w")[0]
    for k in range(3):
        cvec[:, CV_CA + 4 * k:CV_CA + 4 * k + 4] = _chunked(ca[k])
    for k in range(31):
        cvec[:, CV_CB + 4 * k:CV_CB + 4 * k + 4] = _chunked(cb[k])
    cvec[:, CV_BB:CV_BB + 4] = _chunked(f("even_conv_b_bias")[0])
    cvec[:, CV_LG:CV_LG + 4] = _chunked(f("even_conv_ln_g")[0])
    cvec[:, CV_LB:CV_LB + 4] = _chunked(f("even_conv_ln_b")[0])
    for L in range(2):
        cvec[:, CV_MG + 8 * L:CV_MG + 8 * L + 8] = _chunked(f("mix_ln_g")[L])
        cvec[:, CV_MB + 8 * L:CV_MB + 8 * L + 8] = _chunked(f("mix_ln_b")[L])
        cvec[:, CV_FG + 8 * L:CV_FG + 8 * L + 8] = _chunked(f("ffn_ln_g")[L])
        cvec[:, CV_FB + 8 * L:CV_FB + 8 * L + 8] = _chunked(f("ffn_ln_b")[L])
    row = np.zeros((NCB,), np.float32)
    row[CB_GG:CB_GG + 512] = f("odd_gmlp_ln_g")[0].reshape(-1)
    row[CB_GB:CB_GB + 512] = f("odd_gmlp_ln_b")[0].reshape(-1)
    row[CB_SB:CB_SB + 512] = f("odd_spatial_b")[0].reshape(-1)
    row[CB_SG:CB_SG + 128] = f("odd_subln_g")[0]
    row[CB_LQ1:CB_LQ1 + 64] = f("odd_lambda_q1")[0]
    row[CB_LK1:CB_LK1 + 64] = f("odd_lambda_k1")[0]
    row[CB_LQ2:CB_LQ2 + 64] = f("odd_lambda_q2")[0]
    row[CB_LK2:CB_LK2 + 64] = f("odd_lambda_k2")[0]
    cbc = np.ascontiguousarray(np.broadcast_to(row[None, :], (128, NCB)))
    wst = np.ascontiguousarray(f("odd_spatial_w")[0].transpose(2, 0, 1))
    return {"wall": wall, "cvec": cvec, "cbc": cbc, "wst": wst}


class Op:
    __slots__ = ("eng", "fn", "deps", "signal", "sigval", "dma", "tag")


class Sched:
    ENGS = ("pe", "act", "dve", "pool", "sp")

    def __init__(self, nc, es):
        self.nc, self.es = nc, es
        self.ops = {e: [] for e in self.ENGS}
        self.last_w, self.readers = {}, {}
        self.sems = {}
        self.dma_cnt = {}
        self.final = []
        self.tag = ""

    def sem(self, key):
        if key not in self.sems:
            self.sems[key] = self.es.enter_context(self.nc.semaphore("s_" + key))
        return self.sems[key]

    def add(self, eng, fn, reads=(), writes=(), dma=None, final=False):
        op = Op()
        op.eng, op.fn, op.dma, op.signal, op.sigval = eng, fn, dma, False, 0
        op.tag = self.tag
        writes = list(writes) + [r for r in reads if isinstance(r, tuple) and r[0] == "ps"]
        deps = []
        for r in reads:
            d = self.last_w.get(r)
            if d is not None:
                deps.append(d)
        for w in writes:
            d = self.last_w.get(w)
            if d is not None:
                deps.append(d)
            deps.extend(self.readers.get(w, ()))
        for r in reads:
            self.readers.setdefault(r, []).append(op)
        for w in writes:
            self.last_w[w] = op
            self.readers[w] = []
        seen, dd = set(), []
        for d in deps:
            if d is op or id(d) in seen:
                continue
            seen.add(id(d))
            dd.append(d)
            if not (d.dma is None and d.eng == eng and eng in ("pe", "sp")):
                d.signal = True
        op.deps = dd
        if dma is not None:
            self.dma_cnt[dma] = self.dma_cnt.get(dma, 0) + 1
            op.sigval = 16 * self.dma_cnt[dma]
            self.sem(dma)
            if final:
                self.final.append(op)
        self.ops[eng].append(op)
        return op

    def emit(self):
        nc = self.nc
        for e in self.ENGS:
            c = 0
            for op in self.ops[e]:
                if op.dma is None and op.signal:
                    c += 1
                    op.sigval = c
            self.sem("eng_" + e)
        block = self.es.enter_context(nc.Block())

        def tok(d):
            return ("eng_" + d.eng if d.dma is None else d.dma), d.sigval

        def run(e, eng):
            waited = {}
            for op in self.ops[e]:
                for d in op.deps:
                    if d.dma is None and d.eng == e and e in ("pe", "sp"):
                        continue
                    key, val = tok(d)
                    if waited.get(key, 0) >= val:
                        continue
                    eng.wait_ge(self.sems[key], val)
                    waited[key] = val
                ins = op.fn(eng)
                if op.dma is not None:
                    ins.then_inc(self.sems[op.dma], 16)
                elif op.signal:
                    ins.then_inc(self.sems["eng_" + e], 1)
            if e == "act":
                for op in self.final:
                    eng.wait_ge(self.sems[op.dma], op.sigval)

        block.tensor(lambda eng: run("pe", eng))
        block.scalar(lambda eng: run("act", eng))
        block.vector(lambda eng: run("dve", eng))
        block.gpsimd(lambda eng: run("pool", eng))
        block.sync(lambda eng: run("sp", eng))


class _Stop(Exception):
    pass


def build_program(n_tiles=NT, dbg=None, stop_at=None):
    nc = bass.Bass("TRN2", target_bir_lowering=False)
    x_d = nc.dram_tensor("x", [D, S], F32, kind="ExternalInput").ap()
    wall_d = nc.dram_tensor("wall", [NP, 128, PW], F32, kind="ExternalInput").ap()
    cvec_d = nc.dram_tensor("cvec", [128, NCV], F32, kind="ExternalInput").ap()
    cbc_d = nc.dram_tensor("cbc", [128, NCB], F32, kind="ExternalInput").ap()
    wst_d = nc.dram_tensor("wst", [128, 4, 128], F32, kind="ExternalInput").ap()
    out_d = nc.dram_tensor("out", [D, S], F32, kind="ExternalOutput").ap()
    wsc_d = nc.dram_tensor("wsc", [NP, 128, PW], BF16, kind="Internal").ap()
    dbg_d = {}
    if dbg:
        for name, shape in dbg.items():
            dbg_d[name] = nc.dram_tensor("dbg_" + name, list(shape), F32, kind="ExternalOutput").ap()

    es = ExitStack()
    with es:
        sb = lambda name, shape, dt: es.enter_context(nc.sbuf_tensor("sb_" + name, shape, dt))
        xf = sb("xf", [128, KC, TT], F32)
        xb = sb("xb", [128, KC, TT], BF16)
        KT = sb("KT", [128, 4, S], BF16)
        Vx = sb("Vx", [128, S // 128, 4, 130], BF16)
        wring = sb("wring", [128, NSLOT, PW], BF16)
        FF = sb("FF", [128, 12, TT], F32)
        arena = sb("arena", [128, ARENA_B // 2], BF16)
        cvec = sb("cvec", [128, NCV], F32)
        cbc = sb("cbc", [128, NCB], F32)
        wstf = sb("wstf", [128, 4, 128], F32)
        WcT = sb("WcT", [128, 4, 128], BF16)
        identb = sb("identb", [128, 128], BF16)
        identf = sb("identf", [128, 128], F32)
        trif = sb("trif", [128, 128], F32)
        trib = sb("trib", [128, 128], BF16)
        ones1024 = sb("ones1024", [128, 128], F32)
        ones512 = sb("ones512", [128, 128], F32)
        S1 = sb("S1", [128, TT], F32)
        S2 = sb("S2", [128, TT], F32)
        MEAN = sb("MEAN", [128, TT], F32)
        RSTD = sb("RSTD", [128, TT], F32)
        SG = sb("SG", [128, 3, TT], F32)
        zhalo = sb("zhalo", [128, 4, 30], BF16)
        chhalo = sb("chhalo", [128, 4, 2], F32)
        st16 = sb("st16", [128, 6, 16], F32)
        gsc = sb("gsc", [128, 128], F32)
        lamt = sb("lamt", [128, 64], F32)
        lams = sb("lams", [128, 8], F32)
        ep = sb("ep", [128, 2, 8], F32)
        ss16 = sb("ss16", [128, 32], F32)
        lnd = sb("lnd", [128, 1], F32)
        epT = sb("epT", [128, 2, 128], F32)
        epO = sb("epO", [128, 2, 128], F32)
        epJ = sb("epJ", [128, 128], F32)
        ps = es.enter_context(nc.psum_tensor("ps", [128, 8, 512], F32))

        sch = Sched(nc, es)
        add = sch.add

        def aview(off, dt, n):
            a = arena[:, off // 2:off // 2 + (n * (4 if dt == F32 else 2)) // 2]
            return a.bitcast(F32) if dt == F32 else a

        def ablk(lo, hi):
            return [("A", i) for i in range(lo // 1024, (hi - 1) // 1024 + 1)]

        hT = arena[:, 0:NFC * 512].rearrange("p (c n) -> p c n", n=512)
        yT = arena[:, 0:8 * 512].rearrange("p (c n) -> p c n", n=512)
        ZOFF, CHOFF, DG1OFF = 8192, 12544, 20768
        zbuf = aview(ZOFF, BF16, 4 * 542).rearrange("p (c n) -> p c n", n=542)
        chbuf = aview(CHOFF, F32, 4 * 514).rearrange("p (c n) -> p c n", n=514)
        Dg = [aview(o, BF16, 31 * 128).rearrange("p (k n) -> p k n", n=128) for o in (CHOFF, DG1OFF)]
        R_dg = [ablk(o, o + 31 * 128 * 2) for o in (CHOFF, DG1OFF)]
        QOFF, EOFF, YCOFF = 8192, 16384, 20480
        qz = [aview(QOFF + m * 4096, BF16, 4 * 512).rearrange("p (c n) -> p c n", n=512) for m in range(2)]
        E2 = [aview(EOFF + i * 2048, BF16, 2 * 512).rearrange("p (c n) -> p c n", n=512) for i in range(2)]
        yc = aview(YCOFF, BF16, 4 * 512).rearrange("p (c n) -> p c n", n=512)
        R_hT = lambda fc: [("A", fc)]
        R_yT = lambda c: [("A", c)]
        R_z = lambda c: ablk(ZOFF + c * 542 * 2, ZOFF + (c + 1) * 542 * 2)
        R_ch = lambda c: ablk(CHOFF + c * 514 * 4, CHOFF + (c + 1) * 514 * 4)
        R_q = lambda h: ablk(QOFF + h * 1024, QOFF + (h + 1) * 1024) + ablk(QOFF + 4096 + h * 1024, QOFF + 4096 + (h + 1) * 1024)
        R_qall = ablk(QOFF, QOFF + 8192)
        R_E2 = lambda i: ablk(EOFF + i * 2048, EOFF + (i + 1) * 2048)
        R_yc = lambda q: ablk(YCOFF + q * 1024, YCOFF + (q + 1) * 1024)

        def bcast_mid(ap2d, n):
            pr = [list(p) for p in ap2d.ap]
            return bass.AP(ap2d.tensor, ap2d.offset, [pr[0], [0, n]] + pr[1:])

        def bcast_last(ap2d, n):
            pr = [list(p) for p in ap2d.ap]
            return bass.AP(ap2d.tensor, ap2d.offset, pr + [[0, n]])

        cv = lambda col: cvec[:, col:col + 1]

        def I(eng, meth, *args, reads=(), writes=(), dma=None, final=False, **kw):
            return add(eng, lambda e: getattr(e, meth)(*args, **kw), reads, writes, dma, final)

        XF = [("xf", k) for k in range(8)]
        F0r = [("F0", n) for n in range(4)]
        F1r = [("F1", n) for n in range(4)]
        F2r = [("F2", n) for n in range(4)]
        F0 = FF[:, 0:4, :]
        F1 = FF[:, 4:8, :]
        F2 = FF[:, 8:12, :]

        def x_src(ti):
            return x_d[:, ti * TT:(ti + 1) * TT].rearrange("(k p) n -> p k n", p=128)

        def load_xb(ti):
            I("pool", "dma_start", out=xb[:], in_=x_src(ti), writes=[("xb", k) for k in range(8)], dma="xbin")

        def load_xf(ti):
            I("pool", "dma_start", out=xf[:], in_=x_src(ti), writes=XF, dma="xfin")

        load_xb(0)
        def convert_weights():
            for g in range(NP // 2):
                I("pool", "dma_start", out=wsc_d[2 * g:2 * g + 2].rearrange("a p n -> p a n"),
                  in_=wall_d[2 * g:2 * g + 2].rearrange("a p n -> p a n"), writes=[("wsc", g)], dma="cv%d" % g)
                if g == 1:
                    load_xf(0)

        convert_weights()
        I("sp", "dma_start", out=cvec[:], in_=cvec_d, writes=["cvec"], dma="c0")
        I("sp", "dma_start", out=cbc[:], in_=cbc_d, writes=["cbc"], dma="c1")
        I("sp", "dma_start", out=wstf[:], in_=wst_d, writes=["wstf"], dma="c2")

        I("pool", "memset", identf[:], 0.0, writes=["identf"])
        I("pool", "affine_select", out=identf[:], in_=identf[:], pattern=[[-1, 128]], compare_op=ALU.not_equal,
          fill=1.0, base=0, channel_multiplier=1, reads=["identf"], writes=["identf"])
        I("pool", "tensor_copy", identb[:], identf[:], reads=["identf"], writes=["identb"])
        I("pool", "memset", trif[:], 1.0, writes=["trif"])
        I("pool", "affine_select", out=trif[:], in_=trif[:], pattern=[[1, 128]], compare_op=ALU.is_ge,
          fill=0.0, base=0, channel_multiplier=-1, reads=["trif"], writes=["trif"])
        I("pool", "tensor_copy", trib[:], trif[:], reads=["trif"], writes=["trib"])
        I("pool", "memset", ones1024[:], 1.0 / 1024.0, writes=["ones1024"])
        I("pool", "memset", ones512[:], 1.0 / 512.0, writes=["ones512"])
        I("pool", "memset", zhalo[:], 0.0, writes=["zhalo"])
        I("pool", "memset", chhalo[:], 0.0, writes=["chhalo"])
        I("pool", "memset", Vx[:, :, :, 128:130], 1.0, writes=["Vx_all"])
        I("pool", "tensor_tensor", out=wstf[:], in0=wstf[:], in1=bcast_mid(trif[:], 4), op=ALU.mult,
          reads=["wstf", "trif"], writes=["wstf"])
        I("pool", "tensor_copy", WcT[:], wstf[:], reads=["wstf"], writes=["WcT"])
        I("dve", "tensor_scalar", out=gsc[:], in0=cbc[:, CB_SG:CB_SG + 128],
          scalar1=(1.0 - LAMBDA_INIT) * math.sqrt(128.0), scalar2=None, op0=ALU.mult, reads=["cbc"], writes=["gsc"])
        I("dve", "tensor_tensor", out=lamt[:], in0=cbc[:, CB_LQ1:CB_LQ1 + 64], in1=cbc[:, CB_LK1:CB_LK1 + 64],
          op=ALU.mult, reads=["cbc"], writes=["lamt"])
        I("dve", "tensor_reduce", out=lams[:, 0:1], in_=lamt[:], axis=AX.X, op=ALU.add, reads=["lamt"], writes=["lams"])
        I("dve", "tensor_tensor", out=lamt[:], in0=cbc[:, CB_LQ2:CB_LQ2 + 64], in1=cbc[:, CB_LK2:CB_LK2 + 64],
          op=ALU.mult, reads=["cbc", "lams"], writes=["lamt"])
        I("dve", "tensor_reduce", out=lams[:, 1:2], in_=lamt[:], axis=AX.X, op=ALU.add, reads=["lamt"], writes=["lams"])
        I("act", "activation", lams[:, 2:4], lams[:, 0:2], AF.Exp, reads=["lams"], writes=["lams"])
        I("dve", "tensor_tensor", out=lams[:, 4:5], in0=lams[:, 3:4], in1=lams[:, 2:3], op=ALU.subtract,
          reads=["lams"], writes=["lams"])
        I("dve", "tensor_scalar", out=lams[:, 4:5], in0=lams[:, 4:5], scalar1=-LAMBDA_INIT, scalar2=None,
          op0=ALU.add, reads=["lams"], writes=["lams"])
        neglam = lams[:, 4:5]

        state = {"slot": 0, "bg": 0, "rb": 0}

        def load_piece(pidx):
            slot = state["slot"]
            state["slot"] = (slot + 1) % NSLOT
            I("sp", "dma_start", out=wring[:, slot, :], in_=wsc_d[pidx],
              reads=[("wsc", pidx // 2)], writes=[("w", slot)], dma="w%d" % slot)
            return slot

        def bank_group():
            g = state["bg"]
            state["bg"] = 1 - g
            return [4 * g + i for i in range(4)]

        def one_bank():
            b = state["rb"]
            state["rb"] = (b + 1) % 8
            return b

        def wv(slot, pat, **kw):
            return wring[:, slot, :].rearrange(pat, **kw)

        def mm_feature(slot, banks, rhs_of, rhs_res, nk=8, kouter=False):
            w = wv(slot, "p (k n) -> p k n", n=512)
            order = [(k, n) for k in range(nk) for n in range(4)] if kouter else \
                    [(k, n) for n in range(4) for k in range(nk)]
            for k, n in order:
                if True:
                    I("pe", "matmul", ps[:, banks[n], :], w[:, k, n * 128:(n + 1) * 128], rhs_of(k),
                      start=(k == 0), stop=(k == nk - 1),
                      reads=[("w", slot)] + rhs_res(k), writes=[("ps", banks[n])])

        def mm_token(slot, banks, kouter=False):
            w = wv(slot, "p (k n) -> p k n", n=512)
            order = [(k, tt) for k in range(8) for tt in range(4)] if kouter else \
                    [(k, tt) for tt in range(4) for k in range(8)]
            for k, tt in order:
                if True:
                    I("pe", "matmul", ps[:, banks[tt], :], xb[:, k, tt * 128:(tt + 1) * 128], w[:, k, :],
                      start=(k == 0), stop=(k == 7), reads=[("w", slot), ("xb", k)], writes=[("ps", banks[tt])])

        R_xb = lambda k: [("xb", k)]
        xb_of = lambda k: xb[:, k, :]

        def stats_tail(bm, be, after_var=None):
            I("act", "activation", RSTD[:], ps[:, bm, :], AF.Square, reads=[("ps", bm)], writes=["RSTD"])
            I("act", "activation", MEAN[:], ps[:, bm, :], AF.Copy, reads=[("ps", bm)], writes=["MEAN"])
            I("dve", "scalar_tensor_tensor", out=RSTD[:], in0=ps[:, be, :], scalar=LN_EPS, in1=RSTD[:],
              op0=ALU.add, op1=ALU.subtract, reads=[("ps", be), "RSTD"], writes=["RSTD"])
            I("act", "activation", RSTD[:], RSTD[:], AF.Ln, reads=["RSTD"], writes=["RSTD"])
            I("act", "activation", RSTD[:], RSTD[:], AF.Exp, scale=-0.5, reads=["RSTD"], writes=["RSTD"])
            if after_var is not None:
                after_var()

        R_FF = lambda c: [("F0", c)] if c < 4 else [("F1", c - 4)]

        def layer_norm(gcol, bcol, write_xb=True):
            bm, be = one_bank(), one_bank()
            I("pe", "matmul", ps[:, bm, :], ones1024[:], S1[:], start=True, stop=True,
              reads=["S1", "ones1024"], writes=[("ps", bm)])
            I("pe", "matmul", ps[:, be, :], ones1024[:], S2[:], start=True, stop=True,
              reads=["S2", "ones1024"], writes=[("ps", be)])
            def sub(k):
                I("dve", "tensor_tensor", out=xf[:, k, :], in0=xf[:, k, :], in1=MEAN[:], op=ALU.subtract,
                  reads=[("xf", k), "MEAN"], writes=[("xf", k)])

            def center():
                for k in range(4):
                    sub(k)
            stats_tail(bm, be, center)
            for k in range(8):
                I("dve", "tensor_tensor", out=xf[:, k, :], in0=xf[:, k, :], in1=RSTD[:], op=ALU.mult,
                  reads=[("xf", k), "RSTD"], writes=[("xf", k)])
                if k + 4 < 8:
                    sub(k + 4)
                if write_xb:
                    I("act", "activation", xb[:, k, :], xf[:, k, :], AF.Identity, bias=cv(bcol + k),
                      scale=cv(gcol + k), reads=[("xf", k), "cvec"], writes=[("xb", k)])
            for k in range(8):
                I("act", "activation", xf[:, k, :], xf[:, k, :], AF.Identity, bias=cv(bcol + k), scale=cv(gcol + k),
                  reads=[("xf", k), "cvec"], writes=[("xf", k)])

        def residual_evac(banks, c0):
            for n in range(4):
                c = c0 + n
                I("dve", "scalar_tensor_tensor", out=xf[:, c, :], in0=xf[:, c, :], scalar=ALPHA,
                  in1=ps[:, banks[n], :], op0=ALU.mult, op1=ALU.add,
                  reads=[("xf", c), ("ps", banks[n])], writes=[("xf", c)])
                I("act", "activation", FF[:, c, :], xf[:, c, :], AF.Square, reads=[("xf", c)], writes=R_FF(c))
                if c == 6:
                    I("act", "activation", lnd[:], ones1024[:, 0:1], AF.Ln, reads=["ones1024"], writes=["lnd"])
                if c == 1:
                    I("dve", "tensor_tensor", out=S1[:], in0=xf[:, 0, :], in1=xf[:, 1, :], op=ALU.add,
                      reads=[("xf", 0), ("xf", 1)], writes=["S1"])
                    I("pool", "tensor_tensor", out=S2[:], in0=FF[:, 0, :], in1=FF[:, 1, :], op=ALU.add,
                      reads=R_FF(0) + R_FF(1), writes=["S2"])
                elif c > 1:
                    I("dve", "tensor_tensor", out=S1[:], in0=S1[:], in1=xf[:, c, :], op=ALU.add,
                      reads=[("xf", c), "S1"], writes=["S1"])
                    I("dve" if c == 7 else "pool", "tensor_tensor", out=S2[:], in0=S2[:], in1=FF[:, c, :], op=ALU.add,
                      reads=R_FF(c) + ["S2"], writes=["S2"])

        def out_proj(pbase):
            for j in range(2):
                slot = load_piece(pbase + 5 + j)
                banks = bank_group()
                mm_feature(slot, banks, lambda k: yT[:, k, :], R_yT)
                residual_evac(banks, 4 * j)

        def ffn(pbase, after_gateup=None):
            for j in range(11):
                slot = load_piece(pbase + 7 + j)
                banks = bank_group()
                w = wv(slot, "p (k a n) -> p k a n", a=2, n=256)
                order = [(k, a, hf) for k in range(8) for a in range(2) for hf in range(2)] if j == 0 else \
                        [(k, a, hf) for hf in range(2) for a in range(2) for k in range(8)]
                for k, a, hf in order:
                    if True:
                        if True:
                            b = banks[2 * a + hf]
                            I("pe", "matmul", ps[:, b, :], w[:, k, a, hf * 128:(hf + 1) * 128], xb[:, k, :],
                              start=(k == 0), stop=(k == 7), reads=[("w", slot), ("xb", k)], writes=[("ps", b)])
                for hf in range(2):
                    fc = 2 * j + hf
                    sgi = fc % 3
                    I("act", "activation", SG[:, sgi, :], ps[:, banks[hf], :], AF.Silu,
                      reads=[("ps", banks[hf])], writes=[("SG", sgi)])
                    I("dve", "tensor_tensor", out=hT[:, fc, :], in0=SG[:, sgi, :], in1=ps[:, banks[2 + hf], :],
                      op=ALU.mult, reads=[("SG", sgi), ("ps", banks[2 + hf])], writes=R_hT(fc))
            if after_gateup is not None:
                after_gateup()
            for dh in range(2):
                banks = bank_group()
                for g, (lo, hi) in enumerate(DOWN_GROUPS):
                    slot = load_piece(pbase + 18 + dh * 3 + g)
                    w = wv(slot, "p (k n) -> p k n", n=512)
                    order = [(i, fc, n) for i, fc in enumerate(range(lo, hi)) for n in range(4)] if g < 2 else \
                            [(i, fc, n) for n in range(4) for i, fc in enumerate(range(lo, hi))]
                    for i, fc, n in order:
                        if True:
                            I("pe", "matmul", ps[:, banks[n], :], w[:, i, n * 128:(n + 1) * 128], hT[:, fc, :],
                              start=(fc == 0), stop=(fc == NFC - 1),
                              reads=[("w", slot)] + R_hT(fc), writes=[("ps", banks[n])])
                residual_evac(banks, 4 * dh)

        def build_dg(n):
            wtap = cvec[:, CV_CB + n:CV_CB + n + 1]
            pr = [list(p) for p in wtap.ap]
            w_bc = bass.AP(wtap.tensor, wtap.offset, [pr[0], [4, 31], [0, 128]])
            I("dve", "tensor_tensor", out=Dg[(n + 1) % 2], in0=bcast_mid(identb[:], 31), in1=w_bc, op=ALU.mult,
              reads=["identb", "cvec"], writes=R_dg[(n + 1) % 2])

        def mixer0(ti, pbase):
            build_dg(0)
            for n in range(4):
                I("pool", "tensor_copy", zbuf[:, n, 0:30], zhalo[:, n, :], reads=["zhalo"], writes=R_z(n))
            slot = load_piece(pbase + 1)
            banks = bank_group()
            mm_feature(slot, banks, xb_of, R_xb, kouter=True)
            for n in range(4):
                I("act", "activation", F0[:, n, :], ps[:, banks[n], :], AF.Copy,
                  reads=[("ps", banks[n])], writes=[("F0", n)])
            slot = load_piece(pbase + 2)
            banks = bank_group()
            mm_feature(slot, banks, xb_of, R_xb)
            for n in range(4):
                I("pool", "tensor_copy", chbuf[:, n, 0:2], chhalo[:, n, :], reads=["chhalo"], writes=R_ch(n))
                I("dve", "tensor_tensor", out=chbuf[:, n, 2:514], in0=F0[:, n, :], in1=ps[:, banks[n], :], op=ALU.mult,
                  reads=[("F0", n), ("ps", banks[n])], writes=R_ch(n))
            slot = load_piece(pbase + 0)
            banks = bank_group()
            mm_feature(slot, banks, xb_of, R_xb)
            for n in range(4):
                I("act", "activation", F1[:, n, :], ps[:, banks[n], :], AF.Copy,
                  reads=[("ps", banks[n])], writes=[("F1", n)])
            for n in range(4):
                I("dve", "tensor_scalar", out=F0[:, n, :], in0=chbuf[:, n, 2:514], scalar1=cv(CV_CA + 8 + n),
                  scalar2=None, op0=ALU.mult, reads=R_ch(n) + ["cvec"], writes=[("F0", n)])
                for k in (1, 0):
                    I("dve", "scalar_tensor_tensor", out=F0[:, n, :], in0=chbuf[:, n, k:k + 512],
                      scalar=cv(CV_CA + 4 * k + n), in1=F0[:, n, :], op0=ALU.mult, op1=ALU.add,
                      reads=R_ch(n) + [("F0", n)], writes=[("F0", n)])
                I("pool", "tensor_copy", chhalo[:, n, :], chbuf[:, n, 512:514], reads=R_ch(n), writes=["chhalo"])
                I("dve", "tensor_tensor", out=yT[:, n, :], in0=F0[:, n, :], in1=F1[:, n, :], op=ALU.mult,
                  reads=[("F0", n), ("F1", n)], writes=R_yT(n))
            build_dg(1)
            slot = load_piece(pbase + 4)
            banks = bank_group()
            mm_feature(slot, banks, xb_of, R_xb)
            for n in range(4):
                I("act", "activation", F0[:, n, :], ps[:, banks[n], :], AF.Sigmoid,
                  reads=[("ps", banks[n])], writes=[("F0", n)])
            slot = load_piece(pbase + 3)
            banks = bank_group()
            mm_feature(slot, banks, xb_of, R_xb)
            for n in range(4):
                I("dve", "tensor_tensor", out=zbuf[:, n, 30:542], in0=F0[:, n, :], in1=ps[:, banks[n], :], op=ALU.mult,
                  reads=[("F0", n), ("ps", banks[n])], writes=R_z(n))
            banks = bank_group()
            for n in range(4):
                d = (n + 1) % 2
                for k in range(31):
                    I("pe", "matmul", ps[:, banks[n], :], Dg[d][:, k, :], zbuf[:, n, k:k + 512],
                      start=(k == 0), stop=(k == 30), reads=R_dg[d] + R_z(n), writes=[("ps", banks[n])])
                if n + 2 < 4:
                    build_dg(n + 2)
                I("pool", "tensor_copy", zhalo[:, n, :], zbuf[:, n, 512:542], reads=R_z(n), writes=["zhalo"])
                I("act", "activation", F1[:, n, :], ps[:, banks[n], :], AF.Identity, bias=cv(CV_BB + n),
                  reads=[("ps", banks[n]), "cvec"], writes=[("F1", n)])
                I("act", "activation", F2[:, n, :], ps[:, banks[n], :], AF.Square, bias=cv(CV_BB + n),
                  reads=[("ps", banks[n]), "cvec"], writes=[("F2", n)])
                if n == 2:
                    I("act", "activation", lnd[:], ones1024[:, 0:1], AF.Ln, reads=["ones1024"], writes=["lnd"])
                if n == 1:
                    I("dve", "tensor_tensor", out=S1[:], in0=F1[:, 0, :], in1=F1[:, 1, :], op=ALU.add,
                      reads=[("F1", 0), ("F1", 1)], writes=["S1"])
                    I("pool", "tensor_tensor", out=S2[:], in0=F2[:, 0, :], in1=F2[:, 1, :], op=ALU.add,
                      reads=[("F2", 0), ("F2", 1)], writes=["S2"])
                elif n > 1:
                    I("dve", "tensor_tensor", out=S1[:], in0=S1[:], in1=F1[:, n, :], op=ALU.add,
                      reads=[("F1", n), "S1"], writes=["S1"])
                    I("pool", "tensor_tensor", out=S2[:], in0=S2[:], in1=F2[:, n, :], op=ALU.add,
                      reads=[("F2", n), "S2"], writes=["S2"])
            bm, be = one_bank(), one_bank()
            I("pe", "matmul", ps[:, bm, :], ones512[:], S1[:], start=True, stop=True,
              reads=["S1", "ones512"], writes=[("ps", bm)])
            I("pe", "matmul", ps[:, be, :], ones512[:], S2[:], start=True, stop=True,
              reads=["S2", "ones512"], writes=[("ps", be)])
            def center():
                for n in range(4):
                    I("dve", "tensor_tensor", out=F1[:, n, :], in0=F1[:, n, :], in1=MEAN[:], op=ALU.subtract,
                      reads=[("F1", n), "MEAN"], writes=[("F1", n)])
            stats_tail(bm, be, center)
            for n in range(4):
                I("dve", "tensor_tensor", out=F1[:, n, :], in0=F1[:, n, :], in1=RSTD[:], op=ALU.mult,
                  reads=[("F1", n), "RSTD"], writes=[("F1", n)])
                I("act", "activation", yT[:, 4 + n, :], F1[:, n, :], AF.Silu, bias=cv(CV_LB + n), scale=cv(CV_LG + n),
                  reads=[("F1", n), "cvec"], writes=R_yT(4 + n))

        VGOFF = YCOFF + 4096
        vgln = aview(VGOFF, BF16, 4 * 512).rearrange("p (c n) -> p c n", n=512)
        R_vg = ablk(VGOFF, VGOFF + 4096)
        psb = ps[:].bitcast(BF16)

        def mixer1(ti, pbase):
            slot = load_piece(pbase + 4)
            banks = bank_group()
            mm_token(slot, banks, kouter=True)
            for tt in range(4):
                I("act", "activation", F1[:, tt, :], ps[:, banks[tt], :], AF.Gelu_apprx_tanh,
                  reads=[("ps", banks[tt])], writes=[("F1", tt)])
            F1g = F1.rearrange("p t (g d) -> p (t g) d", d=128)
            F2g = F2.rearrange("p t (g d) -> p (t g) d", d=128)
            I("dve", "tensor_reduce", out=st16[:, 0, :], in_=F1g, axis=AX.X, op=ALU.add, reads=F1r, writes=["st16"])
            I("act", "activation", F2, F1, AF.Square, reads=F1r, writes=F2r)
            I("dve", "tensor_reduce", out=st16[:, 1, :], in_=F2g, axis=AX.X, op=ALU.add,
              reads=F2r + ["st16"], writes=["st16"])
            I("dve", "tensor_scalar", out=st16[:, 2, :], in0=st16[:, 0, :], scalar1=1.0 / 128.0, scalar2=None,
              op0=ALU.mult, reads=["st16"], writes=["st16"])
            I("dve", "tensor_tensor", out=st16[:, 3, :], in0=st16[:, 2, :], in1=st16[:, 2, :], op=ALU.mult,
              reads=["st16"], writes=["st16"])
            I("dve", "scalar_tensor_tensor", out=st16[:, 4, :], in0=st16[:, 1, :], scalar=1.0 / 128.0,
              in1=st16[:, 3, :], op0=ALU.mult, op1=ALU.subtract, reads=["st16"], writes=["st16"])
            I("dve", "tensor_scalar", out=st16[:, 5, :], in0=st16[:, 4, :], scalar1=LN_EPS, scalar2=None,
              op0=ALU.add, reads=["st16"], writes=["st16"])
            slot = load_piece(pbase + 3)
            banks = bank_group()
            mm_feature(slot, banks, xb_of, R_xb)
            for n in range(4):
                I("act", "activation", F0[:, n, :], ps[:, banks[n], :], AF.Gelu_apprx_tanh,
                  reads=[("ps", banks[n])], writes=[("F0", n)])
            slot = load_piece(pbase + 0)
            banks = bank_group()
            mm_feature(slot, banks, xb_of, R_xb)
            I("pool", "memset", qz[0][64:128, :, :], 0.0, writes=R_qall)
            I("pool", "memset", qz[1][0:64, :, :], 0.0, writes=R_qall)
            for n in range(4):
                I("act", "activation", qz[0][0:64, n, :], ps[0:64, banks[n], :], AF.Copy,
                  reads=[("ps", banks[n])], writes=R_q(n))
                I("act", "activation", qz[1][64:128, n, :], ps[64:128, banks[n], :], AF.Copy,
                  reads=[("ps", banks[n])], writes=R_q(n))
            I("act", "activation", st16[:, 5, :], st16[:, 5, :], AF.Ln, reads=["st16"], writes=["st16"])
            I("act", "activation", st16[:, 5, :], st16[:, 5, :], AF.Exp, scale=-0.5, reads=["st16"], writes=["st16"])
            I("dve", "tensor_tensor", out=F1g, in0=F1g, in1=bcast_last(st16[:, 2, :], 128), op=ALU.subtract,
              reads=F1r + ["st16"], writes=F1r)
            I("dve", "tensor_tensor", out=F1g, in0=F1g, in1=bcast_last(st16[:, 5, :], 128), op=ALU.mult,
              reads=F1r + ["st16"], writes=F1r)
            I("pool", "tensor_tensor", out=F1, in0=F1, in1=bcast_mid(cbc[:, CB_GG:CB_GG + 512], 4), op=ALU.mult,
              reads=F1r + ["cbc"], writes=F1r)
            I("pool", "tensor_tensor", out=vgln, in0=F1, in1=bcast_mid(cbc[:, CB_GB:CB_GB + 512], 4), op=ALU.add,
              reads=F1r + ["cbc"], writes=R_vg)
            slot = load_piece(pbase + 1)
            banks = bank_group()
            mm_feature(slot, banks, xb_of, R_xb)
            for n in range(4):
                I("dve", "tensor_copy", KT[:, n, ti * TT:(ti + 1) * TT], ps[:, banks[n], :],
                  reads=[("ps", banks[n])], writes=[("KT", ti)])
            slot = load_piece(pbase + 2)
            banks = bank_group()
            mm_token(slot, banks)
            for tt in range(4):
                I("act", "activation", Vx[:, 4 * ti + tt, :, 0:128],
                  ps[:, banks[tt], :].rearrange("p (h e) -> p h e", e=128), AF.Copy,
                  reads=[("ps", banks[tt]), "Vx_all"], writes=[("Vx", 4 * ti + tt)])
            nkt = 4 * ti + 4
            spairs = ((3, 4), (5, 6))
            accb = (0, 1, 2)
            acc_sb = (F1, SG[:])
            acc_sb_res = (F1r, [("SG", i) for i in range(3)])

            def acc_ap(qt, m):
                a = qt * 2 + m
                return ps[:, accb[a // 3], (a % 3) * 129:(a % 3) * 129 + 129], ("ps", accb[a // 3])

            def acc_copy_ap(h, qt, m):
                a = qt * 2 + m
                return acc_sb[h % 2][:, a // 3, (a % 3) * 129:(a % 3) * 129 + 129]

            def rec_pv(blk, pi2, bi):
                h, m, kt = blk
                q0 = max(0, kt - 4 * ti)
                for qt in range(q0, 4):
                    dst, dres = acc_ap(qt, m)
                    I("pe", "matmul", dst, E2[pi2][:, bi, (qt - q0) * 128:(qt - q0 + 1) * 128], Vx[:, kt, h, 0:129],
                      start=(kt == 0 and m == 0 and qt in (0, 2, 3)), stop=(kt == 4 * ti + qt),
                      skip_group_check=True, reads=R_E2(pi2) + [("Vx", kt), "Vx_all"], writes=[dres])

            def epilogue(h):
                sbuf_res = acc_sb_res[h % 2]
                for bi_, ncols in ((0, 387), (1, 387), (2, 258)):
                    I("dve", "tensor_copy", acc_sb[h % 2][:, bi_, 0:ncols], ps[:, accb[bi_], 0:ncols],
                      reads=[("ps", accb[bi_])], writes=sbuf_res)
                for qt in range(4):
                    a0 = acc_copy_ap(h, qt, 0)
                    a1 = acc_copy_ap(h, qt, 1)
                    pi = qt % 2
                    EP = [("ep", pi)]
                    eo = F2[:, h, qt * 128:(qt + 1) * 128]
                    I("dve", "reciprocal", ep[:, pi, 0:1], a0[:, 128:129], reads=sbuf_res, writes=EP)
                    I("dve", "reciprocal", ep[:, pi, 1:2], a1[:, 128:129], reads=sbuf_res + EP, writes=EP)
                    I("dve", "tensor_tensor", out=ep[:, pi, 2:3], in0=ep[:, pi, 1:2], in1=neglam, op=ALU.mult,
                      reads=EP + ["lams"], writes=EP)
                    I("dve", "tensor_scalar", out=epT[:, pi, :], in0=a1[:, 0:128], scalar1=ep[:, pi, 2:3],
                      scalar2=None, op0=ALU.mult, reads=sbuf_res + EP, writes=[("epT", pi)])
                    I("dve", "scalar_tensor_tensor", out=eo, in0=a0[:, 0:128], scalar=ep[:, pi, 0:1],
                      in1=epT[:, pi, :], op0=ALU.mult, op1=ALU.add,
                      reads=sbuf_res + [("epT", pi)] + EP, writes=[("F2", h)])
                    I("dve", "scalar_tensor_tensor", out=epJ[:], in0=eo, scalar=1.0, in1=eo, op0=ALU.mult, op1=ALU.mult,
                      accum_out=ss16[:, 4 * h + qt:4 * h + qt + 1], reads=[("F2", h)], writes=["epJ", "ss16"])

            blocks = [(h, m, kt) for h in range(4) for m in range(2) for kt in range(nkt)]
            npair = len(blocks) // 2

            def rec_s(p):
                for bi in range(2):
                    h, m, kt = blocks[2 * p + bi]
                    c0 = max(0, kt - 4 * ti) * 128
                    bk = spairs[p % 2][bi]
                    I("pe", "matmul", ps[:, bk, 0:512 - c0], KT[:, h, kt * 128:(kt + 1) * 128],
                      qz[m][:, h, c0:512], start=True, stop=True,
                      reads=[("KT", kt // 4)] + R_q(h), writes=[("ps", bk)])

            rec_s(0)
            rec_s(1)
            for p in range(npair):
                b0, b1 = spairs[p % 2]
                pi2 = p % 2
                h, m, kt = blocks[2 * p]
                if kt + 1 < 4 * ti:
                    I("act", "activation", E2[pi2], ps[:, b0:b0 + 2, :], AF.Exp, scale=0.125,
                      reads=[("ps", b0), ("ps", b1)], writes=R_E2(pi2))
                else:
                    for bi in range(2):
                        j = kt + bi - 4 * ti
                        ncol = 512 - j * 128
                        I("act", "activation", E2[pi2][:, bi, 0:ncol], ps[:, spairs[p % 2][bi], 0:ncol], AF.Exp,
                          scale=0.125, reads=[("ps", spairs[p % 2][bi])], writes=R_E2(pi2))
                        I("pool", "tensor_tensor", out=E2[pi2][:, bi, 0:128], in0=E2[pi2][:, bi, 0:128], in1=trib[:],
                          op=ALU.mult, reads=R_E2(pi2) + ["trib"], writes=R_E2(pi2))
                if p + 2 < npair:
                    rec_s(p + 2)
                for bi in range(2):
                    rec_pv(blocks[2 * p + bi], pi2, bi)
                if m == 1 and kt + 1 == nkt - 1:
                    epilogue(h)
            I("dve", "tensor_scalar", out=ss16[:, 16:32], in0=ss16[:, 0:16], scalar1=128.0 * LN_EPS, scalar2=None,
              op0=ALU.add, reads=["ss16"], writes=["ss16b"])
            I("act", "activation", ss16[:, 16:32], ss16[:, 16:32], AF.Ln, reads=["ss16b"], writes=["ss16b"])
            I("act", "activation", ss16[:, 16:32], ss16[:, 16:32], AF.Exp, scale=-0.5, reads=["ss16b"], writes=["ss16b"])
            for h in range(4):
                for qt in range(4):
                    I("dve", "scalar_tensor_tensor", out=yc[:, qt, h * 128:(h + 1) * 128],
                      in0=F2[:, h, qt * 128:(qt + 1) * 128], scalar=ss16[:, 16 + 4 * h + qt:17 + 4 * h + qt],
                      in1=gsc[:], op0=ALU.mult, op1=ALU.mult, reads=[("F2", h), "gsc", "ss16b"], writes=R_yc(qt))
            banks = bank_group()
            for g in range(4):
                for tt in range(4):
                    I("pe", "matmul", ps[:, banks[g], tt * 128:(tt + 1) * 128], vgln[:, tt, g * 128:(g + 1) * 128],
                      WcT[:, g, :], start=True, stop=True, reads=R_vg + ["WcT"], writes=[("ps", banks[g])])
            for g in range(4):
                I("dve", "tensor_tensor", out=F2[:, g, :].rearrange("p (t s) -> p t s", s=128),
                  in0=ps[:, banks[g], :].rearrange("p (t s) -> p t s", s=128),
                  in1=bcast_mid(cbc[:, CB_SB + 128 * g:CB_SB + 128 * g + 128], 4), op=ALU.add,
                  reads=[("ps", banks[g]), "cbc"], writes=[("F2", g)])
                I("pool", "tensor_tensor", out=yT[:, 4 + g, :], in0=F2[:, g, :], in1=F0[:, g, :], op=ALU.mult,
                  reads=[("F2", g), ("F0", g)], writes=R_yT(4 + g))
            for h in range(4):
                b = (6, 7)[h % 2]
                for qt in range(4):
                    I("pe", "transpose", psb[:, b, qt * 128:(qt + 1) * 128], yc[:, qt, h * 128:(h + 1) * 128],
                      identb[:], reads=R_yc(qt) + ["identb"], writes=[("ps", b)])
                I("act", "activation", yT[:, h, :], psb[:, b, 0:512], AF.Copy, reads=[("ps", b)], writes=R_yT(h))


        def stage(name):
            sch.tag = "after_" + name
            if stop_at == name:
                raise _Stop()

        def main_loop():
            for ti in range(n_tiles):
                stage("x")
                for L in range(2):
                    pbase = L * NPL
                    if L == 0:
                        mixer0(ti, pbase)
                        if ti > 0:
                            load_xf(ti)
                    else:
                        mixer1(ti, pbase)
                    stage("mix%d" % L)
                    if dbg and ("y%d" % L) in dbg_d and ti == 0:
                        for c in range(8):
                            I("pool", "tensor_copy", FF[:, 8 + c % 4, :], yT[:, c, :], reads=R_yT(c), writes=[("F2", c % 4)])
                            I("pool", "dma_start", out=dbg_d["y%d" % L][c], in_=FF[:, 8 + c % 4, :],
                              reads=[("F2", c % 4)], dma="dbg%s%d" % ("y%d" % L, c), final=True)
                    out_proj(pbase)
                    stage("out%d" % L)
                    layer_norm(CV_MG + 8 * L, CV_MB + 8 * L)
                    stage("ln%d" % L)
                    if dbg and ("m%d" % L) in dbg_d and ti == 0:
                        I("pool", "dma_start", out=dbg_d["m%d" % L].rearrange("k p n -> p k n"), in_=xf[:],
                          reads=XF, dma="dbgm%d" % L, final=True)
                    ffn(pbase, (lambda: load_xb(ti + 1)) if (L == 1 and ti + 1 < n_tiles) else None)
                    stage("ffn%d" % L)
                    layer_norm(CV_FG + 8 * L, CV_FB + 8 * L, write_xb=(L == 0))
                    stage("fln%d" % L)
                    if dbg and ("f%d" % L) in dbg_d and ti == 0:
                        I("pool", "dma_start", out=dbg_d["f%d" % L].rearrange("k p n -> p k n"), in_=xf[:],
                          reads=XF, dma="dbgf%d" % L, final=True)
                I("act", "dma_start", out=out_d[:, ti * TT:(ti + 1) * TT].rearrange("(k p) n -> p k n", p=128), in_=xf[:],
                  reads=XF, dma="xout", final=True)

        try:
            main_loop()
        except _Stop:
            pass

        sch.emit()
    return nc


_NC_CACHE = {}


def kernel(**inputs):
    x = np.asarray(inputs["x"], np.float32)
    shared = make_shared(inputs)
    if "nc" not in _NC_CACHE:
        _NC_CACHE["nc"] = build_program()
    nc = _NC_CACHE["nc"]
    in_maps = [dict(shared, x=np.ascontiguousarray(x[b].T)) for b in range(8)]
    res = run_bass_kernel_spmd(nc, in_maps, core_ids=list(range(8)))
    return np.stack([np.ascontiguousarray(np.asarray(r["out"], np.float32).T) for r in res.results], axis=0)
```

```python
import math
from contextlib import ExitStack
import numpy as np
import concourse.bass as bass
import concourse.mybir as mybir
from concourse.bass_utils import run_bass_kernel_spmd

F32 = mybir.dt.float32
BF16 = mybir.dt.bfloat16
AF = mybir.ActivationFunctionType
ALU = mybir.AluOpType
AX = mybir.AxisListType

S, D, TT, NT = 4096, 1024, 512, 8
KC = 8
NFC = 22
ALPHA = (2 * 2) ** 0.25
LN_EPS = 1e-5
LAMBDA_INIT = 0.8 - 0.6 * math.exp(-0.3 * 1)
NPL = 24
NP = 2 * NPL
PW = 4096
NSLOT = 4
DOWN_GROUPS = [(0, 8), (8, 16), (16, 22)]
ARENA_B = 28704

CV_CA = 0
CV_CB = 12
CV_BB = 136
CV_LG = 140
CV_LB = 144
CV_MG = 148
CV_MB = 164
CV_FG = 180
CV_FB = 196
NCV = 212
CB_GG = 0
CB_GB = 512
CB_SB = 1024
CB_SG = 1536
CB_LQ1 = 1664
CB_LK1 = 1728
CB_LQ2 = 1792
CB_LK2 = 1856
NCB = 1920


def _piece_in(w, j):
    return np.ascontiguousarray(w.reshape(8, 128, -1)[:, :, j * 512:(j + 1) * 512].transpose(1, 0, 2)).reshape(128, PW)


def _piece_gu(g, u, j):
    a = np.empty((128, 8, 2, 256), np.float32)
    a[:, :, 0, :] = g.reshape(8, 128, -1)[:, :, j * 256:(j + 1) * 256].transpose(1, 0, 2)
    a[:, :, 1, :] = u.reshape(8, 128, -1)[:, :, j * 256:(j + 1) * 256].transpose(1, 0, 2)
    return a.reshape(128, PW)


def _piece_down(wd, dh, grp):
    lo, hi = DOWN_GROUPS[grp]
    a = np.zeros((128, 8, 512), np.float32)
    a[:, :hi - lo, :] = wd.reshape(NFC, 128, 1024)[lo:hi, :, dh * 512:(dh + 1) * 512].transpose(1, 0, 2)
    return a.reshape(128, PW)


def _chunked(v):
    return np.ascontiguousarray(np.asarray(v, np.float32).reshape(-1, 128).T)


def make_shared(inp):
    f = lambda k: np.asarray(inp[k], np.float32)
    wall = np.empty((NP, 128, PW), np.float32)
    for L in range(2):
        w_in = f("even_w_in")[0] if L == 0 else f("odd_w_in")[0]
        w_out = f("even_w_out")[0] if L == 0 else f("odd_w_out")[0]
        b = L * NPL
        for j in range(5):
            wall[b + j] = _piece_in(w_in, j)
        for j in range(2):
            wall[b + 5 + j] = _piece_in(w_out, j)
        for j in range(11):
            wall[b + 7 + j] = _piece_gu(f("ffn_w_gate")[L], f("ffn_w_up")[L], j)
        for dh in range(2):
            for g in range(3):
                wall[b + 18 + dh * 3 + g] = _piece_down(f("ffn_w_down")[L], dh, g)
    cvec = np.zeros((128, NCV), np.float32)
    ca = f("even_conv_a_w")[0]
    cb = f("even_conv_b_w")[0]
    for k in range(3):
        cvec[:, CV_CA + 4 * k:CV_CA + 4 * k + 4] = _chunked(ca[k])
    for k in range(31):
        cvec[:, CV_CB + 4 * k:CV_CB + 4 * k + 4] = _chunked(cb[k])
    cvec[:, CV_BB:CV_BB + 4] = _chunked(f("even_conv_b_bias")[0])
    cvec[:, CV_LG:CV_LG + 4] = _chunked(f("even_conv_ln_g")[0])
    cvec[:, CV_LB:CV_LB + 4] = _chunked(f("even_conv_ln_b")[0])
    for L in range(2):
        cvec[:, CV_MG + 8 * L:CV_MG + 8 * L + 8] = _chunked(f("mix_ln_g")[L])
        cvec[:, CV_MB + 8 * L:CV_MB + 8 * L + 8] = _chunked(f("mix_ln_b")[L])
        cvec[:, CV_FG + 8 * L:CV_FG + 8 * L + 8] = _chunked(f("ffn_ln_g")[L])
        cvec[:, CV_FB + 8 * L:CV_FB + 8 * L + 8] = _chunked(f("ffn_ln_b")[L])
    row = np.zeros((NCB,), np.float32)
    row[CB_GG:CB_GG + 512] = f("odd_gmlp_ln_g")[0].reshape(-1)
    row[CB_GB:CB_GB + 512] = f("odd_gmlp_ln_b")[0].reshape(-1)
    row[CB_SB:CB_SB + 512] = f("odd_spatial_b")[0].reshape(-1)
    row[CB_SG:CB_SG + 128] = f("odd_subln_g")[0]
    row[CB_LQ1:CB_LQ1 + 64] = f("odd_lambda_q1")[0]
    row[CB_LK1:CB_LK1 + 64] = f("odd_lambda_k1")[0]
    row[CB_LQ2:CB_LQ2 + 64] = f("odd_lambda_q2")[0]
    row[CB_LK2:CB_LK2 + 64] = f("odd_lambda_k2")[0]
    cbc = np.ascontiguousarray(np.broadcast_to(row[None, :], (128, NCB)))
    wst = np.ascontiguousarray(f("odd_spatial_w")[0].transpose(2, 0, 1))
    return {"wall": wall, "cvec": cvec, "cbc": cbc, "wst": wst}


class Op:
    __slots__ = ("eng", "fn", "deps", "signal", "sigval", "dma", "tag")


class Sched:
    ENGS = ("pe", "act", "dve", "pool", "sp")

    def __init__(self, nc, es):
        self.nc, self.es = nc, es
        self.ops = {e: [] for e in self.ENGS}
        self.last_w, self.readers = {}, {}
        self.sems = {}
        self.dma_cnt = {}
        self.final = []
        self.tag = ""

    def sem(self, key):
        if key not in self.sems:
            self.sems[key] = self.es.enter_context(self.nc.semaphore("s_" + key))
        return self.sems[key]

    def add(self, eng, fn, reads=(), writes=(), dma=None, final=False):
        op = Op()
        op.eng, op.fn, op.dma, op.signal, op.sigval = eng, fn, dma, False, 0
        op.tag = self.tag
        writes = list(writes) + [r for r in reads if isinstance(r, tuple) and r[0] == "ps"]
        deps = []
        for r in reads:
            d = self.last_w.get(r)
            if d is not None:
                deps.append(d)
        for w in writes:
            d = self.last_w.get(w)
            if d is not None:
                deps.append(d)
            deps.extend(self.readers.get(w, ()))
        for r in reads:
            self.readers.setdefault(r, []).append(op)
        for w in writes:
            self.last_w[w] = op
            self.readers[w] = []
        seen, dd = set(), []
        for d in deps:
            if d is op or id(d) in seen:
                continue
            seen.add(id(d))
            dd.append(d)
            if not (d.dma is None and d.eng == eng and eng in ("pe", "sp")):
                d.signal = True
        op.deps = dd
        if dma is not None:
            self.dma_cnt[dma] = self.dma_cnt.get(dma, 0) + 1
            op.sigval = 16 * self.dma_cnt[dma]
            self.sem(dma)
            if final:
                self.final.append(op)
        self.ops[eng].append(op)
        return op

    def emit(self):
        nc = self.nc
        for e in self.ENGS:
            c = 0
            for op in self.ops[e]:
                if op.dma is None and op.signal:
                    c += 1
                    op.sigval = c
            self.sem("eng_" + e)
        block = self.es.enter_context(nc.Block())

        def tok(d):
            return ("eng_" + d.eng if d.dma is None else d.dma), d.sigval

        def run(e, eng):
            waited = {}
            for op in self.ops[e]:
                for d in op.deps:
                    if d.dma is None and d.eng == e and e in ("pe", "sp"):
                        continue
                    key, val = tok(d)
                    if waited.get(key, 0) >= val:
                        continue
                    eng.wait_ge(self.sems[key], val)
                    waited[key] = val
                ins = op.fn(eng)
                if op.dma is not None:
                    ins.then_inc(self.sems[op.dma], 16)
                elif op.signal:
                    ins.then_inc(self.sems["eng_" + e], 1)
            if e == "act":
                for op in self.final:
                    eng.wait_ge(self.sems[op.dma], op.sigval)

        block.tensor(lambda eng: run("pe", eng))
        block.scalar(lambda eng: run("act", eng))
        block.vector(lambda eng: run("dve", eng))
        block.gpsimd(lambda eng: run("pool", eng))
        block.sync(lambda eng: run("sp", eng))


class _Stop(Exception):
    pass


def build_program(n_tiles=NT, dbg=None, stop_at=None):
    nc = bass.Bass("TRN2", target_bir_lowering=False)
    x_d = nc.dram_tensor("x", [D, S], F32, kind="ExternalInput").ap()
    wall_d = nc.dram_tensor("wall", [NP, 128, PW], F32, kind="ExternalInput").ap()
    cvec_d = nc.dram_tensor("cvec", [128, NCV], F32, kind="ExternalInput").ap()
    cbc_d = nc.dram_tensor("cbc", [128, NCB], F32, kind="ExternalInput").ap()
    wst_d = nc.dram_tensor("wst", [128, 4, 128], F32, kind="ExternalInput").ap()
    out_d = nc.dram_tensor("out", [D, S], F32, kind="ExternalOutput").ap()
    wsc_d = nc.dram_tensor("wsc", [NP, 128, PW], BF16, kind="Internal").ap()
    dbg_d = {}
    if dbg:
        for name, shape in dbg.items():
            dbg_d[name] = nc.dram_tensor("dbg_" + name, list(shape), F32, kind="ExternalOutput").ap()

    es = ExitStack()
    with es:
        sb = lambda name, shape, dt: es.enter_context(nc.sbuf_tensor("sb_" + name, shape, dt))
        xf = sb("xf", [128, KC, TT], F32)
        xb = sb("xb", [128, KC, TT], BF16)
        KT = sb("KT", [128, 4, S], BF16)
        Vx = sb("Vx", [128, S // 128, 4, 130], BF16)
        wring = sb("wring", [128, NSLOT, PW], BF16)
        FF = sb("FF", [128, 12, TT], F32)
        arena = sb("arena", [128, ARENA_B // 2], BF16)
        cvec = sb("cvec", [128, NCV], F32)
        cbc = sb("cbc", [128, NCB], F32)
        wstf = sb("wstf", [128, 4, 128], F32)
        WcT = sb("WcT", [128, 4, 128], BF16)
        identb = sb("identb", [128, 128], BF16)
        identf = sb("identf", [128, 128], F32)
        trif = sb("trif", [128, 128], F32)
        trib = sb("trib", [128, 128], BF16)
        ones1024 = sb("ones1024", [128, 128], F32)
        ones512 = sb("ones512", [128, 128], F32)
        S1 = sb("S1", [128, TT], F32)
        S2 = sb("S2", [128, TT], F32)
        MEAN = sb("MEAN", [128, TT], F32)
        RSTD = sb("RSTD", [128, TT], F32)
        SG = sb("SG", [128, 3, TT], F32)
        zhalo = sb("zhalo", [128, 4, 30], BF16)
        chhalo = sb("chhalo", [128, 4, 2], F32)
        st16 = sb("st16", [128, 6, 16], F32)
        gsc = sb("gsc", [128, 128], F32)
        lamt = sb("lamt", [128, 64], F32)
        lams = sb("lams", [128, 8], F32)
        ep = sb("ep", [128, 2, 8], F32)
        ss16 = sb("ss16", [128, 32], F32)
        lnd = sb("lnd", [128, 1], F32)
        epT = sb("epT", [128, 2, 128], F32)
        epO = sb("epO", [128, 2, 128], F32)
        epJ = sb("epJ", [128, 128], F32)
        ps = es.enter_context(nc.psum_tensor("ps", [128, 8, 512], F32))

        sch = Sched(nc, es)
        add = sch.add

        def aview(off, dt, n):
            a = arena[:, off // 2:off // 2 + (n * (4 if dt == F32 else 2)) // 2]
            return a.bitcast(F32) if dt == F32 else a

        def ablk(lo, hi):
            return [("A", i) for i in range(lo // 1024, (hi - 1) // 1024 + 1)]

        hT = arena[:, 0:NFC * 512].rearrange("p (c n) -> p c n", n=512)
        yT = arena[:, 0:8 * 512].rearrange("p (c n) -> p c n", n=512)
        ZOFF, CHOFF, DG1OFF = 8192, 12544, 20768
        zbuf = aview(ZOFF, BF16, 4 * 542).rearrange("p (c n) -> p c n", n=542)
        chbuf = aview(CHOFF, F32, 4 * 514).rearrange("p (c n) -> p c n", n=514)
        Dg = [aview(o, BF16, 31 * 128).rearrange("p (k n) -> p k n", n=128) for o in (CHOFF, DG1OFF)]
        R_dg = [ablk(o, o + 31 * 128 * 2) for o in (CHOFF, DG1OFF)]
        QOFF, EOFF, YCOFF = 8192, 16384, 20480
        qz = [aview(QOFF + m * 4096, BF16, 4 * 512).rearrange("p (c n) -> p c n", n=512) for m in range(2)]
        E2 = [aview(EOFF + i * 2048, BF16, 2 * 512).rearrange("p (c n) -> p c n", n=512) for i in range(2)]
        yc = aview(YCOFF, BF16, 4 * 512).rearrange("p (c n) -> p c n", n=512)
        R_hT = lambda fc: [("A", fc)]
        R_yT = lambda c: [("A", c)]
        R_z = lambda c: ablk(ZOFF + c * 542 * 2, ZOFF + (c + 1) * 542 * 2)
        R_ch = lambda c: ablk(CHOFF + c * 514 * 4, CHOFF + (c + 1) * 514 * 4)
        R_q = lambda h: ablk(QOFF + h * 1024, QOFF + (h + 1) * 1024) + ablk(QOFF + 4096 + h * 1024, QOFF + 4096 + (h + 1) * 1024)
        R_qall = ablk(QOFF, QOFF + 8192)
        R_E2 = lambda i: ablk(EOFF + i * 2048, EOFF + (i + 1) * 2048)
        R_yc = lambda q: ablk(YCOFF + q * 1024, YCOFF + (q + 1) * 1024)

        def bcast_mid(ap2d, n):
            pr = [list(p) for p in ap2d.ap]
            return bass.AP(ap2d.tensor, ap2d.offset, [pr[0], [0, n]] + pr[1:])

        def bcast_last(ap2d, n):
            pr = [list(p) for p in ap2d.ap]
            return bass.AP(ap2d.tensor, ap2d.offset, pr + [[0, n]])

        cv = lambda col: cvec[:, col:col + 1]

        def I(eng, meth, *args, reads=(), writes=(), dma=None, final=False, **kw):
            return add(eng, lambda e: getattr(e, meth)(*args, **kw), reads, writes, dma, final)

        XF = [("xf", k) for k in range(8)]
        F0r = [("F0", n) for n in range(4)]
        F1r = [("F1", n) for n in range(4)]
        F2r = [("F2", n) for n in range(4)]
        F0 = FF[:, 0:4, :]
        F1 = FF[:, 4:8, :]
        F2 = FF[:, 8:12, :]

        def x_src(ti):
            return x_d[:, ti * TT:(ti + 1) * TT].rearrange("(k p) n -> p k n", p=128)

        def load_xb(ti):
            I("pool", "dma_start", out=xb[:], in_=x_src(ti), writes=[("xb", k) for k in range(8)], dma="xbin")

        def load_xf(ti):
            I("pool", "dma_start", out=xf[:], in_=x_src(ti), writes=XF, dma="xfin")

        load_xb(0)
        def convert_weights():
            order = [1, 2, 0, 4, 3] + list(range(5, NPL)) + [NPL + 4, NPL + 3, NPL + 0, NPL + 1, NPL + 2] + \
                    list(range(NPL + 5, NP))
            for i, p in enumerate(order):
                I("pool", "dma_start", out=wsc_d[p], in_=wall_d[p], writes=[("wsc", p)], dma="cv%d" % p)
                if i == 2:
                    load_xf(0)

        convert_weights()
        I("sp", "dma_start", out=cvec[:], in_=cvec_d, writes=["cvec"], dma="c0")
        I("sp", "dma_start", out=cbc[:], in_=cbc_d, writes=["cbc"], dma="c1")
        I("sp", "dma_start", out=wstf[:], in_=wst_d, writes=["wstf"], dma="c2")

        I("pool", "memset", identf[:], 0.0, writes=["identf"])
        I("pool", "affine_select", out=identf[:], in_=identf[:], pattern=[[-1, 128]], compare_op=ALU.not_equal,
          fill=1.0, base=0, channel_multiplier=1, reads=["identf"], writes=["identf"])
        I("pool", "tensor_copy", identb[:], identf[:], reads=["identf"], writes=["identb"])
        I("pool", "memset", trif[:], 1.0, writes=["trif"])
        I("pool", "affine_select", out=trif[:], in_=trif[:], pattern=[[1, 128]], compare_op=ALU.is_ge,
          fill=0.0, base=0, channel_multiplier=-1, reads=["trif"], writes=["trif"])
        I("pool", "tensor_copy", trib[:], trif[:], reads=["trif"], writes=["trib"])
        I("pool", "memset", ones1024[:], 1.0 / 1024.0, writes=["ones1024"])
        I("pool", "memset", ones512[:], 1.0 / 512.0, writes=["ones512"])
        I("pool", "memset", zhalo[:], 0.0, writes=["zhalo"])
        I("pool", "memset", chhalo[:], 0.0, writes=["chhalo"])
        I("pool", "memset", Vx[:, :, :, 128:130], 1.0, writes=["Vx_all"])
        I("pool", "tensor_tensor", out=wstf[:], in0=wstf[:], in1=bcast_mid(trif[:], 4), op=ALU.mult,
          reads=["wstf", "trif"], writes=["wstf"])
        I("pool", "tensor_copy", WcT[:], wstf[:], reads=["wstf"], writes=["WcT"])
        I("dve", "tensor_scalar", out=gsc[:], in0=cbc[:, CB_SG:CB_SG + 128],
          scalar1=(1.0 - LAMBDA_INIT) * math.sqrt(128.0), scalar2=None, op0=ALU.mult, reads=["cbc"], writes=["gsc"])
        I("dve", "tensor_tensor", out=lamt[:], in0=cbc[:, CB_LQ1:CB_LQ1 + 64], in1=cbc[:, CB_LK1:CB_LK1 + 64],
          op=ALU.mult, reads=["cbc"], writes=["lamt"])
        I("dve", "tensor_reduce", out=lams[:, 0:1], in_=lamt[:], axis=AX.X, op=ALU.add, reads=["lamt"], writes=["lams"])
        I("dve", "tensor_tensor", out=lamt[:], in0=cbc[:, CB_LQ2:CB_LQ2 + 64], in1=cbc[:, CB_LK2:CB_LK2 + 64],
          op=ALU.mult, reads=["cbc", "lams"], writes=["lamt"])
        I("dve", "tensor_reduce", out=lams[:, 1:2], in_=lamt[:], axis=AX.X, op=ALU.add, reads=["lamt"], writes=["lams"])
        I("act", "activation", lams[:, 2:4], lams[:, 0:2], AF.Exp, reads=["lams"], writes=["lams"])
        I("dve", "tensor_tensor", out=lams[:, 4:5], in0=lams[:, 3:4], in1=lams[:, 2:3], op=ALU.subtract,
          reads=["lams"], writes=["lams"])
        I("dve", "tensor_scalar", out=lams[:, 4:5], in0=lams[:, 4:5], scalar1=-LAMBDA_INIT, scalar2=None,
          op0=ALU.add, reads=["lams"], writes=["lams"])
        neglam = lams[:, 4:5]

        state = {"slot": 0, "bg": 0, "rb": 0}

        def load_piece(pidx):
            slot = state["slot"]
            state["slot"] = (slot + 1) % NSLOT
            I("sp", "dma_start", out=wring[:, slot, :], in_=wsc_d[pidx],
              reads=[("wsc", pidx)], writes=[("w", slot)], dma="w%d" % slot)
            return slot

        def bank_group():
            g = state["bg"]
            state["bg"] = 1 - g
            return [4 * g + i for i in range(4)]

        def one_bank():
            b = state["rb"]
            state["rb"] = (b + 1) % 8
            return b

        def wv(slot, pat, **kw):
            return wring[:, slot, :].rearrange(pat, **kw)

        def mm_feature(slot, banks, rhs_of, rhs_res, nk=8, kouter=False):
            w = wv(slot, "p (k n) -> p k n", n=512)
            order = [(k, n) for k in range(nk) for n in range(4)] if kouter else \
                    [(k, n) for n in range(4) for k in range(nk)]
            for k, n in order:
                if True:
                    I("pe", "matmul", ps[:, banks[n], :], w[:, k, n * 128:(n + 1) * 128], rhs_of(k),
                      start=(k == 0), stop=(k == nk - 1),
                      reads=[("w", slot)] + rhs_res(k), writes=[("ps", banks[n])])

        def mm_token(slot, banks, kouter=False):
            w = wv(slot, "p (k n) -> p k n", n=512)
            order = [(k, tt) for k in range(8) for tt in range(4)] if kouter else \
                    [(k, tt) for tt in range(4) for k in range(8)]
            for k, tt in order:
                if True:
                    I("pe", "matmul", ps[:, banks[tt], :], xb[:, k, tt * 128:(tt + 1) * 128], w[:, k, :],
                      start=(k == 0), stop=(k == 7), reads=[("w", slot), ("xb", k)], writes=[("ps", banks[tt])])

        R_xb = lambda k: [("xb", k)]
        xb_of = lambda k: xb[:, k, :]

        def stats_tail(bm, be, after_var=None):
            I("act", "activation", RSTD[:], ps[:, bm, :], AF.Square, reads=[("ps", bm)], writes=["RSTD"])
            I("act", "activation", MEAN[:], ps[:, bm, :], AF.Copy, reads=[("ps", bm)], writes=["MEAN"])
            I("dve", "scalar_tensor_tensor", out=RSTD[:], in0=ps[:, be, :], scalar=LN_EPS, in1=RSTD[:],
              op0=ALU.add, op1=ALU.subtract, reads=[("ps", be), "RSTD"], writes=["RSTD"])
            I("act", "activation", RSTD[:], RSTD[:], AF.Ln, reads=["RSTD"], writes=["RSTD"])
            I("act", "activation", RSTD[:], RSTD[:], AF.Exp, scale=-0.5, reads=["RSTD"], writes=["RSTD"])
            if after_var is not None:
                after_var()

        R_FF = lambda c: [("F0", c)] if c < 4 else [("F1", c - 4)]

        def layer_norm(gcol, bcol, write_xb=True):
            bm, be = one_bank(), one_bank()
            I("pe", "matmul", ps[:, bm, :], ones1024[:], S1[:], start=True, stop=True,
              reads=["S1", "ones1024"], writes=[("ps", bm)])
            I("pe", "matmul", ps[:, be, :], ones1024[:], S2[:], start=True, stop=True,
              reads=["S2", "ones1024"], writes=[("ps", be)])
            def sub(k):
                I("dve", "tensor_tensor", out=xf[:, k, :], in0=xf[:, k, :], in1=MEAN[:], op=ALU.subtract,
                  reads=[("xf", k), "MEAN"], writes=[("xf", k)])

            def center():
                for k in range(4):
                    sub(k)
            stats_tail(bm, be, center)
            for k in range(8):
                I("dve", "tensor_tensor", out=xf[:, k, :], in0=xf[:, k, :], in1=RSTD[:], op=ALU.mult,
                  reads=[("xf", k), "RSTD"], writes=[("xf", k)])
                if k + 4 < 8:
                    sub(k + 4)
                if write_xb:
                    I("act", "activation", xb[:, k, :], xf[:, k, :], AF.Identity, bias=cv(bcol + k),
                      scale=cv(gcol + k), reads=[("xf", k), "cvec"], writes=[("xb", k)])
            for k in range(8):
                I("act", "activation", xf[:, k, :], xf[:, k, :], AF.Identity, bias=cv(bcol + k), scale=cv(gcol + k),
                  reads=[("xf", k), "cvec"], writes=[("xf", k)])

        def residual_evac(banks, c0):
            for n in range(4):
                c = c0 + n
                I("dve", "scalar_tensor_tensor", out=xf[:, c, :], in0=xf[:, c, :], scalar=ALPHA,
                  in1=ps[:, banks[n], :], op0=ALU.mult, op1=ALU.add,
                  reads=[("xf", c), ("ps", banks[n])], writes=[("xf", c)])
                I("act", "activation", FF[:, c, :], xf[:, c, :], AF.Square, reads=[("xf", c)], writes=R_FF(c))
                if c == 6:
                    I("act", "activation", lnd[:], ones1024[:, 0:1], AF.Ln, reads=["ones1024"], writes=["lnd"])
                if c == 1:
                    I("dve", "tensor_tensor", out=S1[:], in0=xf[:, 0, :], in1=xf[:, 1, :], op=ALU.add,
                      reads=[("xf", 0), ("xf", 1)], writes=["S1"])
                    I("pool", "tensor_tensor", out=S2[:], in0=FF[:, 0, :], in1=FF[:, 1, :], op=ALU.add,
                      reads=R_FF(0) + R_FF(1), writes=["S2"])
                elif c > 1:
                    I("dve", "tensor_tensor", out=S1[:], in0=S1[:], in1=xf[:, c, :], op=ALU.add,
                      reads=[("xf", c), "S1"], writes=["S1"])
                    I("dve" if c == 7 else "pool", "tensor_tensor", out=S2[:], in0=S2[:], in1=FF[:, c, :], op=ALU.add,
                      reads=R_FF(c) + ["S2"], writes=["S2"])

        def out_proj(pbase):
            for j in range(2):
                slot = load_piece(pbase + 5 + j)
                banks = bank_group()
                mm_feature(slot, banks, lambda k: yT[:, k, :], R_yT)
                residual_evac(banks, 4 * j)

        def ffn(pbase, after_gateup=None):
            for j in range(11):
                slot = load_piece(pbase + 7 + j)
                banks = bank_group()
                w = wv(slot, "p (k a n) -> p k a n", a=2, n=256)
                order = [(k, a, hf) for k in range(8) for a in range(2) for hf in range(2)] if j == 0 else \
                        [(k, a, hf) for hf in range(2) for a in range(2) for k in range(8)]
                for k, a, hf in order:
                    if True:
                        if True:
                            b = banks[2 * a + hf]
                            I("pe", "matmul", ps[:, b, :], w[:, k, a, hf * 128:(hf + 1) * 128], xb[:, k, :],
                              start=(k == 0), stop=(k == 7), reads=[("w", slot), ("xb", k)], writes=[("ps", b)])
                for hf in range(2):
                    fc = 2 * j + hf
                    sgi = fc % 3
                    I("act", "activation", SG[:, sgi, :], ps[:, banks[hf], :], AF.Silu,
                      reads=[("ps", banks[hf])], writes=[("SG", sgi)])
                    I("dve", "tensor_tensor", out=hT[:, fc, :], in0=SG[:, sgi, :], in1=ps[:, banks[2 + hf], :],
                      op=ALU.mult, reads=[("SG", sgi), ("ps", banks[2 + hf])], writes=R_hT(fc))
            if after_gateup is not None:
                after_gateup()
            for dh in range(2):
                banks = bank_group()
                for g, (lo, hi) in enumerate(DOWN_GROUPS):
                    slot = load_piece(pbase + 18 + dh * 3 + g)
                    w = wv(slot, "p (k n) -> p k n", n=512)
                    order = [(i, fc, n) for i, fc in enumerate(range(lo, hi)) for n in range(4)] if g < 2 else \
                            [(i, fc, n) for n in range(4) for i, fc in enumerate(range(lo, hi))]
                    for i, fc, n in order:
                        if True:
                            I("pe", "matmul", ps[:, banks[n], :], w[:, i, n * 128:(n + 1) * 128], hT[:, fc, :],
                              start=(fc == 0), stop=(fc == NFC - 1),
                              reads=[("w", slot)] + R_hT(fc), writes=[("ps", banks[n])])
                residual_evac(banks, 4 * dh)

        def build_dg(n):
            wtap = cvec[:, CV_CB + n:CV_CB + n + 1]
            pr = [list(p) for p in wtap.ap]
            w_bc = bass.AP(wtap.tensor, wtap.offset, [pr[0], [4, 31], [0, 128]])
            I("dve", "tensor_tensor", out=Dg[(n + 1) % 2], in0=bcast_mid(identb[:], 31), in1=w_bc, op=ALU.mult,
              reads=["identb", "cvec"], writes=R_dg[(n + 1) % 2])

        def mixer0(ti, pbase):
            build_dg(0)
            for n in range(4):
                I("pool", "tensor_copy", zbuf[:, n, 0:30], zhalo[:, n, :], reads=["zhalo"], writes=R_z(n))
            slot = load_piece(pbase + 1)
            banks = bank_group()
            mm_feature(slot, banks, xb_of, R_xb, kouter=True)
            for n in range(4):
                I("act", "activation", F0[:, n, :], ps[:, banks[n], :], AF.Copy,
                  reads=[("ps", banks[n])], writes=[("F0", n)])
            slot = load_piece(pbase + 2)
            banks = bank_group()
            mm_feature(slot, banks, xb_of, R_xb)
            for n in range(4):
                I("pool", "tensor_copy", chbuf[:, n, 0:2], chhalo[:, n, :], reads=["chhalo"], writes=R_ch(n))
                I("dve", "tensor_tensor", out=chbuf[:, n, 2:514], in0=F0[:, n, :], in1=ps[:, banks[n], :], op=ALU.mult,
                  reads=[("F0", n), ("ps", banks[n])], writes=R_ch(n))
            slot = load_piece(pbase + 0)
            banks = bank_group()
            mm_feature(slot, banks, xb_of, R_xb)
            for n in range(4):
                I("act", "activation", F1[:, n, :], ps[:, banks[n], :], AF.Copy,
                  reads=[("ps", banks[n])], writes=[("F1", n)])
            for n in range(4):
                I("dve", "tensor_scalar", out=F0[:, n, :], in0=chbuf[:, n, 2:514], scalar1=cv(CV_CA + 8 + n),
                  scalar2=None, op0=ALU.mult, reads=R_ch(n) + ["cvec"], writes=[("F0", n)])
                for k in (1, 0):
                    I("dve", "scalar_tensor_tensor", out=F0[:, n, :], in0=chbuf[:, n, k:k + 512],
                      scalar=cv(CV_CA + 4 * k + n), in1=F0[:, n, :], op0=ALU.mult, op1=ALU.add,
                      reads=R_ch(n) + [("F0", n)], writes=[("F0", n)])
                I("pool", "tensor_copy", chhalo[:, n, :], chbuf[:, n, 512:514], reads=R_ch(n), writes=["chhalo"])
                I("dve", "tensor_tensor", out=yT[:, n, :], in0=F0[:, n, :], in1=F1[:, n, :], op=ALU.mult,
                  reads=[("F0", n), ("F1", n)], writes=R_yT(n))
            build_dg(1)
            slot = load_piece(pbase + 4)
            banks = bank_group()
            mm_feature(slot, banks, xb_of, R_xb)
            for n in range(4):
                I("act", "activation", F0[:, n, :], ps[:, banks[n], :], AF.Sigmoid,
                  reads=[("ps", banks[n])], writes=[("F0", n)])
            slot = load_piece(pbase + 3)
            banks = bank_group()
            mm_feature(slot, banks, xb_of, R_xb)
            for n in range(4):
                I("dve", "tensor_tensor", out=zbuf[:, n, 30:542], in0=F0[:, n, :], in1=ps[:, banks[n], :], op=ALU.mult,
                  reads=[("F0", n), ("ps", banks[n])], writes=R_z(n))
            banks = bank_group()
            for n in range(4):
                d = (n + 1) % 2
                for k in range(31):
                    I("pe", "matmul", ps[:, banks[n], :], Dg[d][:, k, :], zbuf[:, n, k:k + 512],
                      start=(k == 0), stop=(k == 30), reads=R_dg[d] + R_z(n), writes=[("ps", banks[n])])
                if n + 2 < 4:
                    build_dg(n + 2)
                I("pool", "tensor_copy", zhalo[:, n, :], zbuf[:, n, 512:542], reads=R_z(n), writes=["zhalo"])
                I("act", "activation", F1[:, n, :], ps[:, banks[n], :], AF.Identity, bias=cv(CV_BB + n),
                  reads=[("ps", banks[n]), "cvec"], writes=[("F1", n)])
                I("act", "activation", F2[:, n, :], ps[:, banks[n], :], AF.Square, bias=cv(CV_BB + n),
                  reads=[("ps", banks[n]), "cvec"], writes=[("F2", n)])
                if n == 2:
                    I("act", "activation", lnd[:], ones1024[:, 0:1], AF.Ln, reads=["ones1024"], writes=["lnd"])
                if n == 1:
                    I("dve", "tensor_tensor", out=S1[:], in0=F1[:, 0, :], in1=F1[:, 1, :], op=ALU.add,
                      reads=[("F1", 0), ("F1", 1)], writes=["S1"])
                    I("pool", "tensor_tensor", out=S2[:], in0=F2[:, 0, :], in1=F2[:, 1, :], op=ALU.add,
                      reads=[("F2", 0), ("F2", 1)], writes=["S2"])
                elif n > 1:
                    I("dve", "tensor_tensor", out=S1[:], in0=S1[:], in1=F1[:, n, :], op=ALU.add,
                      reads=[("F1", n), "S1"], writes=["S1"])
                    I("pool", "tensor_tensor", out=S2[:], in0=S2[:], in1=F2[:, n, :], op=ALU.add,
                      reads=[("F2", n), "S2"], writes=["S2"])
            bm, be = one_bank(), one_bank()
            I("pe", "matmul", ps[:, bm, :], ones512[:], S1[:], start=True, stop=True,
              reads=["S1", "ones512"], writes=[("ps", bm)])
            I("pe", "matmul", ps[:, be, :], ones512[:], S2[:], start=True, stop=True,
              reads=["S2", "ones512"], writes=[("ps", be)])
            def center():
                for n in range(4):
                    I("dve", "tensor_tensor", out=F1[:, n, :], in0=F1[:, n, :], in1=MEAN[:], op=ALU.subtract,
                      reads=[("F1", n), "MEAN"], writes=[("F1", n)])
            stats_tail(bm, be, center)
            for n in range(4):
                I("dve", "tensor_tensor", out=F1[:, n, :], in0=F1[:, n, :], in1=RSTD[:], op=ALU.mult,
                  reads=[("F1", n), "RSTD"], writes=[("F1", n)])
                I("act", "activation", yT[:, 4 + n, :], F1[:, n, :], AF.Silu, bias=cv(CV_LB + n), scale=cv(CV_LG + n),
                  reads=[("F1", n), "cvec"], writes=R_yT(4 + n))

        VGOFF = YCOFF + 4096
        vgln = aview(VGOFF, BF16, 4 * 512).rearrange("p (c n) -> p c n", n=512)
        R_vg = ablk(VGOFF, VGOFF + 4096)
        psb = ps[:].bitcast(BF16)

        def mixer1(ti, pbase):
            slot = load_piece(pbase + 4)
            banks = bank_group()
            mm_token(slot, banks, kouter=True)
            for tt in range(4):
                I("act", "activation", F1[:, tt, :], ps[:, banks[tt], :], AF.Gelu_apprx_tanh,
                  reads=[("ps", banks[tt])], writes=[("F1", tt)])
            F1g = F1.rearrange("p t (g d) -> p (t g) d", d=128)
            F2g = F2.rearrange("p t (g d) -> p (t g) d", d=128)
            I("dve", "tensor_reduce", out=st16[:, 0, :], in_=F1g, axis=AX.X, op=ALU.add, reads=F1r, writes=["st16"])
            I("act", "activation", F2, F1, AF.Square, reads=F1r, writes=F2r)
            I("dve", "tensor_reduce", out=st16[:, 1, :], in_=F2g, axis=AX.X, op=ALU.add,
              reads=F2r + ["st16"], writes=["st16"])
            I("dve", "tensor_scalar", out=st16[:, 2, :], in0=st16[:, 0, :], scalar1=1.0 / 128.0, scalar2=None,
              op0=ALU.mult, reads=["st16"], writes=["st16"])
            I("dve", "tensor_tensor", out=st16[:, 3, :], in0=st16[:, 2, :], in1=st16[:, 2, :], op=ALU.mult,
              reads=["st16"], writes=["st16"])
            I("dve", "scalar_tensor_tensor", out=st16[:, 4, :], in0=st16[:, 1, :], scalar=1.0 / 128.0,
              in1=st16[:, 3, :], op0=ALU.mult, op1=ALU.subtract, reads=["st16"], writes=["st16"])
            I("dve", "tensor_scalar", out=st16[:, 5, :], in0=st16[:, 4, :], scalar1=LN_EPS, scalar2=None,
              op0=ALU.add, reads=["st16"], writes=["st16"])
            slot = load_piece(pbase + 3)
            banks = bank_group()
            mm_feature(slot, banks, xb_of, R_xb)
            for n in range(4):
                I("act", "activation", F0[:, n, :], ps[:, banks[n], :], AF.Gelu_apprx_tanh,
                  reads=[("ps", banks[n])], writes=[("F0", n)])
            slot = load_piece(pbase + 0)
            banks = bank_group()
            mm_feature(slot, banks, xb_of, R_xb)
            I("pool", "memset", qz[0][64:128, :, :], 0.0, writes=R_qall)
            I("pool", "memset", qz[1][0:64, :, :], 0.0, writes=R_qall)
            for n in range(4):
                I("act", "activation", qz[0][0:64, n, :], ps[0:64, banks[n], :], AF.Copy,
                  reads=[("ps", banks[n])], writes=R_q(n))
                I("act", "activation", qz[1][64:128, n, :], ps[64:128, banks[n], :], AF.Copy,
                  reads=[("ps", banks[n])], writes=R_q(n))
            I("act", "activation", st16[:, 5, :], st16[:, 5, :], AF.Ln, reads=["st16"], writes=["st16"])
            I("act", "activation", st16[:, 5, :], st16[:, 5, :], AF.Exp, scale=-0.5, reads=["st16"], writes=["st16"])
            I("dve", "tensor_tensor", out=F1g, in0=F1g, in1=bcast_last(st16[:, 2, :], 128), op=ALU.subtract,
              reads=F1r + ["st16"], writes=F1r)
            I("dve", "tensor_tensor", out=F1g, in0=F1g, in1=bcast_last(st16[:, 5, :], 128), op=ALU.mult,
              reads=F1r + ["st16"], writes=F1r)
            I("pool", "tensor_tensor", out=F1, in0=F1, in1=bcast_mid(cbc[:, CB_GG:CB_GG + 512], 4), op=ALU.mult,
              reads=F1r + ["cbc"], writes=F1r)
            I("pool", "tensor_tensor", out=vgln, in0=F1, in1=bcast_mid(cbc[:, CB_GB:CB_GB + 512], 4), op=ALU.add,
              reads=F1r + ["cbc"], writes=R_vg)
            slot = load_piece(pbase + 1)
            banks = bank_group()
            mm_feature(slot, banks, xb_of, R_xb)
            for n in range(4):
                I("dve", "tensor_copy", KT[:, n, ti * TT:(ti + 1) * TT], ps[:, banks[n], :],
                  reads=[("ps", banks[n])], writes=[("KT", ti)])
            slot = load_piece(pbase + 2)
            banks = bank_group()
            mm_token(slot, banks)
            for tt in range(4):
                I("act", "activation", Vx[:, 4 * ti + tt, :, 0:128],
                  ps[:, banks[tt], :].rearrange("p (h e) -> p h e", e=128), AF.Copy,
                  reads=[("ps", banks[tt]), "Vx_all"], writes=[("Vx", 4 * ti + tt)])
            nkt = 4 * ti + 4
            spairs = ((3, 4), (5, 6))
            accb = (0, 1, 2)
            acc_sb = (F1, SG[:])
            acc_sb_res = (F1r, [("SG", i) for i in range(3)])

            def acc_ap(qt, m):
                a = qt * 2 + m
                return ps[:, accb[a // 3], (a % 3) * 129:(a % 3) * 129 + 129], ("ps", accb[a // 3])

            def acc_copy_ap(h, qt, m):
                a = qt * 2 + m
                return acc_sb[h % 2][:, a // 3, (a % 3) * 129:(a % 3) * 129 + 129]

            def rec_pv(blk, pi2, bi):
                h, m, kt = blk
                q0 = max(0, kt - 4 * ti)
                for qt in range(q0, 4):
                    dst, dres = acc_ap(qt, m)
                    I("pe", "matmul", dst, E2[pi2][:, bi, (qt - q0) * 128:(qt - q0 + 1) * 128], Vx[:, kt, h, 0:129],
                      start=(kt == 0 and m == 0 and qt in (0, 2, 3)), stop=(kt == 4 * ti + qt),
                      skip_group_check=True, reads=R_E2(pi2) + [("Vx", kt), "Vx_all"], writes=[dres])

            def epilogue(h):
                sbuf_res = acc_sb_res[h % 2]
                for bi_, ncols in ((0, 387), (1, 387), (2, 258)):
                    I("dve", "tensor_copy", acc_sb[h % 2][:, bi_, 0:ncols], ps[:, accb[bi_], 0:ncols],
                      reads=[("ps", accb[bi_])], writes=sbuf_res)
                for qt in range(4):
                    a0 = acc_copy_ap(h, qt, 0)
                    a1 = acc_copy_ap(h, qt, 1)
                    pi = qt % 2
                    EP = [("ep", pi)]
                    eo = F2[:, h, qt * 128:(qt + 1) * 128]
                    I("dve", "reciprocal", ep[:, pi, 0:1], a0[:, 128:129], reads=sbuf_res, writes=EP)
                    I("dve", "reciprocal", ep[:, pi, 1:2], a1[:, 128:129], reads=sbuf_res + EP, writes=EP)
                    I("dve", "tensor_tensor", out=ep[:, pi, 2:3], in0=ep[:, pi, 1:2], in1=neglam, op=ALU.mult,
                      reads=EP + ["lams"], writes=EP)
                    I("dve", "tensor_scalar", out=epT[:, pi, :], in0=a1[:, 0:128], scalar1=ep[:, pi, 2:3],
                      scalar2=None, op0=ALU.mult, reads=sbuf_res + EP, writes=[("epT", pi)])
                    I("dve", "scalar_tensor_tensor", out=eo, in0=a0[:, 0:128], scalar=ep[:, pi, 0:1],
                      in1=epT[:, pi, :], op0=ALU.mult, op1=ALU.add,
                      reads=sbuf_res + [("epT", pi)] + EP, writes=[("F2", h)])
                    I("dve", "scalar_tensor_tensor", out=epJ[:], in0=eo, scalar=1.0, in1=eo, op0=ALU.mult, op1=ALU.mult,
                      accum_out=ss16[:, 4 * h + qt:4 * h + qt + 1], reads=[("F2", h)], writes=["epJ", "ss16"])

            blocks = [(h, m, kt) for h in range(4) for m in range(2) for kt in range(nkt)]
            npair = len(blocks) // 2

            def rec_s(p):
                for bi in range(2):
                    h, m, kt = blocks[2 * p + bi]
                    c0 = max(0, kt - 4 * ti) * 128
                    bk = spairs[p % 2][bi]
                    I("pe", "matmul", ps[:, bk, 0:512 - c0], KT[:, h, kt * 128:(kt + 1) * 128],
                      qz[m][:, h, c0:512], start=True, stop=True,
                      reads=[("KT", kt // 4)] + R_q(h), writes=[("ps", bk)])

            rec_s(0)
            rec_s(1)
            for p in range(npair):
                b0, b1 = spairs[p % 2]
                pi2 = p % 2
                h, m, kt = blocks[2 * p]
                if kt + 1 < 4 * ti:
                    I("act", "activation", E2[pi2], ps[:, b0:b0 + 2, :], AF.Exp, scale=0.125,
                      reads=[("ps", b0), ("ps", b1)], writes=R_E2(pi2))
                else:
                    for bi in range(2):
                        j = kt + bi - 4 * ti
                        ncol = 512 - j * 128
                        I("act", "activation", E2[pi2][:, bi, 0:ncol], ps[:, spairs[p % 2][bi], 0:ncol], AF.Exp,
                          scale=0.125, reads=[("ps", spairs[p % 2][bi])], writes=R_E2(pi2))
                        I("pool", "tensor_tensor", out=E2[pi2][:, bi, 0:128], in0=E2[pi2][:, bi, 0:128], in1=trib[:],
                          op=ALU.mult, reads=R_E2(pi2) + ["trib"], writes=R_E2(pi2))
                if p + 2 < npair:
                    rec_s(p + 2)
                for bi in range(2):
                    rec_pv(blocks[2 * p + bi], pi2, bi)
                if m == 1 and kt + 1 == nkt - 1:
                    epilogue(h)
            I("dve", "tensor_scalar", out=ss16[:, 16:32], in0=ss16[:, 0:16], scalar1=128.0 * LN_EPS, scalar2=None,
              op0=ALU.add, reads=["ss16"], writes=["ss16b"])
            I("act", "activation", ss16[:, 16:32], ss16[:, 16:32], AF.Ln, reads=["ss16b"], writes=["ss16b"])
            I("act", "activation", ss16[:, 16:32], ss16[:, 16:32], AF.Exp, scale=-0.5, reads=["ss16b"], writes=["ss16b"])
            for h in range(4):
                for qt in range(4):
                    I("dve", "scalar_tensor_tensor", out=yc[:, qt, h * 128:(h + 1) * 128],
                      in0=F2[:, h, qt * 128:(qt + 1) * 128], scalar=ss16[:, 16 + 4 * h + qt:17 + 4 * h + qt],
                      in1=gsc[:], op0=ALU.mult, op1=ALU.mult, reads=[("F2", h), "gsc", "ss16b"], writes=R_yc(qt))
            banks = bank_group()
            for g in range(4):
                for tt in range(4):
                    I("pe", "matmul", ps[:, banks[g], tt * 128:(tt + 1) * 128], vgln[:, tt, g * 128:(g + 1) * 128],
                      WcT[:, g, :], start=True, stop=True, reads=R_vg + ["WcT"], writes=[("ps", banks[g])])
            for g in range(4):
                I("dve", "tensor_tensor", out=F2[:, g, :].rearrange("p (t s) -> p t s", s=128),
                  in0=ps[:, banks[g], :].rearrange("p (t s) -> p t s", s=128),
                  in1=bcast_mid(cbc[:, CB_SB + 128 * g:CB_SB + 128 * g + 128], 4), op=ALU.add,
                  reads=[("ps", banks[g]), "cbc"], writes=[("F2", g)])
                I("pool", "tensor_tensor", out=yT[:, 4 + g, :], in0=F2[:, g, :], in1=F0[:, g, :], op=ALU.mult,
                  reads=[("F2", g), ("F0", g)], writes=R_yT(4 + g))
            for h in range(4):
                b = (6, 7)[h % 2]
                for qt in range(4):
                    I("pe", "transpose", psb[:, b, qt * 128:(qt + 1) * 128], yc[:, qt, h * 128:(h + 1) * 128],
                      identb[:], reads=R_yc(qt) + ["identb"], writes=[("ps", b)])
                I("act", "activation", yT[:, h, :], psb[:, b, 0:512], AF.Copy, reads=[("ps", b)], writes=R_yT(h))


        def stage(name):
            sch.tag = "after_" + name
            if stop_at == name:
                raise _Stop()

        def main_loop():
            for ti in range(n_tiles):
                stage("x")
                for L in range(2):
                    pbase = L * NPL
                    if L == 0:
                        mixer0(ti, pbase)
                        if ti > 0:
                            load_xf(ti)
                    else:
                        mixer1(ti, pbase)
                    stage("mix%d" % L)
                    if dbg and ("y%d" % L) in dbg_d and ti == 0:
                        for c in range(8):
                            I("pool", "tensor_copy", FF[:, 8 + c % 4, :], yT[:, c, :], reads=R_yT(c), writes=[("F2", c % 4)])
                            I("pool", "dma_start", out=dbg_d["y%d" % L][c], in_=FF[:, 8 + c % 4, :],
                              reads=[("F2", c % 4)], dma="dbg%s%d" % ("y%d" % L, c), final=True)
                    out_proj(pbase)
                    stage("out%d" % L)
                    layer_norm(CV_MG + 8 * L, CV_MB + 8 * L)
                    stage("ln%d" % L)
                    if dbg and ("m%d" % L) in dbg_d and ti == 0:
                        I("pool", "dma_start", out=dbg_d["m%d" % L].rearrange("k p n -> p k n"), in_=xf[:],
                          reads=XF, dma="dbgm%d" % L, final=True)
                    ffn(pbase, (lambda: load_xb(ti + 1)) if (L == 1 and ti + 1 < n_tiles) else None)
                    stage("ffn%d" % L)
                    layer_norm(CV_FG + 8 * L, CV_FB + 8 * L, write_xb=(L == 0))
                    stage("fln%d" % L)
                    if dbg and ("f%d" % L) in dbg_d and ti == 0:
                        I("pool", "dma_start", out=dbg_d["f%d" % L].rearrange("k p n -> p k n"), in_=xf[:],
                          reads=XF, dma="dbgf%d" % L, final=True)
                I("act", "dma_start", out=out_d[:, ti * TT:(ti + 1) * TT].rearrange("(k p) n -> p k n", p=128), in_=xf[:],
                  reads=XF, dma="xout", final=True)

        try:
            main_loop()
        except _Stop:
            pass

        sch.emit()
    return nc


_NC_CACHE = {}


def kernel(**inputs):
    x = np.asarray(inputs["x"], np.float32)
    shared = make_shared(inputs)
    if "nc" not in _NC_CACHE:
        _NC_CACHE["nc"] = build_program()
    nc = _NC_CACHE["nc"]
    in_maps = [dict(shared, x=np.ascontiguousarray(x[b].T)) for b in range(8)]
    res = run_bass_kernel_spmd(nc, in_maps, core_ids=list(range(8)))
    return np.stack([np.ascontiguousarray(np.asarray(r["out"], np.float32).T) for r in res.results], axis=0)
```

```python
import math
from contextlib import ExitStack
import numpy as np
import concourse.bass as bass
import concourse.mybir as mybir
from concourse.bass_utils import run_bass_kernel_spmd

F32 = mybir.dt.float32
BF16 = mybir.dt.bfloat16
AF = mybir.ActivationFunctionType
ALU = mybir.AluOpType
AX = mybir.AxisListType

S, D, TT, NT = 4096, 1024, 512, 8
KC = 8
NFC = 22
ALPHA = (2 * 2) ** 0.25
LN_EPS = 1e-5
LAMBDA_INIT = 0.8 - 0.6 * math.exp(-0.3 * 1)
NPL = 24
NP = 2 * NPL
PW = 4096
NSLOT = 4
DOWN_GROUPS = [(0, 8), (8, 16), (16, 22)]
ARENA_B = 28704

CV_CA = 0
CV_CB = 12
CV_BB = 136
CV_LG = 140
CV_LB = 144
CV_MG = 148
CV_MB = 164
CV_FG = 180
CV_FB = 196
NCV = 212
CB_GG = 0
CB_GB = 512
CB_SB = 1024
CB_SG = 1536
CB_LQ1 = 1664
CB_LK1 = 1728
CB_LQ2 = 1792
CB_LK2 = 1856
NCB = 1920


def _piece_in(w, j):
    return np.ascontiguousarray(w.reshape(8, 128, -1)[:, :, j * 512:(j + 1) * 512].transpose(1, 0, 2)).reshape(128, PW)


def _piece_gu(g, u, j):
    a = np.empty((128, 8, 2, 256), np.float32)
    a[:, :, 0, :] = g.reshape(8, 128, -1)[:, :, j * 256:(j + 1) * 256].transpose(1, 0, 2)
    a[:, :, 1, :] = u.reshape(8, 128, -1)[:, :, j * 256:(j + 1) * 256].transpose(1, 0, 2)
    return a.reshape(128, PW)


def _piece_down(wd, dh, grp):
    lo, hi = DOWN_GROUPS[grp]
    a = np.zeros((128, 8, 512), np.float32)
    a[:, :hi - lo, :] = wd.reshape(NFC, 128, 1024)[lo:hi, :, dh * 512:(dh + 1) * 512].transpose(1, 0, 2)
    return a.reshape(128, PW)


def _chunked(v):
    return np.ascontiguousarray(np.asarray(v, np.float32).reshape(-1, 128).T)


def make_shared(inp):
    f = lambda k: np.asarray(inp[k], np.float32)
    wall = np.empty((NP, 128, PW), np.float32)
    for L in range(2):
        w_in = f("even_w_in")[0] if L == 0 else f("odd_w_in")[0]
        w_out = f("even_w_out")[0] if L == 0 else f("odd_w_out")[0]
        b = L * NPL
        for j in range(5):
            wall[b + j] = _piece_in(w_in, j)
        for j in range(2):
            wall[b + 5 + j] = _piece_in(w_out, j)
        for j in range(11):
            wall[b + 7 + j] = _piece_gu(f("ffn_w_gate")[L], f("ffn_w_up")[L], j)
        for dh in range(2):
            for g in range(3):
                wall[b + 18 + dh * 3 + g] = _piece_down(f("ffn_w_down")[L], dh, g)
    cvec = np.zeros((128, NCV), np.float32)
    ca = f("even_conv_a_w")[0]
    cb = f("even_conv_b_w")[0]
    for k in range(3):
        cvec[:, CV_CA + 4 * k:CV_CA + 4 * k + 4] = _chunked(ca[k])
    for k in range(31):
        cvec[:, CV_CB + 4 * k:CV_CB + 4 * k + 4] = _chunked(cb[k])
    cvec[:, CV_BB:CV_BB + 4] = _chunked(f("even_conv_b_bias")[0])
    cvec[:, CV_LG:CV_LG + 4] = _chunked(f("even_conv_ln_g")[0])
    cvec[:, CV_LB:CV_LB + 4] = _chunked(f("even_conv_ln_b")[0])
    for L in range(2):
        cvec[:, CV_MG + 8 * L:CV_MG + 8 * L + 8] = _chunked(f("mix_ln_g")[L])
        cvec[:, CV_MB + 8 * L:CV_MB + 8 * L + 8] = _chunked(f("mix_ln_b")[L])
        cvec[:, CV_FG + 8 * L:CV_FG + 8 * L + 8] = _chunked(f("ffn_ln_g")[L])
        cvec[:, CV_FB + 8 * L:CV_FB + 8 * L + 8] = _chunked(f("ffn_ln_b")[L])
    row = np.zeros((NCB,), np.float32)
    row[CB_GG:CB_GG + 512] = f("odd_gmlp_ln_g")[0].reshape(-1)
    row[CB_GB:CB_GB + 512] = f("odd_gmlp_ln_b")[0].reshape(-1)
    row[CB_SB:CB_SB + 512] = f("odd_spatial_b")[0].reshape(-1)
    row[CB_SG:CB_SG + 128] = f("odd_subln_g")[0]
    row[CB_LQ1:CB_LQ1 + 64] = f("odd_lambda_q1")[0]
    row[CB_LK1:CB_LK1 + 64] = f("odd_lambda_k1")[0]
    row[CB_LQ2:CB_LQ2 + 64] = f("odd_lambda_q2")[0]
    row[CB_LK2:CB_LK2 + 64] = f("odd_lambda_k2")[0]
    cbc = np.ascontiguousarray(np.broadcast_to(row[None, :], (128, NCB)))
    wst = np.ascontiguousarray(f("odd_spatial_w")[0].transpose(2, 0, 1))
    return {"wall": wall, "cvec": cvec, "cbc": cbc, "wst": wst}


class Op:
    __slots__ = ("eng", "fn", "deps", "signal", "sigval", "dma", "tag")


class Sched:
    ENGS = ("pe", "act", "dve", "pool", "sp")

    def __init__(self, nc, es):
        self.nc, self.es = nc, es
        self.ops = {e: [] for e in self.ENGS}
        self.last_w, self.readers = {}, {}
        self.sems = {}
        self.dma_cnt = {}
        self.final = []
        self.tag = ""

    def sem(self, key):
        if key not in self.sems:
            self.sems[key] = self.es.enter_context(self.nc.semaphore("s_" + key))
        return self.sems[key]

    def add(self, eng, fn, reads=(), writes=(), dma=None, final=False):
        op = Op()
        op.eng, op.fn, op.dma, op.signal, op.sigval = eng, fn, dma, False, 0
        op.tag = self.tag
        writes = list(writes) + [r for r in reads if isinstance(r, tuple) and r[0] == "ps"]
        deps = []
        for r in reads:
            d = self.last_w.get(r)
            if d is not None:
                deps.append(d)
        for w in writes:
            d = self.last_w.get(w)
            if d is not None:
                deps.append(d)
            deps.extend(self.readers.get(w, ()))
        for r in reads:
            self.readers.setdefault(r, []).append(op)
        for w in writes:
            self.last_w[w] = op
            self.readers[w] = []
        seen, dd = set(), []
        for d in deps:
            if d is op or id(d) in seen:
                continue
            seen.add(id(d))
            dd.append(d)
            if not (d.dma is None and d.eng == eng and eng in ("pe", "sp")):
                d.signal = True
        op.deps = dd
        if dma is not None:
            self.dma_cnt[dma] = self.dma_cnt.get(dma, 0) + 1
            op.sigval = 16 * self.dma_cnt[dma]
            self.sem(dma)
            if final:
                self.final.append(op)
        self.ops[eng].append(op)
        return op

    def emit(self):
        nc = self.nc
        for e in self.ENGS:
            c = 0
            for op in self.ops[e]:
                if op.dma is None and op.signal:
                    c += 1
                    op.sigval = c
            self.sem("eng_" + e)
        block = self.es.enter_context(nc.Block())

        def tok(d):
            return ("eng_" + d.eng if d.dma is None else d.dma), d.sigval

        def run(e, eng):
            waited = {}
            for op in self.ops[e]:
                for d in op.deps:
                    if d.dma is None and d.eng == e and e in ("pe", "sp"):
                        continue
                    key, val = tok(d)
                    if waited.get(key, 0) >= val:
                        continue
                    eng.wait_ge(self.sems[key], val)
                    waited[key] = val
                ins = op.fn(eng)
                if op.dma is not None:
                    ins.then_inc(self.sems[op.dma], 16)
                elif op.signal:
                    ins.then_inc(self.sems["eng_" + e], 1)
            if e == "act":
                for op in self.final:
                    eng.wait_ge(self.sems[op.dma], op.sigval)

        block.tensor(lambda eng: run("pe", eng))
        block.scalar(lambda eng: run("act", eng))
        block.vector(lambda eng: run("dve", eng))
        block.gpsimd(lambda eng: run("pool", eng))
        block.sync(lambda eng: run("sp", eng))


class _Stop(Exception):
    pass


def build_program(n_tiles=NT, dbg=None, stop_at=None):
    nc = bass.Bass("TRN2", target_bir_lowering=False)
    x_d = nc.dram_tensor("x", [D, S], F32, kind="ExternalInput").ap()
    wall_d = nc.dram_tensor("wall", [NP, 128, PW], F32, kind="ExternalInput").ap()
    cvec_d = nc.dram_tensor("cvec", [128, NCV], F32, kind="ExternalInput").ap()
    cbc_d = nc.dram_tensor("cbc", [128, NCB], F32, kind="ExternalInput").ap()
    wst_d = nc.dram_tensor("wst", [128, 4, 128], F32, kind="ExternalInput").ap()
    out_d = nc.dram_tensor("out", [D, S], F32, kind="ExternalOutput").ap()
    wsc_d = nc.dram_tensor("wsc", [NP, 128, PW], BF16, kind="Internal").ap()
    dbg_d = {}
    if dbg:
        for name, shape in dbg.items():
            dbg_d[name] = nc.dram_tensor("dbg_" + name, list(shape), F32, kind="ExternalOutput").ap()

    es = ExitStack()
    with es:
        sb = lambda name, shape, dt: es.enter_context(nc.sbuf_tensor("sb_" + name, shape, dt))
        xf = sb("xf", [128, KC, TT], F32)
        xb = sb("xb", [128, KC, TT], BF16)
        KT = sb("KT", [128, 4, S], BF16)
        Vx = sb("Vx", [128, S // 128, 4, 130], BF16)
        wring = sb("wring", [128, NSLOT, PW], BF16)
        FF = sb("FF", [128, 12, TT], F32)
        arena = sb("arena", [128, ARENA_B // 2], BF16)
        cvec = sb("cvec", [128, NCV], F32)
        cbc = sb("cbc", [128, NCB], F32)
        wstf = sb("wstf", [128, 4, 128], F32)
        WcT = sb("WcT", [128, 4, 128], BF16)
        identb = sb("identb", [128, 128], BF16)
        identf = sb("identf", [128, 128], F32)
        trif = sb("trif", [128, 128], F32)
        trib = sb("trib", [128, 128], BF16)
        ones1024 = sb("ones1024", [128, 128], F32)
        ones512 = sb("ones512", [128, 128], F32)
        S1 = sb("S1", [128, TT], F32)
        S2 = sb("S2", [128, TT], F32)
        MEAN = sb("MEAN", [128, TT], F32)
        RSTD = sb("RSTD", [128, TT], F32)
        SG = sb("SG", [128, 3, TT], F32)
        zhalo = sb("zhalo", [128, 4, 30], BF16)
        chhalo = sb("chhalo", [128, 4, 2], F32)
        st16 = sb("st16", [128, 6, 16], F32)
        gsc = sb("gsc", [128, 128], F32)
        lamt = sb("lamt", [128, 64], F32)
        lams = sb("lams", [128, 8], F32)
        ep = sb("ep", [128, 2, 8], F32)
        ss16 = sb("ss16", [128, 32], F32)
        lnd = sb("lnd", [128, 1], F32)
        epT = sb("epT", [128, 2, 128], F32)
        epO = sb("epO", [128, 2, 128], F32)
        epJ = sb("epJ", [128, 128], F32)
        ps = es.enter_context(nc.psum_tensor("ps", [128, 8, 512], F32))

        sch = Sched(nc, es)
        add = sch.add

        def aview(off, dt, n):
            a = arena[:, off // 2:off // 2 + (n * (4 if dt == F32 else 2)) // 2]
            return a.bitcast(F32) if dt == F32 else a

        def ablk(lo, hi):
            return [("A", i) for i in range(lo // 1024, (hi - 1) // 1024 + 1)]

        hT = arena[:, 0:NFC * 512].rearrange("p (c n) -> p c n", n=512)
        yT = arena[:, 0:8 * 512].rearrange("p (c n) -> p c n", n=512)
        ZOFF, CHOFF, DG1OFF = 8192, 12544, 20768
        zbuf = aview(ZOFF, BF16, 4 * 542).rearrange("p (c n) -> p c n", n=542)
        chbuf = aview(CHOFF, F32, 4 * 514).rearrange("p (c n) -> p c n", n=514)
        Dg = [aview(o, BF16, 31 * 128).rearrange("p (k n) -> p k n", n=128) for o in (CHOFF, DG1OFF)]
        R_dg = [ablk(o, o + 31 * 128 * 2) for o in (CHOFF, DG1OFF)]
        QOFF, EOFF, YCOFF = 8192, 16384, 20480
        qz = [aview(QOFF + m * 4096, BF16, 4 * 512).rearrange("p (c n) -> p c n", n=512) for m in range(2)]
        E2 = [aview(EOFF + i * 2048, BF16, 2 * 512).rearrange("p (c n) -> p c n", n=512) for i in range(2)]
        yc = aview(YCOFF, BF16, 4 * 512).rearrange("p (c n) -> p c n", n=512)
        R_hT = lambda fc: [("A", fc)]
        R_yT = lambda c: [("A", c)]
        R_z = lambda c: ablk(ZOFF + c * 542 * 2, ZOFF + (c + 1) * 542 * 2)
        R_ch = lambda c: ablk(CHOFF + c * 514 * 4, CHOFF + (c + 1) * 514 * 4)
        R_q = lambda h: ablk(QOFF + h * 1024, QOFF + (h + 1) * 1024) + ablk(QOFF + 4096 + h * 1024, QOFF + 4096 + (h + 1) * 1024)
        R_qall = ablk(QOFF, QOFF + 8192)
        R_E2 = lambda i: ablk(EOFF + i * 2048, EOFF + (i + 1) * 2048)
        R_yc = lambda q: ablk(YCOFF + q * 1024, YCOFF + (q + 1) * 1024)

        def bcast_mid(ap2d, n):
            pr = [list(p) for p in ap2d.ap]
            return bass.AP(ap2d.tensor, ap2d.offset, [pr[0], [0, n]] + pr[1:])

        def bcast_last(ap2d, n):
            pr = [list(p) for p in ap2d.ap]
            return bass.AP(ap2d.tensor, ap2d.offset, pr + [[0, n]])

        cv = lambda col: cvec[:, col:col + 1]

        def I(eng, meth, *args, reads=(), writes=(), dma=None, final=False, **kw):
            return add(eng, lambda e: getattr(e, meth)(*args, **kw), reads, writes, dma, final)

        XF = [("xf", k) for k in range(8)]
        F0r = [("F0", n) for n in range(4)]
        F1r = [("F1", n) for n in range(4)]
        F2r = [("F2", n) for n in range(4)]
        F0 = FF[:, 0:4, :]
        F1 = FF[:, 4:8, :]
        F2 = FF[:, 8:12, :]

        def x_src(ti):
            return x_d[:, ti * TT:(ti + 1) * TT].rearrange("(k p) n -> p k n", p=128)

        def load_xb(ti):
            I("pool", "dma_start", out=xb[:], in_=x_src(ti), writes=[("xb", k) for k in range(8)], dma="xbin")

        def load_xf(ti):
            I("pool", "dma_start", out=xf[:], in_=x_src(ti), writes=XF, dma="xfin")

        load_xb(0)
        def convert_weights():
            for g in range(NP // 2):
                I("pool", "dma_start", out=wsc_d[2 * g:2 * g + 2].rearrange("a p n -> p a n"),
                  in_=wall_d[2 * g:2 * g + 2].rearrange("a p n -> p a n"), writes=[("wsc", g)], dma="cv%d" % g)
                if g == 1:
                    load_xf(0)

        convert_weights()
        I("sp", "dma_start", out=cvec[:], in_=cvec_d, writes=["cvec"], dma="c0")
        I("sp", "dma_start", out=cbc[:], in_=cbc_d, writes=["cbc"], dma="c1")
        I("sp", "dma_start", out=wstf[:], in_=wst_d, writes=["wstf"], dma="c2")

        I("pool", "memset", identf[:], 0.0, writes=["identf"])
        I("pool", "affine_select", out=identf[:], in_=identf[:], pattern=[[-1, 128]], compare_op=ALU.not_equal,
          fill=1.0, base=0, channel_multiplier=1, reads=["identf"], writes=["identf"])
        I("pool", "tensor_copy", identb[:], identf[:], reads=["identf"], writes=["identb"])
        I("pool", "memset", trif[:], 1.0, writes=["trif"])
        I("pool", "affine_select", out=trif[:], in_=trif[:], pattern=[[1, 128]], compare_op=ALU.is_ge,
          fill=0.0, base=0, channel_multiplier=-1, reads=["trif"], writes=["trif"])
        I("pool", "tensor_copy", trib[:], trif[:], reads=["trif"], writes=["trib"])
        I("pool", "memset", ones1024[:], 1.0 / 1024.0, writes=["ones1024"])
        I("pool", "memset", ones512[:], 1.0 / 512.0, writes=["ones512"])
        I("pool", "memset", zhalo[:], 0.0, writes=["zhalo"])
        I("pool", "memset", chhalo[:], 0.0, writes=["chhalo"])
        I("pool", "memset", Vx[:, :, :, 128:130], 1.0, writes=["Vx_all"])
        I("pool", "tensor_tensor", out=wstf[:], in0=wstf[:], in1=bcast_mid(trif[:], 4), op=ALU.mult,
          reads=["wstf", "trif"], writes=["wstf"])
        I("pool", "tensor_copy", WcT[:], wstf[:], reads=["wstf"], writes=["WcT"])
        I("dve", "tensor_scalar", out=gsc[:], in0=cbc[:, CB_SG:CB_SG + 128],
          scalar1=(1.0 - LAMBDA_INIT) * math.sqrt(128.0), scalar2=None, op0=ALU.mult, reads=["cbc"], writes=["gsc"])
        I("dve", "tensor_tensor", out=lamt[:], in0=cbc[:, CB_LQ1:CB_LQ1 + 64], in1=cbc[:, CB_LK1:CB_LK1 + 64],
          op=ALU.mult, reads=["cbc"], writes=["lamt"])
        I("dve", "tensor_reduce", out=lams[:, 0:1], in_=lamt[:], axis=AX.X, op=ALU.add, reads=["lamt"], writes=["lams"])
        I("dve", "tensor_tensor", out=lamt[:], in0=cbc[:, CB_LQ2:CB_LQ2 + 64], in1=cbc[:, CB_LK2:CB_LK2 + 64],
          op=ALU.mult, reads=["cbc", "lams"], writes=["lamt"])
        I("dve", "tensor_reduce", out=lams[:, 1:2], in_=lamt[:], axis=AX.X, op=ALU.add, reads=["lamt"], writes=["lams"])
        I("act", "activation", lams[:, 2:4], lams[:, 0:2], AF.Exp, reads=["lams"], writes=["lams"])
        I("dve", "tensor_tensor", out=lams[:, 4:5], in0=lams[:, 3:4], in1=lams[:, 2:3], op=ALU.subtract,
          reads=["lams"], writes=["lams"])
        I("dve", "tensor_scalar", out=lams[:, 4:5], in0=lams[:, 4:5], scalar1=-LAMBDA_INIT, scalar2=None,
          op0=ALU.add, reads=["lams"], writes=["lams"])
        neglam = lams[:, 4:5]

        state = {"slot": 0, "bg": 0, "rb": 0}

        def load_piece(pidx):
            slot = state["slot"]
            state["slot"] = (slot + 1) % NSLOT
            I("sp", "dma_start", out=wring[:, slot, :], in_=wsc_d[pidx],
              reads=[("wsc", pidx // 2)], writes=[("w", slot)], dma="w%d" % slot)
            return slot

        def bank_group():
            g = state["bg"]
            state["bg"] = 1 - g
            return [4 * g + i for i in range(4)]

        def one_bank():
            b = state["rb"]
            state["rb"] = (b + 1) % 8
            return b

        def wv(slot, pat, **kw):
            return wring[:, slot, :].rearrange(pat, **kw)

        def mm_feature(slot, banks, rhs_of, rhs_res, nk=8, kouter=False):
            w = wv(slot, "p (k n) -> p k n", n=512)
            order = [(k, n) for k in range(nk) for n in range(4)] if kouter else \
                    [(k, n) for n in range(4) for k in range(nk)]
            for k, n in order:
                if True:
                    I("pe", "matmul", ps[:, banks[n], :], w[:, k, n * 128:(n + 1) * 128], rhs_of(k),
                      start=(k == 0), stop=(k == nk - 1),
                      reads=[("w", slot)] + rhs_res(k), writes=[("ps", banks[n])])

        def mm_token(slot, banks, kouter=False):
            w = wv(slot, "p (k n) -> p k n", n=512)
            order = [(k, tt) for k in range(8) for tt in range(4)] if kouter else \
                    [(k, tt) for tt in range(4) for k in range(8)]
            for k, tt in order:
                if True:
                    I("pe", "matmul", ps[:, banks[tt], :], xb[:, k, tt * 128:(tt + 1) * 128], w[:, k, :],
                      start=(k == 0), stop=(k == 7), reads=[("w", slot), ("xb", k)], writes=[("ps", banks[tt])])

        R_xb = lambda k: [("xb", k)]
        xb_of = lambda k: xb[:, k, :]

        def stats_tail(bm, be, after_var=None):
            I("act", "activation", RSTD[:], ps[:, bm, :], AF.Square, reads=[("ps", bm)], writes=["RSTD"])
            I("act", "activation", MEAN[:], ps[:, bm, :], AF.Copy, reads=[("ps", bm)], writes=["MEAN"])
            I("dve", "scalar_tensor_tensor", out=RSTD[:], in0=ps[:, be, :], scalar=LN_EPS, in1=RSTD[:],
              op0=ALU.add, op1=ALU.subtract, reads=[("ps", be), "RSTD"], writes=["RSTD"])
            I("act", "activation", RSTD[:], RSTD[:], AF.Ln, reads=["RSTD"], writes=["RSTD"])
            I("act", "activation", RSTD[:], RSTD[:], AF.Exp, scale=-0.5, reads=["RSTD"], writes=["RSTD"])
            if after_var is not None:
                after_var()

        R_FF = lambda c: [("F0", c)] if c < 4 else [("F1", c - 4)]

        def layer_norm(gcol, bcol, write_xb=True):
            bm, be = one_bank(), one_bank()
            I("pe", "matmul", ps[:, bm, :], ones1024[:], S1[:], start=True, stop=True,
              reads=["S1", "ones1024"], writes=[("ps", bm)])
            I("pe", "matmul", ps[:, be, :], ones1024[:], S2[:], start=True, stop=True,
              reads=["S2", "ones1024"], writes=[("ps", be)])
            def sub(k):
                I("dve", "tensor_tensor", out=xf[:, k, :], in0=xf[:, k, :], in1=MEAN[:], op=ALU.subtract,
                  reads=[("xf", k), "MEAN"], writes=[("xf", k)])

            def center():
                for k in range(4):
                    sub(k)
            stats_tail(bm, be, center)
            for k in range(8):
                I("dve", "tensor_tensor", out=xf[:, k, :], in0=xf[:, k, :], in1=RSTD[:], op=ALU.mult,
                  reads=[("xf", k), "RSTD"], writes=[("xf", k)])
                if k + 4 < 8:
                    sub(k + 4)
                if write_xb:
                    I("act", "activation", xb[:, k, :], xf[:, k, :], AF.Identity, bias=cv(bcol + k),
                      scale=cv(gcol + k), reads=[("xf", k), "cvec"], writes=[("xb", k)])
            for k in range(8):
                I("act", "activation", xf[:, k, :], xf[:, k, :], AF.Identity, bias=cv(bcol + k), scale=cv(gcol + k),
                  reads=[("xf", k), "cvec"], writes=[("xf", k)])

        def residual_evac(banks, c0):
            for n in range(4):
                c = c0 + n
                I("dve", "scalar_tensor_tensor", out=xf[:, c, :], in0=xf[:, c, :], scalar=ALPHA,
                  in1=ps[:, banks[n], :], op0=ALU.mult, op1=ALU.add,
                  reads=[("xf", c), ("ps", banks[n])], writes=[("xf", c)])
                I("act", "activation", FF[:, c, :], xf[:, c, :], AF.Square, reads=[("xf", c)], writes=R_FF(c))
                if c == 6:
                    I("act", "activation", lnd[:], ones1024[:, 0:1], AF.Ln, reads=["ones1024"], writes=["lnd"])
                if c == 1:
                    I("dve", "tensor_tensor", out=S1[:], in0=xf[:, 0, :], in1=xf[:, 1, :], op=ALU.add,
                      reads=[("xf", 0), ("xf", 1)], writes=["S1"])
                    I("pool", "tensor_tensor", out=S2[:], in0=FF[:, 0, :], in1=FF[:, 1, :], op=ALU.add,
                      reads=R_FF(0) + R_FF(1), writes=["S2"])
                elif c > 1:
                    I("dve", "tensor_tensor", out=S1[:], in0=S1[:], in1=xf[:, c, :], op=ALU.add,
                      reads=[("xf", c), "S1"], writes=["S1"])
                    I("dve" if c == 7 else "pool", "tensor_tensor", out=S2[:], in0=S2[:], in1=FF[:, c, :], op=ALU.add,
                      reads=R_FF(c) + ["S2"], writes=["S2"])

        def out_proj(pbase):
            for j in range(2):
                slot = load_piece(pbase + 5 + j)
                banks = bank_group()
                mm_feature(slot, banks, lambda k: yT[:, k, :], R_yT, kouter=(j == 0))
                residual_evac(banks, 4 * j)

        def ffn(pbase, after_gateup=None):
            for j in range(11):
                slot = load_piece(pbase + 7 + j)
                banks = bank_group()
                w = wv(slot, "p (k a n) -> p k a n", a=2, n=256)
                order = [(k, a, hf) for k in range(8) for a in range(2) for hf in range(2)] if j == 0 else \
                        [(k, a, hf) for hf in range(2) for a in range(2) for k in range(8)]
                for k, a, hf in order:
                    if True:
                        if True:
                            b = banks[2 * a + hf]
                            I("pe", "matmul", ps[:, b, :], w[:, k, a, hf * 128:(hf + 1) * 128], xb[:, k, :],
                              start=(k == 0), stop=(k == 7), reads=[("w", slot), ("xb", k)], writes=[("ps", b)])
                for hf in range(2):
                    fc = 2 * j + hf
                    sgi = fc % 3
                    I("act", "activation", SG[:, sgi, :], ps[:, banks[hf], :], AF.Silu,
                      reads=[("ps", banks[hf])], writes=[("SG", sgi)])
                    I("dve", "tensor_tensor", out=hT[:, fc, :], in0=SG[:, sgi, :], in1=ps[:, banks[2 + hf], :],
                      op=ALU.mult, reads=[("SG", sgi), ("ps", banks[2 + hf])], writes=R_hT(fc))
            if after_gateup is not None:
                after_gateup()
            for dh in range(2):
                banks = bank_group()
                for g, (lo, hi) in enumerate(DOWN_GROUPS):
                    slot = load_piece(pbase + 18 + dh * 3 + g)
                    w = wv(slot, "p (k n) -> p k n", n=512)
                    order = [(i, fc, n) for i, fc in enumerate(range(lo, hi)) for n in range(4)] if g < 2 else \
                            [(i, fc, n) for n in range(4) for i, fc in enumerate(range(lo, hi))]
                    for i, fc, n in order:
                        if True:
                            I("pe", "matmul", ps[:, banks[n], :], w[:, i, n * 128:(n + 1) * 128], hT[:, fc, :],
                              start=(fc == 0), stop=(fc == NFC - 1),
                              reads=[("w", slot)] + R_hT(fc), writes=[("ps", banks[n])])
                residual_evac(banks, 4 * dh)

        def build_dg(n):
            wtap = cvec[:, CV_CB + n:CV_CB + n + 1]
            pr = [list(p) for p in wtap.ap]
            w_bc = bass.AP(wtap.tensor, wtap.offset, [pr[0], [4, 31], [0, 128]])
            I("dve", "tensor_tensor", out=Dg[(n + 1) % 2], in0=bcast_mid(identb[:], 31), in1=w_bc, op=ALU.mult,
              reads=["identb", "cvec"], writes=R_dg[(n + 1) % 2])

        def mixer0(ti, pbase):
            build_dg(0)
            for n in range(4):
                I("pool", "tensor_copy", zbuf[:, n, 0:30], zhalo[:, n, :], reads=["zhalo"], writes=R_z(n))
            slot = load_piece(pbase + 1)
            banks = bank_group()
            mm_feature(slot, banks, xb_of, R_xb, kouter=True)
            for n in range(4):
                I("act", "activation", F0[:, n, :], ps[:, banks[n], :], AF.Copy,
                  reads=[("ps", banks[n])], writes=[("F0", n)])
            slot = load_piece(pbase + 2)
            banks = bank_group()
            mm_feature(slot, banks, xb_of, R_xb)
            for n in range(4):
                I("pool", "tensor_copy", chbuf[:, n, 0:2], chhalo[:, n, :], reads=["chhalo"], writes=R_ch(n))
                I("dve", "tensor_tensor", out=chbuf[:, n, 2:514], in0=F0[:, n, :], in1=ps[:, banks[n], :], op=ALU.mult,
                  reads=[("F0", n), ("ps", banks[n])], writes=R_ch(n))
            slot = load_piece(pbase + 0)
            banks = bank_group()
            mm_feature(slot, banks, xb_of, R_xb)
            for n in range(4):
                I("act", "activation", F1[:, n, :], ps[:, banks[n], :], AF.Copy,
                  reads=[("ps", banks[n])], writes=[("F1", n)])
            for n in range(4):
                I("dve", "tensor_scalar", out=F0[:, n, :], in0=chbuf[:, n, 2:514], scalar1=cv(CV_CA + 8 + n),
                  scalar2=None, op0=ALU.mult, reads=R_ch(n) + ["cvec"], writes=[("F0", n)])
                for k in (1, 0):
                    I("dve", "scalar_tensor_tensor", out=F0[:, n, :], in0=chbuf[:, n, k:k + 512],
                      scalar=cv(CV_CA + 4 * k + n), in1=F0[:, n, :], op0=ALU.mult, op1=ALU.add,
                      reads=R_ch(n) + [("F0", n)], writes=[("F0", n)])
                I("pool", "tensor_copy", chhalo[:, n, :], chbuf[:, n, 512:514], reads=R_ch(n), writes=["chhalo"])
                I("dve", "tensor_tensor", out=yT[:, n, :], in0=F0[:, n, :], in1=F1[:, n, :], op=ALU.mult,
                  reads=[("F0", n), ("F1", n)], writes=R_yT(n))
            build_dg(1)
            slot = load_piece(pbase + 4)
            banks = bank_group()
            mm_feature(slot, banks, xb_of, R_xb)
            for n in range(4):
                I("act", "activation", F0[:, n, :], ps[:, banks[n], :], AF.Sigmoid,
                  reads=[("ps", banks[n])], writes=[("F0", n)])
            slot = load_piece(pbase + 3)
            banks = bank_group()
            mm_feature(slot, banks, xb_of, R_xb)
            for n in range(4):
                I("dve", "tensor_tensor", out=zbuf[:, n, 30:542], in0=F0[:, n, :], in1=ps[:, banks[n], :], op=ALU.mult,
                  reads=[("F0", n), ("ps", banks[n])], writes=R_z(n))
            banks = bank_group()
            for n in range(4):
                d = (n + 1) % 2
                for k in range(31):
                    I("pe", "matmul", ps[:, banks[n], :], Dg[d][:, k, :], zbuf[:, n, k:k + 512],
                      start=(k == 0), stop=(k == 30), reads=R_dg[d] + R_z(n), writes=[("ps", banks[n])])
                if n + 2 < 4:
                    build_dg(n + 2)
                I("pool", "tensor_copy", zhalo[:, n, :], zbuf[:, n, 512:542], reads=R_z(n), writes=["zhalo"])
                I("act", "activation", F1[:, n, :], ps[:, banks[n], :], AF.Identity, bias=cv(CV_BB + n),
                  reads=[("ps", banks[n]), "cvec"], writes=[("F1", n)])
                I("act", "activation", F2[:, n, :], ps[:, banks[n], :], AF.Square, bias=cv(CV_BB + n),
                  reads=[("ps", banks[n]), "cvec"], writes=[("F2", n)])
                if n == 2:
                    I("act", "activation", lnd[:], ones1024[:, 0:1], AF.Ln, reads=["ones1024"], writes=["lnd"])
                if n == 1:
                    I("dve", "tensor_tensor", out=S1[:], in0=F1[:, 0, :], in1=F1[:, 1, :], op=ALU.add,
                      reads=[("F1", 0), ("F1", 1)], writes=["S1"])
                    I("pool", "tensor_tensor", out=S2[:], in0=F2[:, 0, :], in1=F2[:, 1, :], op=ALU.add,
                      reads=[("F2", 0), ("F2", 1)], writes=["S2"])
                elif n > 1:
                    I("dve", "tensor_tensor", out=S1[:], in0=S1[:], in1=F1[:, n, :], op=ALU.add,
                      reads=[("F1", n), "S1"], writes=["S1"])
                    I("pool", "tensor_tensor", out=S2[:], in0=S2[:], in1=F2[:, n, :], op=ALU.add,
                      reads=[("F2", n), "S2"], writes=["S2"])
            bm, be = one_bank(), one_bank()
            I("pe", "matmul", ps[:, bm, :], ones512[:], S1[:], start=True, stop=True,
              reads=["S1", "ones512"], writes=[("ps", bm)])
            I("pe", "matmul", ps[:, be, :], ones512[:], S2[:], start=True, stop=True,
              reads=["S2", "ones512"], writes=[("ps", be)])
            def center():
                for n in range(4):
                    I("dve", "tensor_tensor", out=F1[:, n, :], in0=F1[:, n, :], in1=MEAN[:], op=ALU.subtract,
                      reads=[("F1", n), "MEAN"], writes=[("F1", n)])
            stats_tail(bm, be, center)
            for n in range(4):
                I("dve", "tensor_tensor", out=F1[:, n, :], in0=F1[:, n, :], in1=RSTD[:], op=ALU.mult,
                  reads=[("F1", n), "RSTD"], writes=[("F1", n)])
                I("act", "activation", yT[:, 4 + n, :], F1[:, n, :], AF.Silu, bias=cv(CV_LB + n), scale=cv(CV_LG + n),
                  reads=[("F1", n), "cvec"], writes=R_yT(4 + n))

        VGOFF = YCOFF + 4096
        vgln = aview(VGOFF, BF16, 4 * 512).rearrange("p (c n) -> p c n", n=512)
        R_vg = ablk(VGOFF, VGOFF + 4096)
        psb = ps[:].bitcast(BF16)

        def mixer1(ti, pbase):
            slot = load_piece(pbase + 4)
            banks = bank_group()
            mm_token(slot, banks, kouter=True)
            for tt in range(4):
                I("act", "activation", F1[:, tt, :], ps[:, banks[tt], :], AF.Gelu_apprx_tanh,
                  reads=[("ps", banks[tt])], writes=[("F1", tt)])
            F1g = F1.rearrange("p t (g d) -> p (t g) d", d=128)
            F2g = F2.rearrange("p t (g d) -> p (t g) d", d=128)
            I("dve", "tensor_reduce", out=st16[:, 0, :], in_=F1g, axis=AX.X, op=ALU.add, reads=F1r, writes=["st16"])
            I("act", "activation", F2, F1, AF.Square, reads=F1r, writes=F2r)
            I("dve", "tensor_reduce", out=st16[:, 1, :], in_=F2g, axis=AX.X, op=ALU.add,
              reads=F2r + ["st16"], writes=["st16"])
            I("dve", "tensor_scalar", out=st16[:, 2, :], in0=st16[:, 0, :], scalar1=1.0 / 128.0, scalar2=None,
              op0=ALU.mult, reads=["st16"], writes=["st16"])
            I("dve", "tensor_tensor", out=st16[:, 3, :], in0=st16[:, 2, :], in1=st16[:, 2, :], op=ALU.mult,
              reads=["st16"], writes=["st16"])
            I("dve", "scalar_tensor_tensor", out=st16[:, 4, :], in0=st16[:, 1, :], scalar=1.0 / 128.0,
              in1=st16[:, 3, :], op0=ALU.mult, op1=ALU.subtract, reads=["st16"], writes=["st16"])
            I("dve", "tensor_scalar", out=st16[:, 5, :], in0=st16[:, 4, :], scalar1=LN_EPS, scalar2=None,
              op0=ALU.add, reads=["st16"], writes=["st16"])
            slot = load_piece(pbase + 3)
            banks = bank_group()
            mm_feature(slot, banks, xb_of, R_xb)
            for n in range(4):
                I("act", "activation", F0[:, n, :], ps[:, banks[n], :], AF.Gelu_apprx_tanh,
                  reads=[("ps", banks[n])], writes=[("F0", n)])
            slot = load_piece(pbase + 0)
            banks = bank_group()
            mm_feature(slot, banks, xb_of, R_xb)
            I("pool", "memset", qz[0][64:128, :, :], 0.0, writes=R_qall)
            I("pool", "memset", qz[1][0:64, :, :], 0.0, writes=R_qall)
            for n in range(4):
                I("act", "activation", qz[0][0:64, n, :], ps[0:64, banks[n], :], AF.Copy,
                  reads=[("ps", banks[n])], writes=R_q(n))
                I("act", "activation", qz[1][64:128, n, :], ps[64:128, banks[n], :], AF.Copy,
                  reads=[("ps", banks[n])], writes=R_q(n))
            I("act", "activation", st16[:, 5, :], st16[:, 5, :], AF.Ln, reads=["st16"], writes=["st16"])
            I("act", "activation", st16[:, 5, :], st16[:, 5, :], AF.Exp, scale=-0.5, reads=["st16"], writes=["st16"])
            I("dve", "tensor_tensor", out=F1g, in0=F1g, in1=bcast_last(st16[:, 2, :], 128), op=ALU.subtract,
              reads=F1r + ["st16"], writes=F1r)
            I("dve", "tensor_tensor", out=F1g, in0=F1g, in1=bcast_last(st16[:, 5, :], 128), op=ALU.mult,
              reads=F1r + ["st16"], writes=F1r)
            I("pool", "tensor_tensor", out=F1, in0=F1, in1=bcast_mid(cbc[:, CB_GG:CB_GG + 512], 4), op=ALU.mult,
              reads=F1r + ["cbc"], writes=F1r)
            I("pool", "tensor_tensor", out=vgln, in0=F1, in1=bcast_mid(cbc[:, CB_GB:CB_GB + 512], 4), op=ALU.add,
              reads=F1r + ["cbc"], writes=R_vg)
            slot = load_piece(pbase + 1)
            banks = bank_group()
            mm_feature(slot, banks, xb_of, R_xb)
            for n in range(4):
                I("dve", "tensor_copy", KT[:, n, ti * TT:(ti + 1) * TT], ps[:, banks[n], :],
                  reads=[("ps", banks[n])], writes=[("KT", ti)])
            slot = load_piece(pbase + 2)
            banks = bank_group()
            mm_token(slot, banks)
            for tt in range(4):
                I("act", "activation", Vx[:, 4 * ti + tt, :, 0:128],
                  ps[:, banks[tt], :].rearrange("p (h e) -> p h e", e=128), AF.Copy,
                  reads=[("ps", banks[tt]), "Vx_all"], writes=[("Vx", 4 * ti + tt)])
            nkt = 4 * ti + 4
            spairs = ((3, 4), (5, 6))
            accb = (0, 1, 2)
            acc_sb = (F1, SG[:])
            acc_sb_res = (F1r, [("SG", i) for i in range(3)])

            def acc_ap(qt, m):
                a = qt * 2 + m
                return ps[:, accb[a // 3], (a % 3) * 129:(a % 3) * 129 + 129], ("ps", accb[a // 3])

            def acc_copy_ap(h, qt, m):
                a = qt * 2 + m
                return acc_sb[h % 2][:, a // 3, (a % 3) * 129:(a % 3) * 129 + 129]

            def rec_pv(blk, pi2, bi):
                h, m, kt = blk
                q0 = max(0, kt - 4 * ti)
                for qt in range(q0, 4):
                    dst, dres = acc_ap(qt, m)
                    I("pe", "matmul", dst, E2[pi2][:, bi, (qt - q0) * 128:(qt - q0 + 1) * 128], Vx[:, kt, h, 0:129],
                      start=(kt == 0 and m == 0 and qt in (0, 2, 3)), stop=(kt == 4 * ti + qt),
                      skip_group_check=True, reads=R_E2(pi2) + [("Vx", kt), "Vx_all"], writes=[dres])

            def epilogue(h):
                sbuf_res = acc_sb_res[h % 2]
                for bi_, ncols in ((0, 387), (1, 387), (2, 258)):
                    I("dve", "tensor_copy", acc_sb[h % 2][:, bi_, 0:ncols], ps[:, accb[bi_], 0:ncols],
                      reads=[("ps", accb[bi_])], writes=sbuf_res)
                for qt in range(4):
                    a0 = acc_copy_ap(h, qt, 0)
                    a1 = acc_copy_ap(h, qt, 1)
                    pi = qt % 2
                    EP = [("ep", pi)]
                    eo = F2[:, h, qt * 128:(qt + 1) * 128]
                    I("dve", "reciprocal", ep[:, pi, 0:1], a0[:, 128:129], reads=sbuf_res, writes=EP)
                    I("dve", "reciprocal", ep[:, pi, 1:2], a1[:, 128:129], reads=sbuf_res + EP, writes=EP)
                    I("dve", "tensor_tensor", out=ep[:, pi, 2:3], in0=ep[:, pi, 1:2], in1=neglam, op=ALU.mult,
                      reads=EP + ["lams"], writes=EP)
                    I("dve", "tensor_scalar", out=epT[:, pi, :], in0=a1[:, 0:128], scalar1=ep[:, pi, 2:3],
                      scalar2=None, op0=ALU.mult, reads=sbuf_res + EP, writes=[("epT", pi)])
                    I("dve", "scalar_tensor_tensor", out=eo, in0=a0[:, 0:128], scalar=ep[:, pi, 0:1],
                      in1=epT[:, pi, :], op0=ALU.mult, op1=ALU.add,
                      reads=sbuf_res + [("epT", pi)] + EP, writes=[("F2", h)])
                    I("dve", "scalar_tensor_tensor", out=epJ[:], in0=eo, scalar=1.0, in1=eo, op0=ALU.mult, op1=ALU.mult,
                      accum_out=ss16[:, 4 * h + qt:4 * h + qt + 1], reads=[("F2", h)], writes=["epJ", "ss16"])

            blocks = [(h, m, kt) for h in range(4) for m in range(2) for kt in range(nkt)]
            npair = len(blocks) // 2

            def rec_s(p):
                for bi in range(2):
                    h, m, kt = blocks[2 * p + bi]
                    c0 = max(0, kt - 4 * ti) * 128
                    bk = spairs[p % 2][bi]
                    I("pe", "matmul", ps[:, bk, 0:512 - c0], KT[:, h, kt * 128:(kt + 1) * 128],
                      qz[m][:, h, c0:512], start=True, stop=True,
                      reads=[("KT", kt // 4)] + R_q(h), writes=[("ps", bk)])

            rec_s(0)
            rec_s(1)
            for p in range(npair):
                b0, b1 = spairs[p % 2]
                pi2 = p % 2
                h, m, kt = blocks[2 * p]
                if kt + 1 < 4 * ti:
                    I("act", "activation", E2[pi2], ps[:, b0:b0 + 2, :], AF.Exp, scale=0.125,
                      reads=[("ps", b0), ("ps", b1)], writes=R_E2(pi2))
                else:
                    for bi in range(2):
                        j = kt + bi - 4 * ti
                        ncol = 512 - j * 128
                        I("act", "activation", E2[pi2][:, bi, 0:ncol], ps[:, spairs[p % 2][bi], 0:ncol], AF.Exp,
                          scale=0.125, reads=[("ps", spairs[p % 2][bi])], writes=R_E2(pi2))
                        I("pool", "tensor_tensor", out=E2[pi2][:, bi, 0:128], in0=E2[pi2][:, bi, 0:128], in1=trib[:],
                          op=ALU.mult, reads=R_E2(pi2) + ["trib"], writes=R_E2(pi2))
                if p + 2 < npair:
                    rec_s(p + 2)
                for bi in range(2):
                    rec_pv(blocks[2 * p + bi], pi2, bi)
                if m == 1 and kt + 1 == nkt - 1:
                    epilogue(h)
            I("dve", "tensor_scalar", out=ss16[:, 16:32], in0=ss16[:, 0:16], scalar1=128.0 * LN_EPS, scalar2=None,
              op0=ALU.add, reads=["ss16"], writes=["ss16b"])
            I("act", "activation", ss16[:, 16:32], ss16[:, 16:32], AF.Ln, reads=["ss16b"], writes=["ss16b"])
            I("act", "activation", ss16[:, 16:32], ss16[:, 16:32], AF.Exp, scale=-0.5, reads=["ss16b"], writes=["ss16b"])
            for h in range(4):
                for qt in range(4):
                    I("dve", "scalar_tensor_tensor", out=yc[:, qt, h * 128:(h + 1) * 128],
                      in0=F2[:, h, qt * 128:(qt + 1) * 128], scalar=ss16[:, 16 + 4 * h + qt:17 + 4 * h + qt],
                      in1=gsc[:], op0=ALU.mult, op1=ALU.mult, reads=[("F2", h), "gsc", "ss16b"], writes=R_yc(qt))
            banks = bank_group()
            for g in range(4):
                for tt in range(4):
                    I("pe", "matmul", ps[:, banks[g], tt * 128:(tt + 1) * 128], vgln[:, tt, g * 128:(g + 1) * 128],
                      WcT[:, g, :], start=True, stop=True, reads=R_vg + ["WcT"], writes=[("ps", banks[g])])
            for g in range(4):
                I("dve", "tensor_tensor", out=F2[:, g, :].rearrange("p (t s) -> p t s", s=128),
                  in0=ps[:, banks[g], :].rearrange("p (t s) -> p t s", s=128),
                  in1=bcast_mid(cbc[:, CB_SB + 128 * g:CB_SB + 128 * g + 128], 4), op=ALU.add,
                  reads=[("ps", banks[g]), "cbc"], writes=[("F2", g)])
                I("pool", "tensor_tensor", out=yT[:, 4 + g, :], in0=F2[:, g, :], in1=F0[:, g, :], op=ALU.mult,
                  reads=[("F2", g), ("F0", g)], writes=R_yT(4 + g))
            for h in range(4):
                b = (6, 7)[h % 2]
                for qt in range(4):
                    I("pe", "transpose", psb[:, b, qt * 128:(qt + 1) * 128], yc[:, qt, h * 128:(h + 1) * 128],
                      identb[:], reads=R_yc(qt) + ["identb"], writes=[("ps", b)])
                I("act", "activation", yT[:, h, :], psb[:, b, 0:512], AF.Copy, reads=[("ps", b)], writes=R_yT(h))


        def stage(name):
            sch.tag = "after_" + name
            if stop_at == name:
                raise _Stop()

        def main_loop():
            for ti in range(n_tiles):
                stage("x")
                for L in range(2):
                    pbase = L * NPL
                    if L == 0:
                        mixer0(ti, pbase)
                        if ti > 0:
                            load_xf(ti)
                    else:
                        mixer1(ti, pbase)
                    stage("mix%d" % L)
                    if dbg and ("y%d" % L) in dbg_d and ti == 0:
                        for c in range(8):
                            I("pool", "tensor_copy", FF[:, 8 + c % 4, :], yT[:, c, :], reads=R_yT(c), writes=[("F2", c % 4)])
                            I("pool", "dma_start", out=dbg_d["y%d" % L][c], in_=FF[:, 8 + c % 4, :],
                              reads=[("F2", c % 4)], dma="dbg%s%d" % ("y%d" % L, c), final=True)
                    out_proj(pbase)
                    stage("out%d" % L)
                    layer_norm(CV_MG + 8 * L, CV_MB + 8 * L)
                    stage("ln%d" % L)
                    if dbg and ("m%d" % L) in dbg_d and ti == 0:
                        I("pool", "dma_start", out=dbg_d["m%d" % L].rearrange("k p n -> p k n"), in_=xf[:],
                          reads=XF, dma="dbgm%d" % L, final=True)
                    ffn(pbase, (lambda: load_xb(ti + 1)) if (L == 1 and ti + 1 < n_tiles) else None)
                    stage("ffn%d" % L)
                    layer_norm(CV_FG + 8 * L, CV_FB + 8 * L, write_xb=(L == 0))
                    stage("fln%d" % L)
                    if dbg and ("f%d" % L) in dbg_d and ti == 0:
                        I("pool", "dma_start", out=dbg_d["f%d" % L].rearrange("k p n -> p k n"), in_=xf[:],
                          reads=XF, dma="dbgf%d" % L, final=True)
                I("act", "dma_start", out=out_d[:, ti * TT:(ti + 1) * TT].rearrange("(k p) n -> p k n", p=128), in_=xf[:],
                  reads=XF, dma="xout", final=True)

        try:
            main_loop()
        except _Stop:
            pass

        sch.emit()
    return nc


_NC_CACHE = {}


def kernel(**inputs):
    x = np.asarray(inputs["x"], np.float32)
    shared = make_shared(inputs)
    if "nc" not in _NC_CACHE:
        _NC_CACHE["nc"] = build_program()
    nc = _NC_CACHE["nc"]
    in_maps = [dict(shared, x=np.ascontiguousarray(x[b].T)) for b in range(8)]
    res = run_bass_kernel_spmd(nc, in_maps, core_ids=list(range(8)))
    return np.stack([np.ascontiguousarray(np.asarray(r["out"], np.float32).T) for r in res.results], axis=0)
```

```python
import math
from contextlib import ExitStack
import numpy as np
import concourse.bass as bass
import concourse.mybir as mybir
from concourse.bass_utils import run_bass_kernel_spmd

F32 = mybir.dt.float32
BF16 = mybir.dt.bfloat16
AF = mybir.ActivationFunctionType
ALU = mybir.AluOpType
AX = mybir.AxisListType

S, D, TT, NT = 4096, 1024, 512, 8
KC = 8
NFC = 22
ALPHA = (2 * 2) ** 0.25
LN_EPS = 1e-5
LAMBDA_INIT = 0.8 - 0.6 * math.exp(-0.3 * 1)
NPL = 24
NP = 2 * NPL
PW = 4096
NSLOT = 4
DOWN_GROUPS = [(0, 8), (8, 16), (16, 22)]
ARENA_B = 28704

CV_CA = 0
CV_CB = 12
CV_BB = 136
CV_LG = 140
CV_LB = 144
CV_MG = 148
CV_MB = 164
CV_FG = 180
CV_FB = 196
NCV = 212
CB_GG = 0
CB_GB = 512
CB_SB = 1024
CB_SG = 1536
CB_LQ1 = 1664
CB_LK1 = 1728
CB_LQ2 = 1792
CB_LK2 = 1856
NCB = 1920


def _piece_in(w, j):
    return np.ascontiguousarray(w.reshape(8, 128, -1)[:, :, j * 512:(j + 1) * 512].transpose(1, 0, 2)).reshape(128, PW)


def _piece_gu(g, u, j):
    a = np.empty((128, 8, 2, 256), np.float32)
    a[:, :, 0, :] = g.reshape(8, 128, -1)[:, :, j * 256:(j + 1) * 256].transpose(1, 0, 2)
    a[:, :, 1, :] = u.reshape(8, 128, -1)[:, :, j * 256:(j + 1) * 256].transpose(1, 0, 2)
    return a.reshape(128, PW)


def _piece_down(wd, dh, grp):
    lo, hi = DOWN_GROUPS[grp]
    a = np.zeros((128, 8, 512), np.float32)
    a[:, :hi - lo, :] = wd.reshape(NFC, 128, 1024)[lo:hi, :, dh * 512:(dh + 1) * 512].transpose(1, 0, 2)
    return a.reshape(128, PW)


def _chunked(v):
    return np.ascontiguousarray(np.asarray(v, np.float32).reshape(-1, 128).T)


def make_shared(inp):
    f = lambda k: np.asarray(inp[k], np.float32)
    wall = np.empty((NP, 128, PW), np.float32)
    for L in range(2):
        w_in = f("even_w_in")[0] if L == 0 else f("odd_w_in")[0]
        w_out = f("even_w_out")[0] if L == 0 else f("odd_w_out")[0]
        b = L * NPL
        for j in range(5):
            wall[b + j] = _piece_in(w_in, j)
        for j in range(2):
            wall[b + 5 + j] = _piece_in(w_out, j)
        for j in range(11):
            wall[b + 7 + j] = _piece_gu(f("ffn_w_gate")[L], f("ffn_w_up")[L], j)
        for dh in range(2):
            for g in range(3):
                wall[b + 18 + dh * 3 + g] = _piece_down(f("ffn_w_down")[L], dh, g)
    cvec = np.zeros((128, NCV), np.float32)
    ca = f("even_conv_a_w")[0]
    cb = f("even_conv_b_w")[0]
    for k in range(3):
        cvec[:, CV_CA + 4 * k:CV_CA + 4 * k + 4] = _chunked(ca[k])
    for k in range(31):
        cvec[:, CV_CB + 4 * k:CV_CB + 4 * k + 4] = _chunked(cb[k])
    cvec[:, CV_BB:CV_BB + 4] = _chunked(f("even_conv_b_bias")[0])
    cvec[:, CV_LG:CV_LG + 4] = _chunked(f("even_conv_ln_g")[0])
    cvec[:, CV_LB:CV_LB + 4] = _chunked(f("even_conv_ln_b")[0])
    for L in range(2):
        cvec[:, CV_MG + 8 * L:CV_MG + 8 * L + 8] = _chunked(f("mix_ln_g")[L])
        cvec[:, CV_MB + 8 * L:CV_MB + 8 * L + 8] = _chunked(f("mix_ln_b")[L])
        cvec[:, CV_FG + 8 * L:CV_FG + 8 * L + 8] = _chunked(f("ffn_ln_g")[L])
        cvec[:, CV_FB + 8 * L:CV_FB + 8 * L + 8] = _chunked(f("ffn_ln_b")[L])
    row = np.zeros((NCB,), np.float32)
    row[CB_GG:CB_GG + 512] = f("odd_gmlp_ln_g")[0].reshape(-1)
    row[CB_GB:CB_GB + 512] = f("odd_gmlp_ln_b")[0].reshape(-1)
    row[CB_SB:CB_SB + 512] = f("odd_spatial_b")[0].reshape(-1)
    row[CB_SG:CB_SG + 128] = f("odd_subln_g")[0]
    row[CB_LQ1:CB_LQ1 + 64] = f("odd_lambda_q1")[0]
    row[CB_LK1:CB_LK1 + 64] = f("odd_lambda_k1")[0]
    row[CB_LQ2:CB_LQ2 + 64] = f("odd_lambda_q2")[0]
    row[CB_LK2:CB_LK2 + 64] = f("odd_lambda_k2")[0]
    cbc = np.ascontiguousarray(np.broadcast_to(row[None, :], (128, NCB)))
    wst = np.ascontiguousarray(f("odd_spatial_w")[0].transpose(2, 0, 1))
    return {"wall": wall, "cvec": cvec, "cbc": cbc, "wst": wst}


class Op:
    __slots__ = ("eng", "fn", "deps", "signal", "sigval", "dma", "tag")


class Sched:
    ENGS = ("pe", "act", "dve", "pool", "sp")

    def __init__(self, nc, es):
        self.nc, self.es = nc, es
        self.ops = {e: [] for e in self.ENGS}
        self.last_w, self.readers = {}, {}
        self.sems = {}
        self.dma_cnt = {}
        self.final = []
        self.tag = ""

    def sem(self, key):
        if key not in self.sems:
            self.sems[key] = self.es.enter_context(self.nc.semaphore("s_" + key))
        return self.sems[key]

    def add(self, eng, fn, reads=(), writes=(), dma=None, final=False):
        op = Op()
        op.eng, op.fn, op.dma, op.signal, op.sigval = eng, fn, dma, False, 0
        op.tag = self.tag
        writes = list(writes) + [r for r in reads if isinstance(r, tuple) and r[0] == "ps"]
        deps = []
        for r in reads:
            d = self.last_w.get(r)
            if d is not None:
                deps.append(d)
        for w in writes:
            d = self.last_w.get(w)
            if d is not None:
                deps.append(d)
            deps.extend(self.readers.get(w, ()))
        for r in reads:
            self.readers.setdefault(r, []).append(op)
        for w in writes:
            self.last_w[w] = op
            self.readers[w] = []
        seen, dd = set(), []
        for d in deps:
            if d is op or id(d) in seen:
                continue
            seen.add(id(d))
            dd.append(d)
            if not (d.dma is None and d.eng == eng and eng in ("pe", "sp")):
                d.signal = True
        op.deps = dd
        if dma is not None:
            self.dma_cnt[dma] = self.dma_cnt.get(dma, 0) + 1
            op.sigval = 16 * self.dma_cnt[dma]
            self.sem(dma)
            if final:
                self.final.append(op)
        self.ops[eng].append(op)
        return op

    def emit(self):
        nc = self.nc
        for e in self.ENGS:
            c = 0
            for op in self.ops[e]:
                if op.dma is None and op.signal:
                    c += 1
                    op.sigval = c
            self.sem("eng_" + e)
        block = self.es.enter_context(nc.Block())

        def tok(d):
            return ("eng_" + d.eng if d.dma is None else d.dma), d.sigval

        def run(e, eng):
            waited = {}
            for op in self.ops[e]:
                for d in op.deps:
                    if d.dma is None and d.eng == e and e in ("pe", "sp"):
                        continue
                    key, val = tok(d)
                    if waited.get(key, 0) >= val:
                        continue
                    eng.wait_ge(self.sems[key], val)
                    waited[key] = val
                ins = op.fn(eng)
                if op.dma is not None:
                    ins.then_inc(self.sems[op.dma], 16)
                elif op.signal:
                    ins.then_inc(self.sems["eng_" + e], 1)
            if e == "act":
                for op in self.final:
                    eng.wait_ge(self.sems[op.dma], op.sigval)

        block.tensor(lambda eng: run("pe", eng))
        block.scalar(lambda eng: run("act", eng))
        block.vector(lambda eng: run("dve", eng))
        block.gpsimd(lambda eng: run("pool", eng))
        block.sync(lambda eng: run("sp", eng))


class _Stop(Exception):
    pass


def build_program(n_tiles=NT, dbg=None, stop_at=None):
    nc = bass.Bass("TRN2", target_bir_lowering=False)
    x_d = nc.dram_tensor("x", [D, S], F32, kind="ExternalInput").ap()
    wall_d = nc.dram_tensor("wall", [NP, 128, PW], F32, kind="ExternalInput").ap()
    cvec_d = nc.dram_tensor("cvec", [128, NCV], F32, kind="ExternalInput").ap()
    cbc_d = nc.dram_tensor("cbc", [128, NCB], F32, kind="ExternalInput").ap()
    wst_d = nc.dram_tensor("wst", [128, 4, 128], F32, kind="ExternalInput").ap()
    out_d = nc.dram_tensor("out", [D, S], F32, kind="ExternalOutput").ap()
    wsc_d = nc.dram_tensor("wsc", [NP, 128, PW], BF16, kind="Internal").ap()
    dbg_d = {}
    if dbg:
        for name, shape in dbg.items():
            dbg_d[name] = nc.dram_tensor("dbg_" + name, list(shape), F32, kind="ExternalOutput").ap()

    es = ExitStack()
    with es:
        sb = lambda name, shape, dt: es.enter_context(nc.sbuf_tensor("sb_" + name, shape, dt))
        xf = sb("xf", [128, KC, TT], F32)
        xb = sb("xb", [128, KC, TT], BF16)
        KT = sb("KT", [128, 4, S], BF16)
        Vx = sb("Vx", [128, S // 128, 4, 130], BF16)
        wring = sb("wring", [128, NSLOT, PW], BF16)
        FF = sb("FF", [128, 12, TT], F32)
        arena = sb("arena", [128, ARENA_B // 2], BF16)
        cvec = sb("cvec", [128, NCV], F32)
        cbc = sb("cbc", [128, NCB], F32)
        wstf = sb("wstf", [128, 4, 128], F32)
        WcT = sb("WcT", [128, 4, 128], BF16)
        identb = sb("identb", [128, 128], BF16)
        identf = sb("identf", [128, 128], F32)
        trif = sb("trif", [128, 128], F32)
        trib = sb("trib", [128, 128], BF16)
        ones1024 = sb("ones1024", [128, 128], F32)
        ones512 = sb("ones512", [128, 128], F32)
        S1 = sb("S1", [128, TT], F32)
        S2 = sb("S2", [128, TT], F32)
        MEAN = sb("MEAN", [128, TT], F32)
        RSTD = sb("RSTD", [128, TT], F32)
        SG = sb("SG", [128, 3, TT], F32)
        zhalo = sb("zhalo", [128, 4, 30], BF16)
        chhalo = sb("chhalo", [128, 4, 2], F32)
        st16 = sb("st16", [128, 6, 16], F32)
        gsc = sb("gsc", [128, 128], F32)
        lamt = sb("lamt", [128, 64], F32)
        lams = sb("lams", [128, 8], F32)
        ep = sb("ep", [128, 2, 8], F32)
        ss16 = sb("ss16", [128, 32], F32)
        lnd = sb("lnd", [128, 1], F32)
        epT = sb("epT", [128, 2, 128], F32)
        epO = sb("epO", [128, 2, 128], F32)
        epJ = sb("epJ", [128, 128], F32)
        ps = es.enter_context(nc.psum_tensor("ps", [128, 8, 512], F32))

        sch = Sched(nc, es)
        add = sch.add

        def aview(off, dt, n):
            a = arena[:, off // 2:off // 2 + (n * (4 if dt == F32 else 2)) // 2]
            return a.bitcast(F32) if dt == F32 else a

        def ablk(lo, hi):
            return [("A", i) for i in range(lo // 1024, (hi - 1) // 1024 + 1)]

        hT = arena[:, 0:NFC * 512].rearrange("p (c n) -> p c n", n=512)
        yT = arena[:, 0:8 * 512].rearrange("p (c n) -> p c n", n=512)
        ZOFF, CHOFF, DG1OFF = 8192, 12544, 20768
        zbuf = aview(ZOFF, BF16, 4 * 542).rearrange("p (c n) -> p c n", n=542)
        chbuf = aview(CHOFF, F32, 4 * 514).rearrange("p (c n) -> p c n", n=514)
        Dg = [aview(o, BF16, 31 * 128).rearrange("p (k n) -> p k n", n=128) for o in (CHOFF, DG1OFF)]
        R_dg = [ablk(o, o + 31 * 128 * 2) for o in (CHOFF, DG1OFF)]
        QOFF, EOFF, YCOFF = 8192, 16384, 20480
        qz = [aview(QOFF + m * 4096, BF16, 4 * 512).rearrange("p (c n) -> p c n", n=512) for m in range(2)]
        E2 = [aview(EOFF + i * 2048, BF16, 2 * 512).rearrange("p (c n) -> p c n", n=512) for i in range(2)]
        yc = aview(YCOFF, BF16, 4 * 512).rearrange("p (c n) -> p c n", n=512)
        R_hT = lambda fc: [("A", fc)]
        R_yT = lambda c: [("A", c)]
        R_z = lambda c: ablk(ZOFF + c * 542 * 2, ZOFF + (c + 1) * 542 * 2)
        R_ch = lambda c: ablk(CHOFF + c * 514 * 4, CHOFF + (c + 1) * 514 * 4)
        R_q = lambda h: ablk(QOFF + h * 1024, QOFF + (h + 1) * 1024) + ablk(QOFF + 4096 + h * 1024, QOFF + 4096 + (h + 1) * 1024)
        R_qall = ablk(QOFF, QOFF + 8192)
        R_E2 = lambda i: ablk(EOFF + i * 2048, EOFF + (i + 1) * 2048)
        R_yc = lambda q: ablk(YCOFF + q * 1024, YCOFF + (q + 1) * 1024)

        def bcast_mid(ap2d, n):
            pr = [list(p) for p in ap2d.ap]
            return bass.AP(ap2d.tensor, ap2d.offset, [pr[0], [0, n]] + pr[1:])

        def bcast_last(ap2d, n):
            pr = [list(p) for p in ap2d.ap]
            return bass.AP(ap2d.tensor, ap2d.offset, pr + [[0, n]])

        cv = lambda col: cvec[:, col:col + 1]

        def I(eng, meth, *args, reads=(), writes=(), dma=None, final=False, **kw):
            return add(eng, lambda e: getattr(e, meth)(*args, **kw), reads, writes, dma, final)

        XF = [("xf", k) for k in range(8)]
        F0r = [("F0", n) for n in range(4)]
        F1r = [("F1", n) for n in range(4)]
        F2r = [("F2", n) for n in range(4)]
        F0 = FF[:, 0:4, :]
        F1 = FF[:, 4:8, :]
        F2 = FF[:, 8:12, :]

        def x_src(ti):
            return x_d[:, ti * TT:(ti + 1) * TT].rearrange("(k p) n -> p k n", p=128)

        def load_xb(ti):
            I("pool", "dma_start", out=xb[:], in_=x_src(ti), writes=[("xb", k) for k in range(8)], dma="xbin")

        def load_xf(ti):
            I("pool", "dma_start", out=xf[:], in_=x_src(ti), writes=XF, dma="xfin")

        load_xb(0)
        PRIMED = {1: 0, 2: 1, 0: 2, 4: 3}
        for _p, _s in PRIMED.items():
            I("pool", "dma_start", out=wring[:, _s, :], in_=wall_d[_p], writes=[("w", _s)], dma="wp%d" % _s)
        primed_left = dict(PRIMED)
        def convert_weights():
            for g in range(NP // 2):
                I("pool", "dma_start", out=wsc_d[2 * g:2 * g + 2].rearrange("a p n -> p a n"),
                  in_=wall_d[2 * g:2 * g + 2].rearrange("a p n -> p a n"), writes=[("wsc", g)], dma="cv%d" % g)
                if g == 1:
                    load_xf(0)

        convert_weights()
        I("sp", "dma_start", out=cvec[:], in_=cvec_d, writes=["cvec"], dma="c0")
        I("sp", "dma_start", out=cbc[:], in_=cbc_d, writes=["cbc"], dma="c1")
        I("sp", "dma_start", out=wstf[:], in_=wst_d, writes=["wstf"], dma="c2")

        I("pool", "memset", identf[:], 0.0, writes=["identf"])
        I("pool", "affine_select", out=identf[:], in_=identf[:], pattern=[[-1, 128]], compare_op=ALU.not_equal,
          fill=1.0, base=0, channel_multiplier=1, reads=["identf"], writes=["identf"])
        I("pool", "tensor_copy", identb[:], identf[:], reads=["identf"], writes=["identb"])
        I("pool", "memset", trif[:], 1.0, writes=["trif"])
        I("pool", "affine_select", out=trif[:], in_=trif[:], pattern=[[1, 128]], compare_op=ALU.is_ge,
          fill=0.0, base=0, channel_multiplier=-1, reads=["trif"], writes=["trif"])
        I("pool", "tensor_copy", trib[:], trif[:], reads=["trif"], writes=["trib"])
        I("pool", "memset", ones1024[:], 1.0 / 1024.0, writes=["ones1024"])
        I("pool", "memset", ones512[:], 1.0 / 512.0, writes=["ones512"])
        I("pool", "memset", zhalo[:], 0.0, writes=["zhalo"])
        I("pool", "memset", chhalo[:], 0.0, writes=["chhalo"])
        I("pool", "memset", Vx[:, :, :, 128:130], 1.0, writes=["Vx_all"])
        I("pool", "tensor_tensor", out=wstf[:], in0=wstf[:], in1=bcast_mid(trif[:], 4), op=ALU.mult,
          reads=["wstf", "trif"], writes=["wstf"])
        I("pool", "tensor_copy", WcT[:], wstf[:], reads=["wstf"], writes=["WcT"])
        I("dve", "tensor_scalar", out=gsc[:], in0=cbc[:, CB_SG:CB_SG + 128],
          scalar1=(1.0 - LAMBDA_INIT) * math.sqrt(128.0), scalar2=None, op0=ALU.mult, reads=["cbc"], writes=["gsc"])
        I("dve", "tensor_tensor", out=lamt[:], in0=cbc[:, CB_LQ1:CB_LQ1 + 64], in1=cbc[:, CB_LK1:CB_LK1 + 64],
          op=ALU.mult, reads=["cbc"], writes=["lamt"])
        I("dve", "tensor_reduce", out=lams[:, 0:1], in_=lamt[:], axis=AX.X, op=ALU.add, reads=["lamt"], writes=["lams"])
        I("dve", "tensor_tensor", out=lamt[:], in0=cbc[:, CB_LQ2:CB_LQ2 + 64], in1=cbc[:, CB_LK2:CB_LK2 + 64],
          op=ALU.mult, reads=["cbc", "lams"], writes=["lamt"])
        I("dve", "tensor_reduce", out=lams[:, 1:2], in_=lamt[:], axis=AX.X, op=ALU.add, reads=["lamt"], writes=["lams"])
        I("act", "activation", lams[:, 2:4], lams[:, 0:2], AF.Exp, reads=["lams"], writes=["lams"])
        I("dve", "tensor_tensor", out=lams[:, 4:5], in0=lams[:, 3:4], in1=lams[:, 2:3], op=ALU.subtract,
          reads=["lams"], writes=["lams"])
        I("dve", "tensor_scalar", out=lams[:, 4:5], in0=lams[:, 4:5], scalar1=-LAMBDA_INIT, scalar2=None,
          op0=ALU.add, reads=["lams"], writes=["lams"])
        neglam = lams[:, 4:5]

        state = {"slot": 0, "bg": 0, "rb": 0}

        def load_piece(pidx):
            slot = state["slot"]
            state["slot"] = (slot + 1) % NSLOT
            if primed_left.get(pidx) == slot:
                del primed_left[pidx]
                return slot
            I("sp", "dma_start", out=wring[:, slot, :], in_=wsc_d[pidx],
              reads=[("wsc", pidx // 2)], writes=[("w", slot)], dma="w%d" % slot)
            return slot

        def bank_group():
            g = state["bg"]
            state["bg"] = 1 - g
            return [4 * g + i for i in range(4)]

        def one_bank():
            b = state["rb"]
            state["rb"] = (b + 1) % 8
            return b

        def wv(slot, pat, **kw):
            return wring[:, slot, :].rearrange(pat, **kw)

        def mm_feature(slot, banks, rhs_of, rhs_res, nk=8, kouter=False):
            w = wv(slot, "p (k n) -> p k n", n=512)
            order = [(k, n) for k in range(nk) for n in range(4)] if kouter else \
                    [(k, n) for n in range(4) for k in range(nk)]
            for k, n in order:
                if True:
                    I("pe", "matmul", ps[:, banks[n], :], w[:, k, n * 128:(n + 1) * 128], rhs_of(k),
                      start=(k == 0), stop=(k == nk - 1),
                      reads=[("w", slot)] + rhs_res(k), writes=[("ps", banks[n])])

        def mm_token(slot, banks, kouter=False):
            w = wv(slot, "p (k n) -> p k n", n=512)
            order = [(k, tt) for k in range(8) for tt in range(4)] if kouter else \
                    [(k, tt) for tt in range(4) for k in range(8)]
            for k, tt in order:
                if True:
                    I("pe", "matmul", ps[:, banks[tt], :], xb[:, k, tt * 128:(tt + 1) * 128], w[:, k, :],
                      start=(k == 0), stop=(k == 7), reads=[("w", slot), ("xb", k)], writes=[("ps", banks[tt])])

        R_xb = lambda k: [("xb", k)]
        xb_of = lambda k: xb[:, k, :]

        def stats_tail(bm, be, after_var=None):
            I("act", "activation", RSTD[:], ps[:, bm, :], AF.Square, reads=[("ps", bm)], writes=["RSTD"])
            I("act", "activation", MEAN[:], ps[:, bm, :], AF.Copy, reads=[("ps", bm)], writes=["MEAN"])
            I("dve", "scalar_tensor_tensor", out=RSTD[:], in0=ps[:, be, :], scalar=LN_EPS, in1=RSTD[:],
              op0=ALU.add, op1=ALU.subtract, reads=[("ps", be), "RSTD"], writes=["RSTD"])
            I("act", "activation", RSTD[:], RSTD[:], AF.Ln, reads=["RSTD"], writes=["RSTD"])
            I("act", "activation", RSTD[:], RSTD[:], AF.Exp, scale=-0.5, reads=["RSTD"], writes=["RSTD"])
            if after_var is not None:
                after_var()

        R_FF = lambda c: [("F0", c)] if c < 4 else [("F1", c - 4)]

        def layer_norm(gcol, bcol, write_xb=True):
            bm, be = one_bank(), one_bank()
            I("pe", "matmul", ps[:, bm, :], ones1024[:], S1[:], start=True, stop=True,
              reads=["S1", "ones1024"], writes=[("ps", bm)])
            I("pe", "matmul", ps[:, be, :], ones1024[:], S2[:], start=True, stop=True,
              reads=["S2", "ones1024"], writes=[("ps", be)])
            def sub(k):
                I("dve", "tensor_tensor", out=xf[:, k, :], in0=xf[:, k, :], in1=MEAN[:], op=ALU.subtract,
                  reads=[("xf", k), "MEAN"], writes=[("xf", k)])

            def center():
                for k in range(4):
                    sub(k)
            stats_tail(bm, be, center)
            for k in range(8):
                I("dve", "tensor_tensor", out=xf[:, k, :], in0=xf[:, k, :], in1=RSTD[:], op=ALU.mult,
                  reads=[("xf", k), "RSTD"], writes=[("xf", k)])
                if k + 4 < 8:
                    sub(k + 4)
                if write_xb:
                    I("act", "activation", xb[:, k, :], xf[:, k, :], AF.Identity, bias=cv(bcol + k),
                      scale=cv(gcol + k), reads=[("xf", k), "cvec"], writes=[("xb", k)])
            for k in range(8):
                I("act", "activation", xf[:, k, :], xf[:, k, :], AF.Identity, bias=cv(bcol + k), scale=cv(gcol + k),
                  reads=[("xf", k), "cvec"], writes=[("xf", k)])

        def residual_evac(banks, c0):
            for n in range(4):
                c = c0 + n
                I("dve", "scalar_tensor_tensor", out=xf[:, c, :], in0=xf[:, c, :], scalar=ALPHA,
                  in1=ps[:, banks[n], :], op0=ALU.mult, op1=ALU.add,
                  reads=[("xf", c), ("ps", banks[n])], writes=[("xf", c)])
                I("act", "activation", FF[:, c, :], xf[:, c, :], AF.Square, reads=[("xf", c)], writes=R_FF(c))
                if c == 6:
                    I("act", "activation", lnd[:], ones1024[:, 0:1], AF.Ln, reads=["ones1024"], writes=["lnd"])
                if c == 1:
                    I("dve", "tensor_tensor", out=S1[:], in0=xf[:, 0, :], in1=xf[:, 1, :], op=ALU.add,
                      reads=[("xf", 0), ("xf", 1)], writes=["S1"])
                    I("pool", "tensor_tensor", out=S2[:], in0=FF[:, 0, :], in1=FF[:, 1, :], op=ALU.add,
                      reads=R_FF(0) + R_FF(1), writes=["S2"])
                elif c > 1:
                    I("dve", "tensor_tensor", out=S1[:], in0=S1[:], in1=xf[:, c, :], op=ALU.add,
                      reads=[("xf", c), "S1"], writes=["S1"])
                    I("dve" if c == 7 else "pool", "tensor_tensor", out=S2[:], in0=S2[:], in1=FF[:, c, :], op=ALU.add,
                      reads=R_FF(c) + ["S2"], writes=["S2"])

        def out_proj(pbase):
            for j in range(2):
                slot = load_piece(pbase + 5 + j)
                banks = bank_group()
                mm_feature(slot, banks, lambda k: yT[:, k, :], R_yT)
                residual_evac(banks, 4 * j)

        def ffn(pbase, after_gateup=None):
            for j in range(11):
                slot = load_piece(pbase + 7 + j)
                banks = bank_group()
                w = wv(slot, "p (k a n) -> p k a n", a=2, n=256)
                order = [(k, a, hf) for k in range(8) for a in range(2) for hf in range(2)] if j == 0 else \
                        [(k, a, hf) for hf in range(2) for a in range(2) for k in range(8)]
                for k, a, hf in order:
                    if True:
                        if True:
                            b = banks[2 * a + hf]
                            I("pe", "matmul", ps[:, b, :], w[:, k, a, hf * 128:(hf + 1) * 128], xb[:, k, :],
                              start=(k == 0), stop=(k == 7), reads=[("w", slot), ("xb", k)], writes=[("ps", b)])
                for hf in range(2):
                    fc = 2 * j + hf
                    sgi = fc % 3
                    I("act", "activation", SG[:, sgi, :], ps[:, banks[hf], :], AF.Silu,
                      reads=[("ps", banks[hf])], writes=[("SG", sgi)])
                    I("dve", "tensor_tensor", out=hT[:, fc, :], in0=SG[:, sgi, :], in1=ps[:, banks[2 + hf], :],
                      op=ALU.mult, reads=[("SG", sgi), ("ps", banks[2 + hf])], writes=R_hT(fc))
            if after_gateup is not None:
                after_gateup()
            for dh in range(2):
                banks = bank_group()
                for g, (lo, hi) in enumerate(DOWN_GROUPS):
                    slot = load_piece(pbase + 18 + dh * 3 + g)
                    w = wv(slot, "p (k n) -> p k n", n=512)
                    order = [(i, fc, n) for i, fc in enumerate(range(lo, hi)) for n in range(4)] if g < 2 else \
                            [(i, fc, n) for n in range(4) for i, fc in enumerate(range(lo, hi))]
                    for i, fc, n in order:
                        if True:
                            I("pe", "matmul", ps[:, banks[n], :], w[:, i, n * 128:(n + 1) * 128], hT[:, fc, :],
                              start=(fc == 0), stop=(fc == NFC - 1),
                              reads=[("w", slot)] + R_hT(fc), writes=[("ps", banks[n])])
                residual_evac(banks, 4 * dh)

        def build_dg(n):
            wtap = cvec[:, CV_CB + n:CV_CB + n + 1]
            pr = [list(p) for p in wtap.ap]
            w_bc = bass.AP(wtap.tensor, wtap.offset, [pr[0], [4, 31], [0, 128]])
            I("dve", "tensor_tensor", out=Dg[(n + 1) % 2], in0=bcast_mid(identb[:], 31), in1=w_bc, op=ALU.mult,
              reads=["identb", "cvec"], writes=R_dg[(n + 1) % 2])

        def mixer0(ti, pbase):
            build_dg(0)
            for n in range(4):
                I("pool", "tensor_copy", zbuf[:, n, 0:30], zhalo[:, n, :], reads=["zhalo"], writes=R_z(n))
            slot = load_piece(pbase + 1)
            banks = bank_group()
            mm_feature(slot, banks, xb_of, R_xb, kouter=True)
            for n in range(4):
                I("act", "activation", F0[:, n, :], ps[:, banks[n], :], AF.Copy,
                  reads=[("ps", banks[n])], writes=[("F0", n)])
            slot = load_piece(pbase + 2)
            banks = bank_group()
            mm_feature(slot, banks, xb_of, R_xb)
            for n in range(4):
                I("pool", "tensor_copy", chbuf[:, n, 0:2], chhalo[:, n, :], reads=["chhalo"], writes=R_ch(n))
                I("dve", "tensor_tensor", out=chbuf[:, n, 2:514], in0=F0[:, n, :], in1=ps[:, banks[n], :], op=ALU.mult,
                  reads=[("F0", n), ("ps", banks[n])], writes=R_ch(n))
            slot = load_piece(pbase + 0)
            banks = bank_group()
            mm_feature(slot, banks, xb_of, R_xb)
            for n in range(4):
                I("act", "activation", F1[:, n, :], ps[:, banks[n], :], AF.Copy,
                  reads=[("ps", banks[n])], writes=[("F1", n)])
            for n in range(4):
                I("dve", "tensor_scalar", out=F0[:, n, :], in0=chbuf[:, n, 2:514], scalar1=cv(CV_CA + 8 + n),
                  scalar2=None, op0=ALU.mult, reads=R_ch(n) + ["cvec"], writes=[("F0", n)])
                for k in (1, 0):
                    I("dve", "scalar_tensor_tensor", out=F0[:, n, :], in0=chbuf[:, n, k:k + 512],
                      scalar=cv(CV_CA + 4 * k + n), in1=F0[:, n, :], op0=ALU.mult, op1=ALU.add,
                      reads=R_ch(n) + [("F0", n)], writes=[("F0", n)])
                I("pool", "tensor_copy", chhalo[:, n, :], chbuf[:, n, 512:514], reads=R_ch(n), writes=["chhalo"])
                I("dve", "tensor_tensor", out=yT[:, n, :], in0=F0[:, n, :], in1=F1[:, n, :], op=ALU.mult,
                  reads=[("F0", n), ("F1", n)], writes=R_yT(n))
            build_dg(1)
            slot = load_piece(pbase + 4)
            banks = bank_group()
            mm_feature(slot, banks, xb_of, R_xb)
            for n in range(4):
                I("act", "activation", F0[:, n, :], ps[:, banks[n], :], AF.Sigmoid,
                  reads=[("ps", banks[n])], writes=[("F0", n)])
            slot = load_piece(pbase + 3)
            banks = bank_group()
            mm_feature(slot, banks, xb_of, R_xb)
            for n in range(4):
                I("dve", "tensor_tensor", out=zbuf[:, n, 30:542], in0=F0[:, n, :], in1=ps[:, banks[n], :], op=ALU.mult,
                  reads=[("F0", n), ("ps", banks[n])], writes=R_z(n))
            banks = bank_group()
            for n in range(4):
                d = (n + 1) % 2
                for k in range(31):
                    I("pe", "matmul", ps[:, banks[n], :], Dg[d][:, k, :], zbuf[:, n, k:k + 512],
                      start=(k == 0), stop=(k == 30), reads=R_dg[d] + R_z(n), writes=[("ps", banks[n])])
                if n + 2 < 4:
                    build_dg(n + 2)
                I("pool", "tensor_copy", zhalo[:, n, :], zbuf[:, n, 512:542], reads=R_z(n), writes=["zhalo"])
                I("act", "activation", F1[:, n, :], ps[:, banks[n], :], AF.Identity, bias=cv(CV_BB + n),
                  reads=[("ps", banks[n]), "cvec"], writes=[("F1", n)])
                I("act", "activation", F2[:, n, :], ps[:, banks[n], :], AF.Square, bias=cv(CV_BB + n),
                  reads=[("ps", banks[n]), "cvec"], writes=[("F2", n)])
                if n == 2:
                    I("act", "activation", lnd[:], ones1024[:, 0:1], AF.Ln, reads=["ones1024"], writes=["lnd"])
                if n == 1:
                    I("dve", "tensor_tensor", out=S1[:], in0=F1[:, 0, :], in1=F1[:, 1, :], op=ALU.add,
                      reads=[("F1", 0), ("F1", 1)], writes=["S1"])
                    I("pool", "tensor_tensor", out=S2[:], in0=F2[:, 0, :], in1=F2[:, 1, :], op=ALU.add,
                      reads=[("F2", 0), ("F2", 1)], writes=["S2"])
                elif n > 1:
                    I("dve", "tensor_tensor", out=S1[:], in0=S1[:], in1=F1[:, n, :], op=ALU.add,
                      reads=[("F1", n), "S1"], writes=["S1"])
                    I("pool", "tensor_tensor", out=S2[:], in0=S2[:], in1=F2[:, n, :], op=ALU.add,
                      reads=[("F2", n), "S2"], writes=["S2"])
            bm, be = one_bank(), one_bank()
            I("pe", "matmul", ps[:, bm, :], ones512[:], S1[:], start=True, stop=True,
              reads=["S1", "ones512"], writes=[("ps", bm)])
            I("pe", "matmul", ps[:, be, :], ones512[:], S2[:], start=True, stop=True,
              reads=["S2", "ones512"], writes=[("ps", be)])
            def center():
                for n in range(4):
                    I("dve", "tensor_tensor", out=F1[:, n, :], in0=F1[:, n, :], in1=MEAN[:], op=ALU.subtract,
                      reads=[("F1", n), "MEAN"], writes=[("F1", n)])
            stats_tail(bm, be, center)
            for n in range(4):
                I("dve", "tensor_tensor", out=F1[:, n, :], in0=F1[:, n, :], in1=RSTD[:], op=ALU.mult,
                  reads=[("F1", n), "RSTD"], writes=[("F1", n)])
                I("act", "activation", yT[:, 4 + n, :], F1[:, n, :], AF.Silu, bias=cv(CV_LB + n), scale=cv(CV_LG + n),
                  reads=[("F1", n), "cvec"], writes=R_yT(4 + n))

        VGOFF = YCOFF + 4096
        vgln = aview(VGOFF, BF16, 4 * 512).rearrange("p (c n) -> p c n", n=512)
        R_vg = ablk(VGOFF, VGOFF + 4096)
        psb = ps[:].bitcast(BF16)

        def mixer1(ti, pbase):
            slot = load_piece(pbase + 4)
            banks = bank_group()
            mm_token(slot, banks, kouter=True)
            for tt in range(4):
                I("act", "activation", F1[:, tt, :], ps[:, banks[tt], :], AF.Gelu_apprx_tanh,
                  reads=[("ps", banks[tt])], writes=[("F1", tt)])
            F1g = F1.rearrange("p t (g d) -> p (t g) d", d=128)
            F2g = F2.rearrange("p t (g d) -> p (t g) d", d=128)
            I("dve", "tensor_reduce", out=st16[:, 0, :], in_=F1g, axis=AX.X, op=ALU.add, reads=F1r, writes=["st16"])
            I("act", "activation", F2, F1, AF.Square, reads=F1r, writes=F2r)
            I("dve", "tensor_reduce", out=st16[:, 1, :], in_=F2g, axis=AX.X, op=ALU.add,
              reads=F2r + ["st16"], writes=["st16"])
            I("dve", "tensor_scalar", out=st16[:, 2, :], in0=st16[:, 0, :], scalar1=1.0 / 128.0, scalar2=None,
              op0=ALU.mult, reads=["st16"], writes=["st16"])
            I("dve", "tensor_tensor", out=st16[:, 3, :], in0=st16[:, 2, :], in1=st16[:, 2, :], op=ALU.mult,
              reads=["st16"], writes=["st16"])
            I("dve", "scalar_tensor_tensor", out=st16[:, 4, :], in0=st16[:, 1, :], scalar=1.0 / 128.0,
              in1=st16[:, 3, :], op0=ALU.mult, op1=ALU.subtract, reads=["st16"], writes=["st16"])
            I("dve", "tensor_scalar", out=st16[:, 5, :], in0=st16[:, 4, :], scalar1=LN_EPS, scalar2=None,
              op0=ALU.add, reads=["st16"], writes=["st16"])
            slot = load_piece(pbase + 3)
            banks = bank_group()
            mm_feature(slot, banks, xb_of, R_xb)
            for n in range(4):
                I("act", "activation", F0[:, n, :], ps[:, banks[n], :], AF.Gelu_apprx_tanh,
                  reads=[("ps", banks[n])], writes=[("F0", n)])
            slot = load_piece(pbase + 0)
            banks = bank_group()
            mm_feature(slot, banks, xb_of, R_xb)
            I("pool", "memset", qz[0][64:128, :, :], 0.0, writes=R_qall)
            I("pool", "memset", qz[1][0:64, :, :], 0.0, writes=R_qall)
            for n in range(4):
                I("act", "activation", qz[0][0:64, n, :], ps[0:64, banks[n], :], AF.Copy,
                  reads=[("ps", banks[n])], writes=R_q(n))
                I("act", "activation", qz[1][64:128, n, :], ps[64:128, banks[n], :], AF.Copy,
                  reads=[("ps", banks[n])], writes=R_q(n))
            I("act", "activation", st16[:, 5, :], st16[:, 5, :], AF.Ln, reads=["st16"], writes=["st16"])
            I("act", "activation", st16[:, 5, :], st16[:, 5, :], AF.Exp, scale=-0.5, reads=["st16"], writes=["st16"])
            I("dve", "tensor_tensor", out=F1g, in0=F1g, in1=bcast_last(st16[:, 2, :], 128), op=ALU.subtract,
              reads=F1r + ["st16"], writes=F1r)
            I("dve", "tensor_tensor", out=F1g, in0=F1g, in1=bcast_last(st16[:, 5, :], 128), op=ALU.mult,
              reads=F1r + ["st16"], writes=F1r)
            I("pool", "tensor_tensor", out=F1, in0=F1, in1=bcast_mid(cbc[:, CB_GG:CB_GG + 512], 4), op=ALU.mult,
              reads=F1r + ["cbc"], writes=F1r)
            I("pool", "tensor_tensor", out=vgln, in0=F1, in1=bcast_mid(cbc[:, CB_GB:CB_GB + 512], 4), op=ALU.add,
              reads=F1r + ["cbc"], writes=R_vg)
            slot = load_piece(pbase + 1)
            banks = bank_group()
            mm_feature(slot, banks, xb_of, R_xb)
            for n in range(4):
                I("dve", "tensor_copy", KT[:, n, ti * TT:(ti + 1) * TT], ps[:, banks[n], :],
                  reads=[("ps", banks[n])], writes=[("KT", ti)])
            slot = load_piece(pbase + 2)
            banks = bank_group()
            mm_token(slot, banks)
            for tt in range(4):
                I("act", "activation", Vx[:, 4 * ti + tt, :, 0:128],
                  ps[:, banks[tt], :].rearrange("p (h e) -> p h e", e=128), AF.Copy,
                  reads=[("ps", banks[tt]), "Vx_all"], writes=[("Vx", 4 * ti + tt)])
            nkt = 4 * ti + 4
            spairs = ((3, 4), (5, 6))
            accb = (0, 1, 2)
            acc_sb = (F1, SG[:])
            acc_sb_res = (F1r, [("SG", i) for i in range(3)])

            def acc_ap(qt, m):
                a = qt * 2 + m
                return ps[:, accb[a // 3], (a % 3) * 129:(a % 3) * 129 + 129], ("ps", accb[a // 3])

            def acc_copy_ap(h, qt, m):
                a = qt * 2 + m
                return acc_sb[h % 2][:, a // 3, (a % 3) * 129:(a % 3) * 129 + 129]

            def rec_pv(blk, pi2, bi):
                h, m, kt = blk
                q0 = max(0, kt - 4 * ti)
                for qt in range(q0, 4):
                    dst, dres = acc_ap(qt, m)
                    I("pe", "matmul", dst, E2[pi2][:, bi, (qt - q0) * 128:(qt - q0 + 1) * 128], Vx[:, kt, h, 0:129],
                      start=(kt == 0 and m == 0 and qt in (0, 2, 3)), stop=(kt == 4 * ti + qt),
                      skip_group_check=True, reads=R_E2(pi2) + [("Vx", kt), "Vx_all"], writes=[dres])

            def epilogue(h):
                sbuf_res = acc_sb_res[h % 2]
                for bi_, ncols in ((0, 387), (1, 387), (2, 258)):
                    I("dve", "tensor_copy", acc_sb[h % 2][:, bi_, 0:ncols], ps[:, accb[bi_], 0:ncols],
                      reads=[("ps", accb[bi_])], writes=sbuf_res)
                for qt in range(4):
                    a0 = acc_copy_ap(h, qt, 0)
                    a1 = acc_copy_ap(h, qt, 1)
                    pi = qt % 2
                    EP = [("ep", pi)]
                    eo = F2[:, h, qt * 128:(qt + 1) * 128]
                    I("dve", "reciprocal", ep[:, pi, 0:1], a0[:, 128:129], reads=sbuf_res, writes=EP)
                    I("dve", "reciprocal", ep[:, pi, 1:2], a1[:, 128:129], reads=sbuf_res + EP, writes=EP)
                    I("dve", "tensor_tensor", out=ep[:, pi, 2:3], in0=ep[:, pi, 1:2], in1=neglam, op=ALU.mult,
                      reads=EP + ["lams"], writes=EP)
                    I("dve", "tensor_scalar", out=epT[:, pi, :], in0=a1[:, 0:128], scalar1=ep[:, pi, 2:3],
                      scalar2=None, op0=ALU.mult, reads=sbuf_res + EP, writes=[("epT", pi)])
                    I("dve", "scalar_tensor_tensor", out=eo, in0=a0[:, 0:128], scalar=ep[:, pi, 0:1],
                      in1=epT[:, pi, :], op0=ALU.mult, op1=ALU.add,
                      reads=sbuf_res + [("epT", pi)] + EP, writes=[("F2", h)])
                    I("dve", "scalar_tensor_tensor", out=epJ[:], in0=eo, scalar=1.0, in1=eo, op0=ALU.mult, op1=ALU.mult,
                      accum_out=ss16[:, 4 * h + qt:4 * h + qt + 1], reads=[("F2", h)], writes=["epJ", "ss16"])

            blocks = [(h, m, kt) for h in range(4) for m in range(2) for kt in range(nkt)]
            npair = len(blocks) // 2

            def rec_s(p):
                for bi in range(2):
                    h, m, kt = blocks[2 * p + bi]
                    c0 = max(0, kt - 4 * ti) * 128
                    bk = spairs[p % 2][bi]
                    I("pe", "matmul", ps[:, bk, 0:512 - c0], KT[:, h, kt * 128:(kt + 1) * 128],
                      qz[m][:, h, c0:512], start=True, stop=True,
                      reads=[("KT", kt // 4)] + R_q(h), writes=[("ps", bk)])

            rec_s(0)
            rec_s(1)
            for p in range(npair):
                b0, b1 = spairs[p % 2]
                pi2 = p % 2
                h, m, kt = blocks[2 * p]
                if kt + 1 < 4 * ti:
                    I("act", "activation", E2[pi2], ps[:, b0:b0 + 2, :], AF.Exp, scale=0.125,
                      reads=[("ps", b0), ("ps", b1)], writes=R_E2(pi2))
                else:
                    for bi in range(2):
                        j = kt + bi - 4 * ti
                        ncol = 512 - j * 128
                        I("act", "activation", E2[pi2][:, bi, 0:ncol], ps[:, spairs[p % 2][bi], 0:ncol], AF.Exp,
                          scale=0.125, reads=[("ps", spairs[p % 2][bi])], writes=R_E2(pi2))
                        I("pool", "tensor_tensor", out=E2[pi2][:, bi, 0:128], in0=E2[pi2][:, bi, 0:128], in1=trib[:],
                          op=ALU.mult, reads=R_E2(pi2) + ["trib"], writes=R_E2(pi2))
                if p + 2 < npair:
                    rec_s(p + 2)
                for bi in range(2):
                    rec_pv(blocks[2 * p + bi], pi2, bi)
                if m == 1 and kt + 1 == nkt - 1:
                    epilogue(h)
            I("dve", "tensor_scalar", out=ss16[:, 16:32], in0=ss16[:, 0:16], scalar1=128.0 * LN_EPS, scalar2=None,
              op0=ALU.add, reads=["ss16"], writes=["ss16b"])
            I("act", "activation", ss16[:, 16:32], ss16[:, 16:32], AF.Ln, reads=["ss16b"], writes=["ss16b"])
            I("act", "activation", ss16[:, 16:32], ss16[:, 16:32], AF.Exp, scale=-0.5, reads=["ss16b"], writes=["ss16b"])
            for h in range(4):
                for qt in range(4):
                    I("dve", "scalar_tensor_tensor", out=yc[:, qt, h * 128:(h + 1) * 128],
                      in0=F2[:, h, qt * 128:(qt + 1) * 128], scalar=ss16[:, 16 + 4 * h + qt:17 + 4 * h + qt],
                      in1=gsc[:], op0=ALU.mult, op1=ALU.mult, reads=[("F2", h), "gsc", "ss16b"], writes=R_yc(qt))
            banks = bank_group()
            for g in range(4):
                for tt in range(4):
                    I("pe", "matmul", ps[:, banks[g], tt * 128:(tt + 1) * 128], vgln[:, tt, g * 128:(g + 1) * 128],
                      WcT[:, g, :], start=True, stop=True, reads=R_vg + ["WcT"], writes=[("ps", banks[g])])
            for g in range(4):
                I("dve", "tensor_tensor", out=F2[:, g, :].rearrange("p (t s) -> p t s", s=128),
                  in0=ps[:, banks[g], :].rearrange("p (t s) -> p t s", s=128),
                  in1=bcast_mid(cbc[:, CB_SB + 128 * g:CB_SB + 128 * g + 128], 4), op=ALU.add,
                  reads=[("ps", banks[g]), "cbc"], writes=[("F2", g)])
                I("pool", "tensor_tensor", out=yT[:, 4 + g, :], in0=F2[:, g, :], in1=F0[:, g, :], op=ALU.mult,
                  reads=[("F2", g), ("F0", g)], writes=R_yT(4 + g))
            for h in range(4):
                b = (6, 7)[h % 2]
                for qt in range(4):
                    I("pe", "transpose", psb[:, b, qt * 128:(qt + 1) * 128], yc[:, qt, h * 128:(h + 1) * 128],
                      identb[:], reads=R_yc(qt) + ["identb"], writes=[("ps", b)])
                I("act", "activation", yT[:, h, :], psb[:, b, 0:512], AF.Copy, reads=[("ps", b)], writes=R_yT(h))


        def stage(name):
            sch.tag = "after_" + name
            if stop_at == name:
                raise _Stop()

        def main_loop():
            for ti in range(n_tiles):
                stage("x")
                for L in range(2):
                    pbase = L * NPL
                    if L == 0:
                        mixer0(ti, pbase)
                        if ti > 0:
                            load_xf(ti)
                    else:
                        mixer1(ti, pbase)
                    stage("mix%d" % L)
                    if dbg and ("y%d" % L) in dbg_d and ti == 0:
                        for c in range(8):
                            I("pool", "tensor_copy", FF[:, 8 + c % 4, :], yT[:, c, :], reads=R_yT(c), writes=[("F2", c % 4)])
                            I("pool", "dma_start", out=dbg_d["y%d" % L][c], in_=FF[:, 8 + c % 4, :],
                              reads=[("F2", c % 4)], dma="dbg%s%d" % ("y%d" % L, c), final=True)
                    out_proj(pbase)
                    stage("out%d" % L)
                    layer_norm(CV_MG + 8 * L, CV_MB + 8 * L)
                    stage("ln%d" % L)
                    if dbg and ("m%d" % L) in dbg_d and ti == 0:
                        I("pool", "dma_start", out=dbg_d["m%d" % L].rearrange("k p n -> p k n"), in_=xf[:],
                          reads=XF, dma="dbgm%d" % L, final=True)
                    ffn(pbase, (lambda: load_xb(ti + 1)) if (L == 1 and ti + 1 < n_tiles) else None)
                    stage("ffn%d" % L)
                    layer_norm(CV_FG + 8 * L, CV_FB + 8 * L, write_xb=(L == 0))
                    stage("fln%d" % L)
                    if dbg and ("f%d" % L) in dbg_d and ti == 0:
                        I("pool", "dma_start", out=dbg_d["f%d" % L].rearrange("k p n -> p k n"), in_=xf[:],
                          reads=XF, dma="dbgf%d" % L, final=True)
                I("act", "dma_start", out=out_d[:, ti * TT:(ti + 1) * TT].rearrange("(k p) n -> p k n", p=128), in_=xf[:],
                  reads=XF, dma="xout", final=True)

        try:
            main_loop()
        except _Stop:
            pass

        sch.emit()
    return nc


_NC_CACHE = {}


def kernel(**inputs):
    x = np.asarray(inputs["x"], np.float32)
    shared = make_shared(inputs)
    if "nc" not in _NC_CACHE:
        _NC_CACHE["nc"] = build_program()
    nc = _NC_CACHE["nc"]
    in_maps = [dict(shared, x=np.ascontiguousarray(x[b].T)) for b in range(8)]
    res = run_bass_kernel_spmd(nc, in_maps, core_ids=list(range(8)))
    return np.stack([np.ascontiguousarray(np.asarray(r["out"], np.float32).T) for r in res.results], axis=0)
```
